# Optimizing a Trainium2 kernel written in Bass

```python
import math
import jax, jax.numpy as jnp
from jax import lax
import numpy as np

D_MODEL = 1024
BATCH = 8
SEQ = 2048
DEPTH = 1

D_MIX = D_MODEL
DN_WIDTH = D_MIX // 2
DN_HEAD_K = 128
DN_HEAD_V = 128
DN_HEADS = DN_WIDTH // DN_HEAD_V
CONV_WIDTH = 4
CHUNK = 64
DF_WIDTH = D_MIX - DN_WIDTH
DF_HEAD_QK = 64
DF_HEAD_V = 2 * DF_HEAD_QK
DF_HEADS = DF_WIDTH // DF_HEAD_V
Q_BLOCK = 128
EPS = 1e-6

DN_QK = DN_HEADS * DN_HEAD_K
DN_V = DN_HEADS * DN_HEAD_V
DF_QK = DF_HEADS * 2 * DF_HEAD_QK
DF_V = DF_HEADS * DF_HEAD_V
COL_SIZES = (DN_QK, DN_QK, DN_V, DN_V, DN_HEADS, DN_HEADS, DF_QK, DF_QK, DF_V, DF_V)
IN_COLS = sum(COL_SIZES)
COL_OFFSETS = tuple(int(o) for o in np.cumsum(COL_SIZES)[:-1])

kernel_name = "hybrid_gdn_diffattn_parallel_heads"


def rmsnorm(x, gain):
    xf = x.astype(jnp.float32)
    y = xf * lax.rsqrt(jnp.mean(xf * xf, axis=-1, keepdims=True) + EPS)
    return (y * gain.astype(jnp.float32)).astype(x.dtype)


def l2norm(x):
    return x * lax.rsqrt(jnp.sum(x * x, axis=-1, keepdims=True) + EPS)


def causal_depthwise_conv(x, w):
    c = x.shape[-1]
    return lax.conv_general_dilated(
        x, w[:, None, :].astype(x.dtype), window_strides=(1,),
        padding=[(CONV_WIDTH - 1, 0)], dimension_numbers=("NWC", "WIO", "NWC"),
        feature_group_count=c)


def gated_delta_rule(q, k, v, g, beta):
    b, t, h, dk = q.shape
    dv = v.shape[-1]
    n = t // CHUNK

    def chunks(a):
        a = jnp.moveaxis(a, 2, 1)
        return a.reshape((b, h, n, CHUNK) + a.shape[3:])

    qc, kc, vc = chunks(q), chunks(k), chunks(v)
    gc = jnp.cumsum(chunks(g), axis=-1)
    bc = chunks(beta)
    k_beta = kc * bc[..., None]
    v_beta = vc * bc[..., None]

    tril = jnp.tril(jnp.ones((CHUNK, CHUNK), bool))
    strict = jnp.tril(jnp.ones((CHUNK, CHUNK), bool), -1)
    diff = gc[..., :, None] - gc[..., None, :]
    decay = jnp.exp(jnp.where(tril, diff, -jnp.inf))

    m = jnp.where(strict, jnp.einsum('bhncd,bhnsd->bhncs', k_beta, kc) * decay, 0.0)
    lmat = m + jnp.eye(CHUNK, dtype=m.dtype)
    rhs = jnp.concatenate([v_beta, k_beta * jnp.exp(gc)[..., None]], axis=-1)
    sol = lax.linalg.triangular_solve(lmat, rhs, left_side=True, lower=True,
                                      unit_diagonal=True)
    u, w = sol[..., :dv], sol[..., dv:]
    a_qk = jnp.einsum('bhncd,bhnsd->bhncs', qc, kc) * decay

    def step(state, inp):
        q_i, k_i, u_i, w_i, g_i, a_i = inp
        v_new = u_i - jnp.einsum('bhcd,bhdv->bhcv', w_i, state)
        o_i = (jnp.einsum('bhcd,bhdv->bhcv', q_i * jnp.exp(g_i)[..., None], state)
               + jnp.einsum('bhcs,bhsv->bhcv', a_i, v_new))
        g_last = g_i[..., -1]
        k_dec = k_i * jnp.exp(g_last[..., None] - g_i)[..., None]
        state = (state * jnp.exp(g_last)[..., None, None]
                 + jnp.einsum('bhcd,bhcv->bhdv', k_dec, v_new))
        return state, o_i

    xs = tuple(jnp.moveaxis(a, 2, 0) for a in (qc, kc, u, w, gc, a_qk))
    s0 = jnp.zeros((b, h, dk, dv), jnp.float32)
    _, o = lax.scan(step, s0, xs)
    o = jnp.transpose(o, (1, 0, 3, 2, 4))
    return o.reshape(b, t, h, dv)


def gated_deltanet_group(q, k, v, z, bb, aa, conv_w, a_log, dt_bias, out_gain):
    b, t, _ = q.shape
    qkv = jax.nn.silu(causal_depthwise_conv(jnp.concatenate([q, k, v], -1), conv_w))
    q, k, v = jnp.split(qkv, [DN_QK, 2 * DN_QK], axis=-1)
    q = l2norm(q.reshape(b, t, DN_HEADS, DN_HEAD_K).astype(jnp.float32)) * (DN_HEAD_K ** -0.5)
    k = l2norm(k.reshape(b, t, DN_HEADS, DN_HEAD_K).astype(jnp.float32))
    v = v.reshape(b, t, DN_HEADS, DN_HEAD_V).astype(jnp.float32)
    beta = jax.nn.sigmoid(bb.astype(jnp.float32))
    g = -jnp.exp(a_log.astype(jnp.float32)) * jax.nn.softplus(
        aa.astype(jnp.float32) + dt_bias.astype(jnp.float32))
    o = gated_delta_rule(q, k, v, g, beta)
    zf = z.reshape(b, t, DN_HEADS, DN_HEAD_V).astype(jnp.float32)
    o = rmsnorm(o, out_gain) * jax.nn.silu(zf)
    return o.reshape(b, t, DN_V).astype(z.dtype)


def diff_attention_group(q, k, v, z, q_gain, k_gain, lq1, lk1, lq2, lk2, out_gain, lambda_init):
    b, t, _ = q.shape
    q = rmsnorm(q.reshape(b, t, DF_HEADS, 2, DF_HEAD_QK), q_gain) * (DF_HEAD_QK ** -0.5)
    k = rmsnorm(k.reshape(b, t, DF_HEADS, 2, DF_HEAD_QK), k_gain)
    v = v.reshape(b, t, DF_HEADS, DF_HEAD_V)
    lam = (jnp.exp(jnp.sum(lq1.astype(jnp.float32) * lk1.astype(jnp.float32)))
           - jnp.exp(jnp.sum(lq2.astype(jnp.float32) * lk2.astype(jnp.float32)))
           + lambda_init)
    nb = t // Q_BLOCK
    qb = jnp.moveaxis(q.reshape(b, nb, Q_BLOCK, DF_HEADS, 2, DF_HEAD_QK), 1, 0)
    starts = jnp.arange(nb, dtype=jnp.int32) * Q_BLOCK
    key_pos = jnp.arange(t, dtype=jnp.int32)
    neg = jnp.finfo(jnp.float32).min

    def block(args):
        q_blk, start = args
        s = jnp.einsum('bqhcd,bkhcd->bhcqk', q_blk, k).astype(jnp.float32)
        q_pos = start + jnp.arange(Q_BLOCK, dtype=jnp.int32)
        mask = key_pos[None, :] <= q_pos[:, None]
        p = jax.nn.softmax(jnp.where(mask, s, neg), axis=-1)
        wts = p[:, :, 0] - lam * p[:, :, 1]
        return jnp.einsum('bhqk,bkhv->bqhv', wts.astype(v.dtype), v)

    o = lax.map(block, (qb, starts))
    o = jnp.moveaxis(o, 0, 1).reshape(b, t, DF_HEADS, DF_HEAD_V)
    o = rmsnorm(o, out_gain).astype(jnp.float32) * (1.0 - lambda_init)
    zf = z.reshape(b, t, DF_HEADS, DF_HEAD_V).astype(jnp.float32)
    o = o * jax.nn.silu(zf)
    return o.reshape(b, t, DF_V).astype(z.dtype)


def setup_inputs(seed: int = 0) -> dict:
    key = jax.random.key(seed)
    ks = jax.random.split(key, 16)
    f32 = jnp.float32
    x = jax.random.normal(ks[0], (BATCH, SEQ, D_MODEL), f32)
    norm_gain = 1.0 + 0.05 * jax.random.normal(ks[1], (DEPTH, D_MODEL), f32)
    w_in = jax.random.normal(ks[2], (DEPTH, D_MODEL, IN_COLS), f32) * D_MODEL ** -0.5
    conv_w = jax.random.normal(ks[3], (DEPTH, CONV_WIDTH, DN_QK * 2 + DN_V), f32) * CONV_WIDTH ** -0.5
    a_log = jnp.log(jax.random.uniform(ks[4], (DEPTH, DN_HEADS), f32, 1.0, 16.0))
    dt = jnp.exp(jax.random.uniform(ks[5], (DEPTH, DN_HEADS), f32,
                                    math.log(1e-3), math.log(1e-1)))
    dt_bias = dt + jnp.log(-jnp.expm1(-dt))
    dn_out_gain = 1.0 + 0.05 * jax.random.normal(ks[6], (DEPTH, DN_HEAD_V), f32)
    q_gain = 1.0 + 0.05 * jax.random.normal(ks[7], (DEPTH, DF_HEAD_QK), f32)
    k_gain = 1.0 + 0.05 * jax.random.normal(ks[8], (DEPTH, DF_HEAD_QK), f32)
    lambda_q1 = 0.1 * jax.random.normal(ks[9], (DEPTH, DF_HEAD_QK), f32)
    lambda_k1 = 0.1 * jax.random.normal(ks[10], (DEPTH, DF_HEAD_QK), f32)
    lambda_q2 = 0.1 * jax.random.normal(ks[11], (DEPTH, DF_HEAD_QK), f32)
    lambda_k2 = 0.1 * jax.random.normal(ks[12], (DEPTH, DF_HEAD_QK), f32)
    df_out_gain = 1.0 + 0.05 * jax.random.normal(ks[13], (DEPTH, DF_HEAD_V), f32)
    w_out = jax.random.normal(ks[14], (DEPTH, D_MIX, D_MODEL), f32) * D_MIX ** -0.5
    return {"x": x, "norm_gain": norm_gain, "w_in": w_in, "conv_w": conv_w,
            "a_log": a_log, "dt_bias": dt_bias, "dn_out_gain": dn_out_gain,
            "q_gain": q_gain, "k_gain": k_gain, "lambda_q1": lambda_q1,
            "lambda_k1": lambda_k1, "lambda_q2": lambda_q2, "lambda_k2": lambda_k2,
            "df_out_gain": df_out_gain, "w_out": w_out}


def reference(x, norm_gain, w_in, conv_w, a_log, dt_bias, dn_out_gain, q_gain, k_gain,
              lambda_q1, lambda_k1, lambda_q2, lambda_k2, df_out_gain, w_out):
    for l in range(DEPTH):
        lambda_init = 0.8 - 0.6 * math.exp(-0.3 * l)
        h = rmsnorm(x, norm_gain[l])
        proj = jnp.einsum('btd,dc->btc', h, w_in[l])
        (dn_q, dn_k, dn_v, dn_z, dn_b, dn_a,
         df_q, df_k, df_v, df_z) = jnp.split(proj, COL_OFFSETS, axis=-1)
        o_dn = gated_deltanet_group(dn_q, dn_k, dn_v, dn_z, dn_b, dn_a, conv_w[l],
                                    a_log[l], dt_bias[l], dn_out_gain[l])
        o_df = diff_attention_group(df_q, df_k, df_v, df_z, q_gain[l], k_gain[l],
                                    lambda_q1[l], lambda_k1[l], lambda_q2[l], lambda_k2[l],
                                    df_out_gain[l], lambda_init)
        mixed = jnp.concatenate([o_dn, o_df], axis=-1)
        x = x + jnp.einsum('btc,cd->btd', mixed, w_out[l])
    return x
```

```python
import numpy as np
import ml_dtypes
from contextlib import ExitStack
import concourse.bass as bass
import concourse.mybir as mybir
from concourse.bass_utils import run_bass_kernel_spmd

F32 = mybir.dt.float32
BF16 = mybir.dt.bfloat16
AF = mybir.ActivationFunctionType
ALU = mybir.AluOpType
AX = mybir.AxisListType

P = 128
T = 2048
D = 1024
NT = 16
KD = 8
EPS = 1e-6
IN_COLS = 4104
LAMBDA_INIT = 0.8 - 0.6
NPRM = 578

DEBUG = None


class Res:
    __slots__ = ("name", "w", "rs")

    def __init__(self, name):
        self.name = name
        self.w = None
        self.rs = []


class Op:
    __slots__ = ("eng", "fn", "deps", "idx", "signal", "cnt", "chan", "waits", "dur")


class Prog:
    DEF_DUR = {"pe": 0.5, "act": 0.6, "dve": 0.7, "pool": 1.5, "sp": 3.0}

    def __init__(self):
        self.ops = []
        self.chans = {}
        self.cuts = []

    def cut(self):
        self.cuts.append(len(self.ops))

    def schedule(self, lat=0.25):
        bounds = [0] + [c for c in self.cuts if 0 < c < len(self.ops)] + [len(self.ops)]
        new = []
        for lo, hi in zip(bounds[:-1], bounds[1:]):
            if hi > lo:
                new += self._sched(self.ops[lo:hi], lat)
        self.ops = new
        for k, op in enumerate(self.ops):
            op.idx = k

    @staticmethod
    def _sched(ops, lat):
        n = len(ops)
        pos = {id(op): k for k, op in enumerate(ops)}
        preds = [[pos[id(d)] for d in op.deps if id(d) in pos] for op in ops]
        succs = [[] for _ in range(n)]
        for k, pl in enumerate(preds):
            for p in pl:
                succs[p].append(k)
        dur = [op.dur for op in ops]
        busy = [0.75 * op.dur if op.chan is not None else op.dur for op in ops]
        prio = [0.0] * n
        for k in range(n - 1, -1, -1):
            m = 0.0
            for s_ in succs[k]:
                if prio[s_] > m:
                    m = prio[s_]
            prio[k] = dur[k] + m
        indeg = [len(pl) for pl in preds]
        rtime = [0.0] * n
        ready = {}
        for k in range(n):
            if indeg[k] == 0:
                ready.setdefault(ops[k].eng, []).append(k)
        free = {}
        order = []
        while len(order) < n:
            best = None
            for e, lst in ready.items():
                if not lst:
                    continue
                t_e = free.get(e, 0.0)
                cand = None
                for k in lst:
                    st = rtime[k] if rtime[k] > t_e else t_e
                    key = (st, -prio[k], k)
                    if cand is None or key < cand[0]:
                        cand = (key, k)
                if best is None or cand[0] < best[0]:
                    best = (cand[0], cand[1], e)
            (st, _, _), k, e = best
            ready[e].remove(k)
            free[e] = st + busy[k]
            fin = st + dur[k]
            order.append(k)
            for s_ in succs[k]:
                if rtime[s_] < fin + lat:
                    rtime[s_] = fin + lat
                indeg[s_] -= 1
                if indeg[s_] == 0:
                    ready.setdefault(ops[s_].eng, []).append(s_)
        return [ops[k] for k in order]

    def mark(self, name):
        import os
        if os.environ.get("KSTOP") == name:
            self.frozen = True

    def add(self, eng, fn, reads=(), writes=(), chan=None, dur=None):
        if getattr(self, "frozen", False):
            return None
        op = Op()
        op.dur = self.DEF_DUR[eng] if dur is None else dur
        op.eng = eng
        op.fn = fn
        op.chan = chan
        op.idx = len(self.ops)
        op.signal = False
        op.cnt = 0
        deps = {}
        for r in reads:
            if r.w is not None:
                deps[r.w.idx] = r.w
        for r in writes:
            if r.w is not None:
                deps[r.w.idx] = r.w
            for rd in r.rs:
                deps[rd.idx] = rd
        op.deps = list(deps.values())
        for r in reads:
            r.rs.append(op)
        for r in writes:
            r.w = op
            r.rs = []
        self.ops.append(op)
        return op

    def barrier(self, mk, pe_extra=(), rd=()):
        rs = {e: Res("bar_" + e) for e in ("pe", "act", "dve", "pool")}
        for e in ("pe", "act", "dve", "pool"):
            self.add(e, mk(e), reads=list(rd), writes=[rs[e]] + (list(pe_extra) if e == "pe" else []))
        allr = list(rs.values())
        for e in ("pe", "act", "dve", "pool"):
            self.add(e, mk(e), reads=allr + list(rd), writes=[Res("bar2_" + e)] + (list(pe_extra) if e == "pe" else []))
        self.add("sp", None, reads=allr)

    def plan(self):
        for op in self.ops:
            for d in op.deps:
                if d.chan is not None:
                    continue
                if d.eng == "pe" and op.eng == "pe":
                    continue
                d.signal = True
        cnt = {}
        for op in self.ops:
            if op.chan is not None:
                c = self.chans.get(op.chan, 0) + 16
                self.chans[op.chan] = c
                op.cnt = c
            elif op.signal:
                c = cnt.get(op.eng, 0) + 1
                cnt[op.eng] = c
                op.cnt = c
        seen = {}
        for op in self.ops:
            need = {}
            for d in op.deps:
                if d.chan is not None:
                    key = ("c", d.chan)
                elif d.eng == "pe" and op.eng == "pe":
                    continue
                else:
                    key = ("e", d.eng)
                if need.get(key, 0) < d.cnt:
                    need[key] = d.cnt
            s = seen.setdefault(op.eng, {})
            w = []
            for key, v in need.items():
                if s.get(key, 0) < v:
                    s[key] = v
                    w.append((key, v))
            op.waits = w


class Buf:
    def __init__(self, t, F, dtype):
        self.t = t
        self.F = F
        self.dtype = dtype

    def ap(self, off=0, dims=None, p0=0, np_=P):
        if dims is None:
            dims = [(1, self.F - off)]
        return bass.AP(self.t, p0 * self.F + off, [[self.F, np_]] + [[s, c] for s, c in dims])


def build_nc(debug=None):
    nc = bass.Bass("TRN2", target_bir_lowering=False)
    x_d = nc.dram_tensor("x", [T, D], F32, kind="ExternalInput").ap()
    win_d = nc.dram_tensor("w_in", [D, IN_COLS], F32, kind="ExternalInput").ap()
    wout_d = nc.dram_tensor("w_out", [D, D], F32, kind="ExternalInput").ap()
    prm_d = nc.dram_tensor("prm", [P, NPRM], F32, kind="ExternalInput").ap()
    cF_d = nc.dram_tensor("cF", [P, 5 * 128], F32, kind="ExternalInput").ap()
    cB_d = nc.dram_tensor("cB", [P, 5 * 128], BF16, kind="ExternalInput").ap()
    y_d = nc.dram_tensor("y", [T, D], F32, kind="ExternalOutput").ap()
    dbg_d = {}
    if debug:
        for k, shp in debug.items():
            dbg_d[k] = nc.dram_tensor("dbg_" + k, list(shp), F32, kind="ExternalOutput").ap()

    pg = Prog()
    es = ExitStack()

    def sb(name, F, dtype):
        t = es.enter_context(nc.sbuf_tensor("sb_" + name, [P, F], dtype))
        return Buf(t, F, dtype)

    cF = sb("cF", 5 * 128, F32)
    cB = sb("cB", 5 * 128, BF16)
    prm = sb("prm", NPRM, F32)
    hT = sb("hT", KD * T, BF16)
    mixT = hT
    small = sb("small", 768, F32)
    REG_BYTES = 167 * 1024 - 256
    reg = sb("reg", REG_BYTES // 2, BF16)
    regf = Buf(reg.t.bitcast(F32), REG_BYTES // 4, F32)

    class Carver:
        def __init__(self, lo, hi):
            self.lo = lo
            self.hi = hi
            self.off = lo

        def reset(self):
            self.off = self.lo

        def take(self, nelem, dtype):
            bpe = 4 if dtype == F32 else 2
            self.off = (self.off + 3) // 4 * 4
            o = self.off
            self.off += nelem * bpe
            assert self.off <= self.hi, ("region overflow", self.off, self.hi)
            base = regf if dtype == F32 else reg
            return View(base, o // bpe, nelem)

    class View:
        def __init__(self, base, off, n):
            self.base = base
            self.off = off
            self.F = n
            self.dtype = base.dtype

        def ap(self, off=0, dims=None, p0=0, np_=P):
            if dims is None:
                dims = [(1, self.F - off)]
            return self.base.ap(self.off + off, dims, p0, np_)

    KB = 1024
    cvG = Carver(0, 64 * KB)
    cvD = Carver(64 * KB, 64 * KB + 65792)
    cvT = Carver(64 * KB + 65792, REG_BYTES)
    R2 = lambda nm: [Res(nm + "0"), Res(nm + "1")]
    H4 = [(128, 4), (1, 128)]

    class Rec:
        def __init__(self):
            self.items = []

        def add(self, eng, fn, reads=(), writes=(), chan=None, cost=0, dur=None):
            if eng == "pe" and cost == 0:
                cost = 4
            if dur is None and eng == "pe":
                dur = 0.06 + 0.115 * cost
            self.items.append((cost, eng, fn, list(reads), list(writes), chan, dur))

    def merge_streams(a, b):
        import os
        if os.environ.get("KOPS"):
            a.items = a.items[:int(os.environ["KOPS"])]
            print("KOPS", len(a.items), [ (i, it[1], it[2].__code__.co_firstlineno) for i, it in enumerate(a.items)][-3:])
        if os.environ.get("KSEQ") in ("1", "2", "3"):
            its = {"1": a.items + b.items, "2": a.items, "3": b.items}[os.environ.get("KSEQ")]
            for it in its:
                pg.add(it[1], it[2], reads=it[3], writes=it[4], chan=it[5], dur=it[6])
            return
        ta = float(sum(it[0] for it in a.items)) or 1.0
        tb = float(sum(it[0] for it in b.items)) or 1.0
        ia = ib = 0
        ca = cb = 0.0
        while ia < len(a.items) or ib < len(b.items):
            if ib >= len(b.items) or (ia < len(a.items) and ca / ta <= cb / tb):
                it = a.items[ia]
                ia += 1
                ca += it[0]
            else:
                it = b.items[ib]
                ib += 1
                cb += it[0]
            pg.add(it[1], it[2], reads=it[3], writes=it[4], chan=it[5], dur=it[6])

    ps = []
    psb = []
    for i in range(8):
        t = es.enter_context(nc.psum_tensor(f"ps{i}", [P, 512], F32))
        ps.append(Buf(t, 512, F32))
        psb.append(Buf(t.bitcast(BF16), 1024, BF16))
    psr = [Res(f"ps{i}") for i in range(8)]

    sems = {}

    def getsem(key):
        if key not in sems:
            sems[key] = es.enter_context(nc.semaphore("s_" + "_".join(str(k) for k in key)))
        return sems[key]

    BT = cF.ap(0, [(1, 128)])
    YS = cF.ap(128, [(1, 128)])
    ONESF = cF.ap(256, [(1, 128)])
    IDB = cB.ap(0, [(1, 128)])
    ONESB = cB.ap(128, [(1, 128)])
    NEGM = cB.ap(256, [(1, 128)])
    BLK2 = cB.ap(384, [(1, 128)])
    YSB = cB.ap(512, [(1, 128)])
    r_const = Res("const")

    def dma(out, in_, chan, reads=(), writes=(), eng="sp", dur=4.5):
        pg.add(eng, lambda e: e.dma_start(out=out, in_=in_), reads=reads, writes=writes, chan=chan, dur=dur)

    dma(cF.ap(), cF_d, "prm", writes=[r_const])
    dma(cB.ap(), cB_d, "prm", writes=[r_const])
    dma(prm.ap(), prm_d, "prm", writes=[r_const])

    def mk_bar(e):
        if e == "pe":
            return lambda eng: eng.matmul(ps[7].ap(0, [(1, 8)], 0, 8), cB.ap(0, [(1, 8)], 0, 8), cB.ap(0, [(1, 8)], 0, 8), start=True, stop=True)
        col = {"act": 740, "dve": 744, "pool": 748}[e]
        if e == "act":
            return lambda eng: eng.copy(small.ap(col, [(1, 2)]), prm.ap(0, [(1, 2)]))
        return lambda eng: eng.tensor_copy(small.ap(col, [(1, 2)]), prm.ap(0, [(1, 2)]))

    def barrier():
        pg.cut()
        pg.barrier(mk_bar, pe_extra=[psr[7]], rd=[r_const])
        pg.cut()

    cv = cvT
    cv.reset()
    xt = [cv.take(D, F32) for _ in range(4)]
    xs = [cv.take(D, BF16) for _ in range(2)]
    junk = cv.take(D, BF16)
    r_xt = [Res(f"xt{j}") for j in range(4)]
    r_xs = [Res("xs0"), Res("xs1")]
    r_ss = [Res(f"ss{i}") for i in range(NT)]
    r_junk = Res("junk")
    r_hT = [Res(f"hT{i}") for i in range(NT)]
    SS0 = 0
    RS0 = 16
    for i in range(NT):
        s = i % 4
        s2 = i % 2
        dma(xt[s].ap(), x_d[i * P:(i + 1) * P, :], f"x{s}", writes=[r_xt[s]])
        pg.add("act", lambda e, s=s, i=i: e.activation(junk.ap(), xt[s].ap(), AF.Square, accum_out=small.ap(SS0 + i, [(1, 1)])),
               reads=[r_xt[s]], writes=[r_ss[i], r_junk], dur=1.1)
        pg.add("act", lambda e, i=i: e.activation(small.ap(RS0 + i, [(1, 1)]), small.ap(SS0 + i, [(1, 1)]), AF.Ln, bias=float(D * EPS), scale=1.0),
               reads=[r_ss[i]], writes=[r_ss[i]], dur=0.25)
        pg.add("act", lambda e, i=i: e.activation(small.ap(RS0 + i, [(1, 1)]), small.ap(RS0 + i, [(1, 1)]), AF.Exp, scale=-0.5),
               reads=[r_ss[i]], writes=[r_ss[i]], dur=0.25)
        pg.add("dve", lambda e, s=s, s2=s2, i=i: e.tensor_scalar(xs[s2].ap(), xt[s].ap(), small.ap(RS0 + i, [(1, 1)]), float(np.sqrt(D)), ALU.mult, ALU.mult),
               reads=[r_xt[s], r_ss[i], r_const], writes=[r_xs[s2]])
        b = i % 2

        def tr(e, s=s2, b=b):
            ins = None
            for kd in range(KD):
                ins = e.transpose(psb[b].ap(kd * 128, [(1, 128)]), xs[s].ap(kd * 128, [(1, 128)]), IDB)
            return ins
        pg.add("pe", tr, reads=[r_xs[s2], r_const], writes=[psr[b]], dur=1.0)
        pg.add("act", lambda e, i=i, b=b: e.copy(hT.ap(i * P, [(T, KD), (1, P)]), psb[b].ap(0, [(128, KD), (1, 128)])),
               reads=[psr[b]], writes=[r_hT[i]], dur=1.05)

    pg.mark("P1")


    def mm_acc(e, out, pairs):
        ins = None
        n = len(pairs)
        for idx, (l, r) in enumerate(pairs):
            ins = e.matmul(out, l, r, start=(idx == 0), stop=(idx == n - 1))
        return ins

    GAINB = prm.ap(0, [(1, KD), (0, 128)])

    cvG.reset()
    gq = cvG.take(4 * T, BF16)
    gk = cvG.take(4 * T, BF16)
    vtok = cvG.take(16 * 512, BF16)
    zgdn = cvG.take(16 * 512, BF16)
    cv = cvD
    cv.reset()
    wst = [cv.take(KD * 256, F32) for _ in range(2)]
    wb = [cv.take(KD * 512, BF16) for _ in range(2)]
    r_wst = [Res("wst0"), Res("wst1")]
    r_wbq = [[Res(f"wb{s}_{q}") for q in range(4)] for s in range(2)]
    wsm = cv.take(KD * 8, BF16)
    wsmf = cv.take(KD * 8, F32)
    raw = [cv.take(4 + T, BF16) for _ in range(2)]
    dgw = [cv.take(4 * 128, BF16) for _ in range(2)]
    accb = [cv.take(512, F32) for _ in range(3)]
    r_accb = [Res(f"accb{j}") for j in range(3)]
    acc_rr = [0]
    r_dgw = [Res("dgw0"), Res("dgw1")]
    sqb = cv.take(T, BF16)
    r_raw = [[Res(f"raw{b}_{tb}") for tb in range(4)] for b in range(2)]
    r_stgf = Res("stgf")
    r_gq = [Res(f"gq{h}") for h in range(4)]
    r_gk = [Res(f"gk{h}") for h in range(4)]
    r_vtok = [Res(f"vtok{i}") for i in range(NT)]
    r_zgdn = [Res(f"zgdn{i}") for i in range(NT)]
    r_sqb = Res("sqb")
    r_lnv = [Res("lnv0"), Res("lnv1")]
    r_ba = Res("ba")
    BA0 = 64

    wq_count = [0]

    def load_wgroup(slot, c0, bufs, ncols=512, src=None, rows_src=None, pw=128, extra_w=(), cp="w"):
        wst, wb, r_wst, r_wbq = bufs
        src = win_d if src is None else src
        nq = pw // 128
        for q0 in range(0, ncols // 128, nq):
            st = wq_count[0] % 2
            wq_count[0] += 1
            dma(wst[st].ap(0, [(pw, KD), (1, pw)]),
                src.rearrange("(k p) c -> p k c", p=P)[:, :, c0 + q0 * 128:c0 + q0 * 128 + pw],
                f"{cp}{st}", writes=[r_wst[st]] + list(extra_w), dur=4.5 * nq)
            pg.add("pool", lambda e, st=st, slot=slot, q0=q0, wb=wb, wst=wst: e.tensor_tensor(
                wb[slot].ap(q0 * 128, [(512, KD), (1, pw)]), wst[st].ap(0, [(pw, KD), (1, pw)]), prm.ap(0, [(1, KD), (0, pw)]), ALU.mult),
                reads=[r_wst[st], r_const], writes=[r_wbq[slot][q0 + j] for j in range(nq)] + list(extra_w), dur=3.6 * nq)

    for b in range(2):
        pg.add("dve", lambda e, b=b: e.memset(raw[b].ap(0, [(1, 3)]), 0.0), writes=[r_raw[b][0]])

    r_wsm = Res("wsm")
    dma(wsmf.ap(0, [(8, KD), (1, 8)]), win_d.rearrange("(k p) c -> p k c", p=P)[:, :, 2048:2056], "wsm", writes=[r_wsm])
    pg.add("dve", lambda e: e.tensor_tensor(wsm.ap(0, [(8, KD), (1, 8)]), wsmf.ap(0, [(8, KD), (1, 8)]), prm.ap(0, [(1, KD), (0, 8)]), ALU.mult),
           reads=[r_wsm, r_const], writes=[r_wsm])

    def ba_mm(e):
        ins = None
        for i in range(NT):
            ins = mm_acc(e, ps[6].ap(i * 8, [(1, 8)]),
                         [(hT.ap(kd * T + i * P, [(1, P)]), wsm.ap(kd * 8, [(1, 8)])) for kd in range(KD)])
        return ins
    pg.add("pe", ba_mm, reads=[r_wsm] + r_hT, writes=[psr[6]])
    pg.add("act", lambda e: e.copy(small.ap(BA0, [(1, 128)]), ps[6].ap(0, [(1, 128)])), reads=[psr[6]], writes=[r_ba])

    groups = [(0, "q"), (512, "k"), (1024, "v"), (1536, "z")]
    bufsA = (wst, wb, r_wst, r_wbq)
    load_wgroup(0, groups[0][0], bufsA, pw=256)
    pcount = [0]
    chunk_count = [0]
    for gi, (c0, kind) in enumerate(groups):
        slot = gi % 2
        if gi + 1 < len(groups):
            load_wgroup((gi + 1) % 2, groups[gi + 1][0], bufsA, pw=256)
        if kind in ("q", "k", "v"):
            for h in range(4):
                ch = {"q": 0, "k": 4, "v": 8}[kind] + h
                rb = chunk_count[0] % 2
                chunk_count[0] += 1
                for tb in range(4):
                    pb = 2 + pcount[0] % 2
                    pcount[0] += 1
                    pg.add("pe", lambda e, pb=pb, slot=slot, h=h, tb=tb: mm_acc(
                        e, ps[pb].ap(), [(wb[slot].ap(kd * 512 + h * 128, [(1, 128)]), hT.ap(kd * T + tb * 512, [(1, 512)])) for kd in range(KD)]),
                        reads=[r_wbq[slot][h]] + r_hT[tb * 4:(tb + 1) * 4], writes=[psr[pb]], dur=2.15)
                    pg.add("act", lambda e, pb=pb, rb=rb, tb=tb: e.copy(raw[rb].ap(3 + tb * 512, [(1, 512)]), ps[pb].ap()),
                           reads=[psr[pb]], writes=[r_raw[rb][tb]])
                db = chunk_count[0] % 2
                pg.add("dve", lambda e, db=db, ch=ch: [e.tensor_scalar(dgw[db].ap(j * 128, [(1, 128)]), IDB, prm.ap(8 + ch * 4 + j, [(1, 1)]), None, ALU.mult) for j in (2, 3)][-1],
                       reads=[r_const], writes=[r_dgw[db]], dur=0.4)
                for tb in range(4):
                    pb = 4 + tb % 2
                    ab = acc_rr[0] % 3
                    acc_rr[0] += 1
                    pg.add("pe", lambda e, pb=pb, rb=rb, tb=tb, db=db: mm_acc(
                        e, ps[pb].ap(), [(dgw[db].ap(j * 128, [(1, 128)]), raw[rb].ap(tb * 512 + j, [(1, 512)])) for j in (2, 3)]),
                        reads=r_raw[rb] + [r_dgw[db]], writes=[psr[pb]], dur=0.6)
                    pg.add("dve", lambda e, pb=pb, rb=rb, tb=tb, ab=ab, ch=ch: e.scalar_tensor_tensor(
                        accb[ab].ap(), raw[rb].ap(tb * 512 + 1, [(1, 512)]), prm.ap(8 + ch * 4 + 1, [(1, 1)]), ps[pb].ap(), ALU.mult, ALU.add),
                        reads=r_raw[rb] + [psr[pb], r_const], writes=[r_accb[ab]], dur=0.65)
                    pg.add("dve", lambda e, rb=rb, tb=tb, ab=ab, ch=ch: e.scalar_tensor_tensor(
                        accb[ab].ap(), raw[rb].ap(tb * 512, [(1, 512)]), prm.ap(8 + ch * 4, [(1, 1)]), accb[ab].ap(), ALU.mult, ALU.add),
                        reads=r_raw[rb] + [r_accb[ab], r_const], writes=[r_accb[ab]], dur=0.65)
                    dst = {"q": gq, "k": gk}.get(kind)
                    if dst is not None:
                        rr = (r_gq if kind == "q" else r_gk)[h]
                        pg.add("act", lambda e, ab=ab, dst=dst, h=h, tb=tb: e.activation(dst.ap(h * T + tb * 512, [(1, 512)]), accb[ab].ap(), AF.Silu),
                               reads=[r_accb[ab]], writes=[rr], dur=0.55)
                    else:
                        pg.add("act", lambda e, ab=ab, tb=tb: e.activation(sqb.ap(tb * 512, [(1, 512)]), accb[ab].ap(), AF.Silu),
                               reads=[r_accb[ab]], writes=[r_sqb], dur=0.55)
                if kind == "v":
                    for g4 in range(4):
                        pb = g4 % 2

                        def trv(e, g4=g4, pb=pb):
                            ins = None
                            for q in range(4):
                                ins = e.transpose(psb[pb].ap(q * 128, [(1, 128)]), sqb.ap((g4 * 4 + q) * 128, [(1, 128)]), IDB)
                            return ins
                        pg.add("pe", trv, reads=[r_sqb, r_const], writes=[psr[pb]])
                        pg.add("dve", lambda e, g4=g4, pb=pb, h=h: e.tensor_copy(
                            vtok.ap((g4 * 4 * 4 + h) * 128, [(512, 4), (1, 128)]), psb[pb].ap(0, [(128, 4), (1, 128)])),
                            reads=[psr[pb]], writes=r_vtok[g4 * 4:(g4 + 1) * 4])
        else:
            for i in range(NT):
                pb = 6 + i % 2
                pg.add("pe", lambda e, pb=pb, slot=slot, i=i: mm_acc(
                    e, ps[pb].ap(), [(hT.ap(kd * T + i * P, [(1, P)]), wb[slot].ap(kd * 512, [(1, 512)])) for kd in range(KD)]),
                    reads=r_wbq[slot] + [r_hT[i]], writes=[psr[pb]], dur=2.15)
                pg.add("act", lambda e, pb=pb, i=i: e.activation(zgdn.ap(i * 512, [(1, 512)]), ps[pb].ap(), AF.Silu),
                       reads=[psr[pb]], writes=[r_zgdn[i]])
                pg.add("pool", lambda e, i=i: e.tensor_tensor(zgdn.ap(i * 512, [(128, 4), (1, 128)]), zgdn.ap(i * 512, [(128, 4), (1, 128)]),
                                                              prm.ap(64, [(0, 4), (1, 128)]), ALU.mult),
                       reads=[r_zgdn[i], r_const], writes=[r_zgdn[i]])

    cvD.reset()
    dQ = cvD.take(4 * T, BF16)
    dK = cvD.take(4 * T, BF16)
    VA = cvD.take(NT * 520, BF16)
    zgdf = cvD.take(NT * 512, BF16)
    cv = cvT
    cv.reset()
    wstB = [cv.take(KD * 128, F32) for _ in range(2)]
    wbB = [cv.take(KD * 512, BF16) for _ in range(2)]
    ND = 3
    sqd = [cv.take(512, BF16) for _ in range(ND)]
    yr = [cv.take(512, BF16) for _ in range(ND)]
    lnvB = [cv.take(512, F32) for _ in range(ND)]
    r_wstB = [Res("bwst0"), Res("bwst1")]
    r_wbqB = [[Res(f"bwb{s_}_{q}") for q in range(4)] for s_ in range(2)]
    r_sqd = [Res(f"sqd{j}") for j in range(ND)]
    r_yr = [Res(f"yr{j}") for j in range(ND)]
    r_lnvB = [Res(f"blnv{j}") for j in range(ND)]
    r_dQ = [Res(f"dQ{h}") for h in range(4)]
    r_dK = [Res(f"dK{h}") for h in range(4)]
    r_VA = [Res(f"VA{i}") for i in range(NT)]
    r_zgdf = [Res(f"zgdf{i}") for i in range(NT)]
    LNQ = float(-0.5 * np.log(128.0))
    PBK = (1, 2, 3)
    OBK = (4, 5, 0)

    def norm_tail(u, ob, kind_bias, fin):
        pass

    def l2_unit(kind, h, tb):
        u = cnt2[0] % ND
        cnt2[0] += 1
        ob = OBK[u]
        buf = gq if kind == "q" else gk
        rr = (r_gq if kind == "q" else r_gk)[h]
        sl = lambda: buf.ap(h * T + tb * 512, [(1, 512)])
        pg.add("act", lambda e: e.activation(sqd[u].ap(), sl(), AF.Square), reads=[rr], writes=[r_sqd[u]], dur=0.55)
        pg.add("pe", lambda e: e.matmul(ps[ob].ap(), ONESB, sqd[u].ap(), start=True, stop=True), reads=[r_sqd[u], r_const], writes=[psr[ob]], dur=0.3)
        pg.add("act", lambda e: e.activation(lnvB[u].ap(), ps[ob].ap(), AF.Ln, bias=float(EPS), scale=1.0), reads=[psr[ob]], writes=[r_lnvB[u]], dur=0.55)
        pg.add("act", lambda e: e.activation(lnvB[u].ap(), lnvB[u].ap(), AF.Exp, bias=(LNQ if kind == "q" else 0.0), scale=-0.5),
               reads=[r_lnvB[u]], writes=[r_lnvB[u]], dur=0.55)
        pg.add("dve", lambda e: e.tensor_tensor(sl(), sl(), lnvB[u].ap(), ALU.mult), reads=[r_lnvB[u], rr], writes=[rr], dur=0.65)

    groupsB = [(2056, "Q"), (2568, "K"), (3080, "V"), (3592, "Z")]
    bufsB = (wstB, wbB, r_wstB, r_wbqB)
    load_wgroup(0, groupsB[0][0], bufsB, extra_w=r_xt + r_xs + [r_junk], cp="wb")
    barrier()
    pg.mark("A1")
    pg.add("pool", lambda e: e.memset(VA.ap(128, [(130, 64), (1, 1)]), 1.0), writes=r_VA)
    cnt2 = [0]
    for gi, (c0, kind) in enumerate(groupsB):
        slot = gi % 2
        if gi + 1 < len(groupsB):
            load_wgroup((gi + 1) % 2, groupsB[gi + 1][0], bufsB, cp="wb")
        if kind in ("Q", "K"):
            dst = dQ if kind == "Q" else dK
            rdst = r_dQ if kind == "Q" else r_dK
            gcol = 320 if kind == "Q" else 321
            ebias = 0.0 if kind == "Q" else float(np.log(8.0))
            for h in range(4):
                for tb in range(4):
                    l2_unit("q" if kind == "Q" else "k", h, tb)
                    u = cnt2[0] % ND
                    cnt2[0] += 1
                    pb, ob = PBK[u], OBK[u]
                    pg.add("pe", lambda e, pb=pb, slot=slot, h=h, tb=tb: mm_acc(
                        e, ps[pb].ap(), [(wbB[slot].ap(kd * 512 + h * 128, [(1, 128)]), hT.ap(kd * T + tb * 512, [(1, 512)])) for kd in range(KD)]),
                        reads=[r_wbqB[slot][h]] + r_hT[tb * 4:(tb + 1) * 4], writes=[psr[pb]], dur=2.15)
                    pg.add("act", lambda e, pb=pb, u=u: e.activation(sqd[u].ap(), ps[pb].ap(), AF.Square), reads=[psr[pb]], writes=[r_sqd[u]], dur=0.55)
                    pg.add("dve", lambda e, pb=pb, u=u: e.tensor_copy(yr[u].ap(), ps[pb].ap()), writes=[psr[pb], r_yr[u]], dur=0.65)
                    pg.add("pe", lambda e, ob=ob, u=u: e.matmul(ps[ob].ap(), BLK2, sqd[u].ap(), start=True, stop=True),
                           reads=[r_sqd[u], r_const], writes=[psr[ob]], dur=0.3)
                    pg.add("act", lambda e, ob=ob, u=u: e.activation(lnvB[u].ap(), ps[ob].ap(), AF.Ln, bias=float(64 * EPS), scale=1.0),
                           reads=[psr[ob]], writes=[r_lnvB[u]], dur=0.55)
                    pg.add("act", lambda e, u=u, ebias=ebias: e.activation(lnvB[u].ap(), lnvB[u].ap(), AF.Exp, bias=ebias, scale=-0.5),
                           reads=[r_lnvB[u]], writes=[r_lnvB[u]], dur=0.55)
                    pg.add("dve", lambda e, u=u, dst=dst, h=h, tb=tb, gcol=gcol: e.scalar_tensor_tensor(
                        dst.ap(h * T + tb * 512, [(1, 512)]), yr[u].ap(), prm.ap(gcol, [(1, 1)]), lnvB[u].ap(), ALU.mult, ALU.mult),
                        reads=[r_yr[u], r_lnvB[u], r_const], writes=[rdst[h]], dur=0.65)
        else:
            if kind == "Z":
                pg.cut()
            for i in range(NT):
                pb = 6 + i % 2
                pg.add("pe", lambda e, pb=pb, slot=slot, i=i: mm_acc(
                    e, ps[pb].ap(), [(hT.ap(kd * T + i * P, [(1, P)]), wbB[slot].ap(kd * 512, [(1, 512)])) for kd in range(KD)]),
                    reads=r_wbqB[slot] + [r_hT[i]], writes=[psr[pb]], dur=2.15)
                if kind == "V":
                    pg.add("act", lambda e, pb=pb, i=i: e.copy(VA.ap(i * 520, [(130, 4), (1, 128)]), ps[pb].ap(0, H4)), reads=[psr[pb]], writes=[r_VA[i]])
                else:
                    pg.add("act", lambda e, pb=pb, i=i: e.activation(zgdf.ap(i * 512, [(1, 512)]), ps[pb].ap(), AF.Silu), reads=[psr[pb]], writes=[r_zgdf[i]])
                    pg.add("pool", lambda e, i=i: e.tensor_tensor(zgdf.ap(i * 512, H4), zgdf.ap(i * 512, H4), prm.ap(192, [(0, 4), (1, 128)]), ALU.mult),
                           reads=[r_zgdf[i], r_const], writes=[r_zgdf[i]])
    if debug and "dQ" in debug:
        for nm, buf, rl in (("dQ", dQ, r_dQ), ("dK", dK, r_dK)):
            for h in range(4):
                for q in range(4):
                    pg.add("dve", lambda e, buf=buf, h=h, q=q: e.tensor_copy(lnvB[0].ap(), buf.ap(h * T + q * 512, [(1, 512)])), reads=rl, writes=[r_lnvB[0]])
                    dma(dbg_d[nm][h * P:(h + 1) * P, q * 512:(q + 1) * 512], lnvB[0].ap(), "dbg", reads=[r_lnvB[0]], writes=[Res("dbgout")])
        for i in range(NT):
            pg.add("dve", lambda e, i=i: e.tensor_copy(lnvB[0].ap(), zgdf.ap(i * 512, [(1, 512)])), reads=r_zgdf, writes=[r_lnvB[0]])
            dma(dbg_d["zgdf"][i * P:(i + 1) * P, :], lnvB[0].ap(), "dbg", reads=[r_lnvB[0]], writes=[Res("dbgout")])
            pg.add("dve", lambda e, i=i: e.tensor_copy(lnvB[0].ap(0, [(130, 4), (1, 128)]), VA.ap(i * 520, [(130, 4), (1, 128)])), reads=r_VA, writes=[r_lnvB[0]])
            pg.add("dve", lambda e, i=i: e.tensor_copy(lnvB[0].ap(128, [(130, 4), (1, 2)]), VA.ap(i * 520 + 128, [(130, 4), (1, 2)])), reads=r_VA, writes=[r_lnvB[0]])
            dma(dbg_d["VA"][i * P:(i + 1) * P, :], lnvB[0].ap(), "dbg", reads=[r_lnvB[0]], writes=[Res("dbgout")])
    barrier()
    pg.mark("B1")


    cv = cvT
    cv.reset()
    class _Proxy:
        cur = None

        def add(self, *a, **k):
            _Proxy.cur.add(*a, **k)
    sg = _Proxy()
    g_prep = Rec()
    _Proxy.cur = g_prep
    f4 = lambda: cv.take(512, F32)
    b4 = lambda: cv.take(512, BF16)
    Xf = [f4()]
    Ef = [b4()]
    ETf = [b4()]
    egcf = [b4()]
    otok = [b4()]
    S32 = f4()
    S16 = b4()
    Pb = [[b4(), b4()]]
    Qb = [[b4(), b4()]]
    Rb = [[b4(), b4()]]
    AqkT = [b4(), b4()]
    qg = [b4(), b4()]
    nkbg = [b4()]
    kdec = [b4(), b4()]
    vb = [b4()]
    ub = [b4(), b4()]
    nwT = [b4(), b4()]
    vn16 = [b4()]
    mixtok = [b4()]
    sqo = mixtok[0]
    r_X, r_E, r_ET, r_egc, r_otok = R2("X"), R2("E"), R2("ET"), R2("egc"), R2("otok")
    r_P = [R2("P0_"), R2("P1_")]
    r_Q = [R2("Q0_"), R2("Q1_")]
    r_R = [R2("R0_"), R2("R1_")]
    r_Aqk, r_qg, r_nkbg, r_kdec, r_vb, r_u, r_nwT, r_vn, r_mixtok = (R2("Aqk"), R2("qg"), R2("nkbg"), R2("kdec"), R2("vb"),
                                                                    R2("u"), R2("nwT"), R2("vn"), R2("mixtok"))
    r_S32, r_S16 = Res("S32"), Res("S16")
    r_sqo = r_mixtok[0]
    r_g = Res("gsmall")
    r_egl = [Res(f"egl{i}") for i in range(NT)]
    r_ssn = Res("ssn")
    r_mixT = [Res(f"mixT{i}") for i in range(NT)]
    BETA0, NB0, G0, AL0, NBG0, KDS0, EGL0, SSN0, TMP0 = 192, 256, 320, 384, 392, 456, 520, 648, 656
    MSB = cF.ap(384, [(0, 4), (1, 128)])
    MITB = cF.ap(512, [(0, 4), (1, 128)])
    BTB = cF.ap(0, [(0, 4), (1, 128)])
    IDB4 = cB.ap(0, [(0, 4), (1, 128)])

    def colb(base, i):
        return small.ap(base + i * 4, [(1, 4), (0, 128)])

    bank_rr = [0]

    def nb():
        bank_rr[0] = (bank_rr[0] + 1) % 2
        return bank_rr[0]

    TH = [(4, 16), (1, 4)]
    sg.add("act", lambda e: e.activation(small.ap(BETA0, TH), small.ap(BA0, [(8, 16), (1, 4)]), AF.Exp, scale=-1.0), reads=[r_ba], writes=[r_g])
    sg.add("dve", lambda e: e.tensor_scalar(small.ap(BETA0, TH), small.ap(BETA0, TH), 1.0, None, ALU.add), reads=[r_g], writes=[r_g])
    sg.add("dve", lambda e: e.reciprocal(small.ap(BETA0, TH), small.ap(BETA0, TH)), reads=[r_g], writes=[r_g])
    sg.add("dve", lambda e: e.tensor_scalar(small.ap(NB0, TH), small.ap(BETA0, TH), -1.0, None, ALU.mult), reads=[r_g], writes=[r_g])
    sg.add("dve", lambda e: e.tensor_tensor(small.ap(G0, TH), small.ap(BA0 + 4, [(8, 16), (1, 4)]), prm.ap(60, [(0, 16), (1, 4)]), ALU.add),
           reads=[r_ba, r_const, r_g], writes=[r_g])
    sg.add("act", lambda e: e.activation(small.ap(G0, TH), small.ap(G0, TH), AF.Exp), reads=[r_g], writes=[r_g])
    sg.add("act", lambda e: e.activation(small.ap(G0, TH), small.ap(G0, TH), AF.Ln, bias=1.0, scale=1.0), reads=[r_g], writes=[r_g])
    sg.add("act", lambda e: e.activation(small.ap(AL0, [(1, 4)]), prm.ap(56, [(1, 4)]), AF.Exp), reads=[r_const, r_g], writes=[r_g])
    sg.add("dve", lambda e: e.scalar_tensor_tensor(small.ap(G0, TH), small.ap(G0, TH), -1.0, small.ap(AL0, [(0, 16), (1, 4)]), ALU.mult, ALU.mult),
           reads=[r_g], writes=[r_g])
    sg.add("pe", lambda e: e.matmul(ps[0].ap(0, [(1, 64)]), BT, small.ap(G0, [(1, 64)]), start=True, stop=True), reads=[r_g, r_const], writes=[psr[0]])
    sg.add("pe", lambda e: e.matmul(ps[1].ap(0, [(1, 64)]), YS, small.ap(G0, [(1, 64)]), start=True, stop=True), reads=[r_g, r_const], writes=[psr[1]])
    sg.add("act", lambda e: e.activation(small.ap(NBG0, [(1, 64)]), ps[0].ap(0, [(1, 64)]), AF.Exp), reads=[psr[0], r_g], writes=[r_g])
    sg.add("dve", lambda e: e.tensor_tensor(small.ap(NBG0, [(1, 64)]), small.ap(NBG0, [(1, 64)]), small.ap(NB0, [(1, 64)]), ALU.mult), reads=[r_g], writes=[r_g])
    sg.add("act", lambda e: e.activation(small.ap(KDS0, [(1, 64)]), ps[1].ap(0, [(1, 64)]), AF.Exp), reads=[psr[1], r_g], writes=[r_g])
    sg.add("dve", lambda e: e.memset(S32.ap(), 0.0), writes=[r_S32])
    sg.add("dve", lambda e: e.memset(S16.ap(), 0.0), writes=[r_S16])

    def mm4(e, bank, lhs, rhs, bf=False, ident_rhs=None):
        ins = None
        for h in range(4):
            out = (psb if bf else ps)[bank].ap(h * 128, [(1, 128)])
            if ident_rhs is None:
                ins = e.matmul(out, lhs(h), rhs(h), start=True, stop=True)
            else:
                e.matmul(out, lhs(h), rhs(h), start=True, stop=False)
                ins = e.matmul(out, IDB, ident_rhs(h), start=False, stop=True)
        return ins

    hv = lambda buf: (lambda h: buf.ap(h * 128, [(1, 128)]))
    copy_rr = [0]

    def evac(bank, dst, r_dst, bf_src=False):
        copy_rr[0] += 1
        src = (psb if bf_src else ps)[bank].ap(0, [(1, 512)])
        if copy_rr[0] % 3:
            sg.add("act", lambda e: e.copy(dst.ap(), src), reads=[psr[bank]], writes=[r_dst])
        else:
            sg.add("dve", lambda e: e.tensor_copy(dst.ap(), src), reads=[psr[bank]], writes=[r_dst])

    def gdn_par(i):
        b = 0
        bb = i % 2
        sg.add("dve", lambda e, b=b, bb=bb, i=i: e.tensor_tensor(Xf[b].ap(0, H4), BTB, colb(G0, i), ALU.mult), reads=[r_g, r_const], writes=[r_X[b]])
        Xhi, Xlo = vb[0], nkbg[0]
        sg.add("dve", lambda e, b=b: e.tensor_copy(Xhi.ap(), Xf[b].ap()), reads=[r_X[b]], writes=[r_vb[0]], dur=0.4)
        sg.add("dve", lambda e, b=b: e.tensor_tensor(Xlo.ap(), Xf[b].ap(), Xhi.ap(), ALU.subtract), reads=[r_X[b], r_vb[0]], writes=[r_nkbg[0]], dur=0.7)
        r_hl = [r_vb[0], r_nkbg[0], r_const]

        def hl4(e, k0):
            ins = None
            for h in range(4):
                out = ps[k0].ap(h * 128, [(1, 128)])
                e.matmul(out, Xhi.ap(h * 128, [(1, 128)]), YSB, start=True, stop=False)
                ins = e.matmul(out, Xlo.ap(h * 128, [(1, 128)]), YSB, start=False, stop=True)
            return ins

        def hl1(e, k, lhs):
            e.matmul(ps[k].ap(), lhs, Xhi.ap(), start=True, stop=False)
            return e.matmul(ps[k].ap(), lhs, Xlo.ap(), start=False, stop=True)
        k0 = nb()
        sg.add("pe", lambda e, k0=k0: hl4(e, k0), reads=r_hl, writes=[psr[k0]], cost=8)
        sg.add("act", lambda e, b=b, bb=bb, k0=k0: e.activation(Ef[b].ap(), ps[k0].ap(), AF.Exp), reads=[psr[k0]], writes=[r_E[b]])
        k1 = nb()
        sg.add("pe", lambda e, k1=k1: hl1(e, k1, YSB), reads=r_hl, writes=[psr[k1]], cost=5)
        sg.add("act", lambda e, b=b, bb=bb, k1=k1: e.activation(ETf[b].ap(), ps[k1].ap(), AF.Exp), reads=[psr[k1]], writes=[r_ET[b]])
        k2 = nb()
        sg.add("pe", lambda e, k2=k2: hl1(e, k2, ONESB), reads=r_hl, writes=[psr[k2]], cost=5)
        sg.add("act", lambda e, b=b, bb=bb, k2=k2: e.activation(egcf[b].ap(), ps[k2].ap(), AF.Exp), reads=[psr[k2]], writes=[r_egc[b]])
        sg.add("act", lambda e, k2=k2, i=i: e.activation(small.ap(EGL0 + i * 8, [(2, 4), (1, 2)]), ps[k2].ap(63, [(128, 4), (64, 2)]), AF.Exp),
               reads=[psr[k2]], writes=[r_egl[i]], dur=0.25)
        sg.add("dve", lambda e, b=b, bb=bb: e.tensor_tensor(Ef[b].ap(0, H4), Ef[b].ap(0, H4), MSB, ALU.mult), reads=[r_E[b], r_const], writes=[r_E[b]])
        sg.add("dve", lambda e, b=b, bb=bb, i=i: e.tensor_tensor(Ef[b].ap(0, H4), Ef[b].ap(0, H4), colb(NB0, i), ALU.mult), reads=[r_E[b], r_g], writes=[r_E[b]])
        sg.add("dve", lambda e, b=b, bb=bb: e.tensor_tensor(ETf[b].ap(0, H4), ETf[b].ap(0, H4), MITB, ALU.mult), reads=[r_ET[b], r_const], writes=[r_ET[b]])
        gkt = lambda h, i=i: gk.ap(h * T + i * P, [(1, P)])
        gqt = lambda h, i=i: gq.ap(h * T + i * P, [(1, P)])
        k3, k4 = nb(), nb()
        sg.add("pe", lambda e, k3=k3, gkt=gkt: mm4(e, k3, gkt, gkt), reads=r_gk, writes=[psr[k3]])
        sg.add("pe", lambda e, k4=k4, gkt=gkt, gqt=gqt: mm4(e, k4, gkt, gqt), reads=r_gk + r_gq, writes=[psr[k4]])
        sg.add("dve", lambda e, b=b, bb=bb, k3=k3: e.tensor_tensor(Pb[b][0].ap(), ps[k3].ap(), Ef[b].ap(), ALU.mult), reads=[psr[k3], r_E[b]], writes=[r_P[b][0]])
        sg.add("dve", lambda e, b=b, bb=bb, k4=k4: e.tensor_tensor(AqkT[bb].ap(), ps[k4].ap(), ETf[b].ap(), ALU.mult), reads=[psr[k4], r_ET[b]], writes=[r_Aqk[bb]])
        k5 = nb()

        def trP(e, b=b, k5=k5):
            ins = None
            for h in range(4):
                ins = e.transpose(psb[k5].ap(h * 128, [(1, 128)]), Pb[b][0].ap(h * 128, [(1, 128)]), IDB)
            return ins
        sg.add("pe", trP, reads=[r_P[b][0], r_const], writes=[psr[k5]])
        evac(k5, Qb[b][0], r_Q[b][0], bf_src=True)
        sg.add("dve", lambda e, b=b, bb=bb: e.tensor_tensor(Rb[b][0].ap(0, H4), Qb[b][0].ap(0, H4), IDB4, ALU.add), reads=[r_Q[b][0], r_const], writes=[r_R[b][0]])
        for k in range(5):
            c, n = k % 2, (k + 1) % 2
            kp = nb()
            sg.add("pe", lambda e, b=b, bb=bb, c=c, kp=kp: mm4(e, kp, hv(Qb[b][c]), hv(Pb[b][c])), reads=[r_Q[b][c], r_P[b][c]], writes=[psr[kp]])
            if k < 4:
                kq = nb()
                sg.add("pe", lambda e, b=b, bb=bb, c=c, kq=kq: mm4(e, kq, hv(Pb[b][c]), hv(Qb[b][c])), reads=[r_Q[b][c], r_P[b][c]], writes=[psr[kq]])
            evac(kp, Pb[b][n], r_P[b][n])
            if k < 4:
                evac(kq, Qb[b][n], r_Q[b][n])
            kr = nb()
            sg.add("pe", lambda e, b=b, bb=bb, c=c, n=n, kr=kr: mm4(e, kr, hv(Pb[b][n]), hv(Rb[b][c])),
                   reads=[r_P[b][n], r_R[b][c]], writes=[psr[kr]], cost=4)
            sg.add("dve", lambda e, b=b, bb=bb, c=c, n=n, kr=kr: e.tensor_tensor(Rb[b][n].ap(), ps[kr].ap(), Rb[b][c].ap(), ALU.add),
                   reads=[psr[kr], r_R[b][c]], writes=[r_R[b][n]])
        TT = Rb[b][1]
        r_TT = r_R[b][1]
        sg.add("dve", lambda e, b=b, bb=bb, i=i: e.tensor_tensor(qg[bb].ap(0, H4), gq.ap(i * P, [(T, 4), (1, P)]), egcf[b].ap(0, H4), ALU.mult),
               reads=r_gq + [r_egc[b]], writes=[r_qg[bb]])
        k6 = nb()

        def trK(e, k6=k6, gkt=gkt):
            ins = None
            for h in range(4):
                ins = e.transpose(psb[k6].ap(h * 128, [(1, 128)]), gkt(h), IDB)
            return ins
        sg.add("pe", trK, reads=r_gk + [r_const], writes=[psr[k6]])
        sg.add("dve", lambda e, b=b, bb=bb, i=i, k6=k6: e.tensor_tensor(nkbg[b].ap(0, H4), psb[k6].ap(0, H4), colb(NBG0, i), ALU.mult),
               reads=[psr[k6], r_g], writes=[r_nkbg[b]])
        sg.add("dve", lambda e, b=b, bb=bb, i=i, k6=k6: e.tensor_tensor(kdec[bb].ap(0, H4), psb[k6].ap(0, H4), colb(KDS0, i), ALU.mult),
               reads=[psr[k6], r_g], writes=[r_kdec[bb]])
        sg.add("pool", lambda e, b=b, bb=bb, i=i: e.tensor_tensor(vb[b].ap(0, H4), vtok.ap(i * 512, H4), colb(BETA0, i), ALU.mult),
               reads=[r_vtok[i], r_g], writes=[r_vb[b]])
        k7, k8 = nb(), nb()
        sg.add("pe", lambda e, b=b, bb=bb, k7=k7, TT=TT: mm4(e, k7, hv(TT), hv(vb[b])), reads=[r_TT, r_vb[b]], writes=[psr[k7]])
        evac(k7, ub[bb], r_u[bb])
        sg.add("pe", lambda e, b=b, bb=bb, k8=k8, TT=TT: mm4(e, k8, hv(nkbg[b]), hv(TT)), reads=[r_TT, r_nkbg[b]], writes=[psr[k8]])
        evac(k8, nwT[bb], r_nwT[bb])

    def gdn_seq(i):
        b = 0
        bb = i % 2
        for ci in range(2):
            r0 = 64 * ci
            rows = lambda buf, h, r0=r0: buf.ap(h * 128, [(1, 128)], p0=r0, np_=64)
            kv = 2
            sg.add("pe", lambda e, b=b, bb=bb, kv=kv: mm4(e, kv, hv(nwT[bb]), hv(S16)), reads=[r_nwT[bb], r_S16], writes=[psr[kv]], cost=4)
            sg.add("dve", lambda e, b=b, bb=bb, r0=r0, kv=kv: e.tensor_tensor(vn16[b].ap(0, [(1, 512)], p0=r0, np_=64), ps[kv].ap(0, [(1, 512)], p0=r0, np_=64),
                                                                       ub[bb].ap(0, [(1, 512)], p0=r0, np_=64), ALU.add),
                   reads=[psr[kv], r_u[bb]], writes=[r_vn[b]])

            def omm(e, b=b, bb=bb, rows=rows):
                ins = None
                for h in range(4):
                    out = ps[3].ap(h * 128, [(1, 128)])
                    e.matmul(out, qg[bb].ap(h * 128, [(1, 128)]), S16.ap(h * 128, [(1, 128)]), start=True, stop=False)
                    ins = e.matmul(out, rows(AqkT[bb], h), rows(vn16[b], h), start=False, stop=True)
                return ins
            sg.add("pe", omm, reads=[r_qg[bb], r_S16, r_Aqk[bb], r_vn[b]], writes=[psr[3]], cost=8)
            sg.add("act", lambda e, b=b, bb=bb, r0=r0: e.copy(otok[b].ap(0, [(1, 512)], p0=r0, np_=64), ps[3].ap(0, [(1, 512)], p0=r0, np_=64)),
                   reads=[psr[3]], writes=[r_otok[b]])
            ks = 2
            sg.add("pe", lambda e, b=b, bb=bb, rows=rows, ks=ks: mm4(e, ks, lambda h: rows(kdec[bb], h), lambda h: rows(vn16[b], h)),
                   reads=[r_kdec[bb], r_vn[b]], writes=[psr[ks]], cost=4)

            def supd(e, i=i, ci=ci, ks=ks):
                ins = None
                for h in range(4):
                    ins = e.scalar_tensor_tensor(S32.ap(h * 128, [(1, 128)]), S32.ap(h * 128, [(1, 128)]),
                                                 small.ap(EGL0 + i * 8 + h * 2 + ci, [(1, 1)]), ps[ks].ap(h * 128, [(1, 128)]), ALU.mult, ALU.add)
                return ins
            sg.add("dve", supd, reads=[psr[ks], r_egl[i], r_S32], writes=[r_S32])
            sg.add("act", lambda e: e.copy(S16.ap(), S32.ap()), reads=[r_S32], writes=[r_S16])
        def sqs(e, b=b):
            ins = None
            for h in range(4):
                ins = e.activation(sqo.ap(h * 128, [(1, 128)]), otok[b].ap(h * 128, [(1, 128)]), AF.Square, accum_out=small.ap(SSN0 + h, [(1, 1)]))
            return ins
        sg.add("act", sqs, reads=[r_otok[b], r_ssn], writes=[r_sqo, r_ssn])
        sg.add("act", lambda e: e.activation(small.ap(SSN0, [(1, 4)]), small.ap(SSN0, [(1, 4)]), AF.Ln, bias=float(128 * EPS), scale=1.0), reads=[r_ssn], writes=[r_ssn])
        sg.add("act", lambda e: e.activation(small.ap(SSN0, [(1, 4)]), small.ap(SSN0, [(1, 4)]), AF.Exp, bias=float(0.5 * np.log(128.0)), scale=-0.5),
               reads=[r_ssn], writes=[r_ssn])
        sg.add("dve", lambda e, b=b, bb=bb: e.tensor_tensor(otok[b].ap(0, H4), otok[b].ap(0, H4), small.ap(SSN0, [(1, 4), (0, 128)]), ALU.mult),
               reads=[r_otok[b], r_ssn], writes=[r_otok[b]])
        sg.add("dve", lambda e, b=b, bb=bb, i=i: e.tensor_tensor(mixtok[b].ap(), otok[b].ap(), zgdn.ap(i * 512, [(1, 512)]), ALU.mult),
               reads=[r_otok[b], r_zgdn[i]], writes=[r_mixtok[b]])
        k9 = 2

        def trM(e, b=b, k9=k9):
            ins = None
            for h in range(4):
                ins = e.transpose(psb[k9].ap(h * 128, [(1, 128)]), mixtok[b].ap(h * 128, [(1, 128)]), IDB)
            return ins
        sg.add("pe", trM, reads=[r_mixtok[b], r_const], writes=[psr[k9]])
        sg.add("act", lambda e, i=i, k9=k9: e.copy(mixT.ap(i * P, [(T, 4), (1, P)]), psb[k9].ap(0, H4)), reads=[psr[k9]], writes=[r_mixT[i]])

    sa = Rec()
    PT = [cv.take(512, BF16) for _ in range(4)]
    r_PT = [Res(f"PT{j}") for j in range(4)]
    o12 = cv.take(4 * 258, F32)
    r_o12 = [Res(f"o12_{h}") for h in range(4)]
    mixtk = cv.take(512, BF16)
    r_mixtk = Res("mixtk")
    r_lam = Res("lam")
    r_rl = Res("rl")
    r_ss2 = Res("ss2")
    LAM0, NLAM, RL0, SS20, LTMP = 700, 704, 708, 720, 192
    O1 = [(258, 4), (1, 128)]
    sa.add("dve", lambda e: e.tensor_tensor(o12.ap(0, [(64, 2), (1, 64)]), prm.ap(322, [(128, 2), (1, 64)]), prm.ap(386, [(128, 2), (1, 64)]), ALU.mult),
           reads=[r_const], writes=r_o12)
    sa.add("dve", lambda e: e.tensor_reduce(small.ap(LAM0, [(1, 2)]), o12.ap(0, [(64, 2), (1, 64)]), AX.X, ALU.add), reads=r_o12, writes=[r_lam])
    sa.add("act", lambda e: e.activation(small.ap(LAM0, [(1, 2)]), small.ap(LAM0, [(1, 2)]), AF.Exp), reads=[r_lam], writes=[r_lam])
    sa.add("dve", lambda e: e.tensor_tensor(small.ap(NLAM, [(1, 1)]), small.ap(LAM0 + 1, [(1, 1)]), small.ap(LAM0, [(1, 1)]), ALU.subtract), reads=[r_lam], writes=[r_lam])
    sa.add("dve", lambda e: e.tensor_scalar(small.ap(NLAM, [(1, 1)]), small.ap(NLAM, [(1, 1)]), float(-LAMBDA_INIT), None, ALU.add), reads=[r_lam], writes=[r_lam])
    EB2 = float(0.5 * np.log(128.0) + np.log(1.0 - LAMBDA_INIT))
    OB = 7

    groups = []
    for qt in range(NT):
        for h in range(4):
            g0s = list(range(0, qt + 1, 4))
            for gi_, g0 in enumerate(g0s):
                groups.append(dict(qt=qt, h=h, kts=list(range(g0, min(g0 + 4, qt + 1))), last=(gi_ == len(g0s) - 1)))
    for k, g in enumerate(groups):
        g["sb"] = (4 + (2 * k) % 3, 4 + (2 * k + 1) % 3)
        g["pj"] = ((2 * k) % 4, (2 * k + 1) % 4)

    def emit_scores(g):
        kts, sbks, h, qt, pjs = g["kts"], g["sb"], g["h"], g["qt"], g["pj"]
        n = len(kts)

        def smm(e):
            ins = None
            for j, kt in enumerate(kts):
                for c in range(2):
                    out = ps[sbks[c]].ap(j * 128, [(1, 128)])
                    lh = dK.ap(h * T + kt * P, [(1, P)], p0=c * 64, np_=64)
                    rh = dQ.ap(h * T + qt * P, [(1, P)], p0=c * 64, np_=64)
                    ins = e.matmul(out, lh, rh, start=True, stop=(kt != qt))
                if kt == qt:
                    for c in range(2):
                        ins = e.matmul(ps[sbks[c]].ap(j * 128, [(1, 128)]), IDB, NEGM, start=False, stop=True)
            return ins
        sa.add("pe", smm, reads=[r_dK[h], r_dQ[h], r_const], writes=[psr[sbks[0]], psr[sbks[1]]], cost=2 * n + (1 if qt in kts else 0))
        for c in range(2):
            sa.add("act", lambda e, sbk=sbks[c], pj=pjs[c]: e.activation(PT[pj].ap(0, [(1, n * 128)]), ps[sbk].ap(0, [(1, n * 128)]), AF.Exp),
                   reads=[psr[sbks[c]]], writes=[r_PT[pjs[c]]])

    def emit_pv(g):
        kts, h, qt, pjs = g["kts"], g["h"], g["qt"], g["pj"]

        def pvm(e):
            ins = None
            for c in range(2):
                for j, kt in enumerate(kts):
                    ins = e.matmul(ps[OB].ap(c * 129, [(1, 129)]), PT[pjs[c]].ap(j * 128, [(1, 128)]), VA.ap(kt * 520 + h * 130, [(1, 129)]),
                                   start=(kt == 0 and c == 0), stop=(kt == qt), skip_group_check=True)
            return ins
        sa.add("pe", pvm, reads=[r_PT[pjs[0]], r_PT[pjs[1]]] + [r_VA[kt] for kt in kts], writes=[psr[OB]], cost=2 * len(kts))
        if g["last"]:
            sa.add("act", lambda e: e.copy(o12.ap(h * 258, [(1, 258)]), ps[OB].ap(0, [(1, 258)])), reads=[psr[OB]], writes=[r_o12[h]])
            if h == 3:
                emit_epilogue(qt)

    def emit_epilogue(qt):
        sa.add("dve", lambda e: e.reciprocal(small.ap(RL0, [(1, 8)]), o12.ap(128, [(129, 8), (1, 1)])), reads=r_o12, writes=[r_rl])
        sa.add("dve", lambda e: e.tensor_tensor(o12.ap(0, O1), o12.ap(0, O1), small.ap(RL0, [(2, 4), (0, 128)]), ALU.mult), reads=r_o12 + [r_rl], writes=r_o12)
        sa.add("dve", lambda e: e.tensor_tensor(o12.ap(129, O1), o12.ap(129, O1), small.ap(RL0 + 1, [(2, 4), (0, 128)]), ALU.mult), reads=r_o12 + [r_rl], writes=r_o12)
        sa.add("dve", lambda e: e.scalar_tensor_tensor(o12.ap(0, O1), o12.ap(129, O1), small.ap(NLAM, [(1, 1)]), o12.ap(0, O1), ALU.mult, ALU.add),
               reads=r_o12 + [r_lam], writes=r_o12)

        def sqs2(e):
            ins = None
            for h in range(4):
                ins = e.activation(o12.ap(h * 258 + 129, [(1, 128)]), o12.ap(h * 258, [(1, 128)]), AF.Square, accum_out=small.ap(SS20 + h, [(1, 1)]))
            return ins
        sa.add("act", sqs2, reads=r_o12 + [r_ss2], writes=r_o12 + [r_ss2])
        sa.add("act", lambda e: e.activation(small.ap(SS20, [(1, 4)]), small.ap(SS20, [(1, 4)]), AF.Ln, bias=float(128 * EPS), scale=1.0), reads=[r_ss2], writes=[r_ss2])
        sa.add("act", lambda e: e.activation(small.ap(SS20, [(1, 4)]), small.ap(SS20, [(1, 4)]), AF.Exp, bias=EB2, scale=-0.5), reads=[r_ss2], writes=[r_ss2])
        sa.add("dve", lambda e: e.tensor_tensor(o12.ap(0, O1), o12.ap(0, O1), small.ap(SS20, [(1, 4), (0, 128)]), ALU.mult), reads=r_o12 + [r_ss2], writes=r_o12)
        sa.add("dve", lambda e: e.tensor_tensor(mixtk.ap(0, H4), o12.ap(0, O1), zgdf.ap(qt * 512, H4), ALU.mult),
               reads=r_o12 + [r_zgdf[qt]], writes=[r_mixtk])
        tbk = groups[min(len(groups) - 1, 0)]["sb"][0]
        tbk = 4 + (qt % 3)

        def trD(e):
            ins = None
            for h in range(4):
                ins = e.transpose(psb[tbk].ap(h * 128, [(1, 128)]), mixtk.ap(h * 128, [(1, 128)]), IDB)
            return ins
        sa.add("pe", trD, reads=[r_mixtk, r_const], writes=[psr[tbk]])
        sa.add("act", lambda e: e.copy(mixT.ap(4 * T + qt * P, [(T, 4), (1, P)]), psb[tbk].ap(0, H4)), reads=[psr[tbk]], writes=[r_mixT[qt]])

    for k, g in enumerate(groups):
        emit_scores(g)
        if k >= 1:
            emit_pv(groups[k - 1])
    emit_pv(groups[-1])

    def merge2(a, b):
        out = Rec()
        ta = float(sum(it[0] for it in a.items)) or 1.0
        tb = float(sum(it[0] for it in b.items)) or 1.0
        ia = ib = 0
        ca = cb = 0.0
        while ia < len(a.items) or ib < len(b.items):
            if ib >= len(b.items) or (ia < len(a.items) and ca / ta <= cb / tb):
                it = a.items[ia]
                ia += 1
                ca += it[0]
            else:
                it = b.items[ib]
                ib += 1
                cb += it[0]
            out.items.append(it)
        return out

    pars, seqs = [], []
    for i in range(NT):
        _Proxy.cur = Rec()
        gdn_par(i)
        pars.append(_Proxy.cur)
        _Proxy.cur = Rec()
        gdn_seq(i)
        seqs.append(_Proxy.cur)
    sgall = Rec()
    sgall.items += g_prep.items + pars[0].items
    for i in range(NT):
        nxt = pars[i + 1] if i + 1 < NT else Rec()
        sgall.items += merge2(nxt, seqs[i]).items
    sg = sgall

    merge_streams(sg, sa)
    if debug and "odn" in debug:
        for c in range(4):
            for q in range(4):
                pg.add("dve", lambda e, c=c, q=q: e.tensor_copy(o12.ap(0, [(1, 512)]), mixT.ap(c * T + q * 512, [(1, 512)])), reads=r_mixT, writes=r_o12)
                dma(dbg_d["odn"][c * P:(c + 1) * P, q * 512:(q + 1) * 512], o12.ap(0, [(1, 512)]), "dbg", reads=r_o12, writes=[Res("dbgout")])
    if debug and "odf" in debug:
        for c in range(4):
            for q in range(4):
                pg.add("dve", lambda e, c=c, q=q: e.tensor_copy(o12.ap(0, [(1, 512)]), mixT.ap((4 + c) * T + q * 512, [(1, 512)])), reads=r_mixT, writes=r_o12)
                dma(dbg_d["odf"][c * P:(c + 1) * P, q * 512:(q + 1) * 512], o12.ap(0, [(1, 512)]), "dbg", reads=r_o12, writes=[Res("dbgout")])

    pg.mark("M")

    cv = cvG
    cv.reset()
    wob = cv.take(KD * 1024, BF16)
    wstC = [cv.take(KD * 256, F32) for _ in range(2)]
    r_wob_parts = {}
    xt2 = [cv.take(D, F32) for _ in range(2)]
    yo = [cv.take(D, F32) for _ in range(2)]
    r_wstC = [Res("cwst0"), Res("cwst1")]
    r_wob = [Res(f"wob{q}") for q in range(8)]
    r_xt2, r_yo = R2("xt2"), R2("yo")
    for q in range(4):
        st = q % 2
        dma(wstC[st].ap(0, [(256, KD), (1, 256)]), wout_d.rearrange("(k p) c -> p k c", p=P)[:, :, q * 256:(q + 1) * 256], f"wc{st}", writes=[r_wstC[st]] + r_gk[2 * st:2 * st + 2], dur=9.0)
        for kd in range(KD):
            ce = ("pool", "dve", "act", "dve")[kd % 4]
            src_ap = lambda st=st, kd=kd: wstC[st].ap(kd * 256, [(1, 256)])
            dst_ap = lambda q=q, kd=kd: wob.ap(kd * 1024 + q * 256, [(1, 256)])
            rw = Res(f"wobp{q}_{kd}")
            r_wob_parts.setdefault(q, []).append(rw)
            if ce == "act":
                pg.add("act", lambda e, src_ap=src_ap, dst_ap=dst_ap: e.copy(dst_ap(), src_ap()), reads=[r_wstC[st]], writes=[rw, r_gq[kd // 2]], dur=0.4)
            else:
                pg.add(ce, lambda e, src_ap=src_ap, dst_ap=dst_ap: e.tensor_copy(dst_ap(), src_ap()), reads=[r_wstC[st]], writes=[rw, r_gq[kd // 2]], dur=(0.9 if ce == "pool" else 0.4))
    for i in range(NT):
        s_ = i % 2
        dma(xt2[s_].ap(), x_d[i * P:(i + 1) * P, :], f"x{s_}", writes=[r_xt2[s_]] + r_vtok[4 * s_:4 * s_ + 4])
        for half in range(2):
            pb = (i * 2 + half) % 4
            pg.add("pe", lambda e, pb=pb, i=i, half=half: mm_acc(
                e, ps[pb].ap(), [(mixT.ap(c * T + i * P, [(1, P)]), wob.ap(c * 1024 + half * 512, [(1, 512)])) for c in range(8)]),
                reads=[r_mixT[i]] + r_wob_parts[2 * half] + r_wob_parts[2 * half + 1], writes=[psr[pb]], dur=2.15)
            pg.add("dve", lambda e, pb=pb, s_=s_, half=half: e.tensor_tensor(yo[s_].ap(half * 512, [(1, 512)]), ps[pb].ap(), xt2[s_].ap(half * 512, [(1, 512)]), ALU.add),
                   reads=[psr[pb], r_xt2[s_]], writes=[r_yo[s_]] + r_vtok[8 + 4 * s_:12 + 4 * s_])
        dma(y_d[i * P:(i + 1) * P, :], yo[s_].ap(), f"y{s_}", reads=[r_yo[s_]], writes=[Res("yout")])

    if debug and "hT" in debug:
        cv.reset()
        stg = cv.take(T, F32)
        r_stg = Res("stg")
        for kd in range(KD):
            pg.add("dve", lambda e, kd=kd: e.tensor_copy(stg.ap(), hT.ap(kd * T, [(1, T)])), reads=r_hT, writes=[r_stg])
            dma(dbg_d["hT"][kd * P:(kd + 1) * P, :], stg.ap(), "dbg", reads=[r_stg], writes=[Res("dbgout")])

    out_chans = [c for c in ("y0", "y1", "dbg")]
    import os as _os
    if _os.environ.get("KNOSCHED") != "1":
        pg.schedule(lat=float(_os.environ.get("KLAT", "0.25")))
    pg.plan()

    esem = {e: getsem(("e", e)) for e in ("pe", "act", "dve", "pool")}

    def semfor(key):
        if key[0] == "e":
            return esem[key[1]]
        return getsem(key)

    def run_engine(eng_name, eng):
        for op in pg.ops:
            if op.eng != eng_name:
                continue
            for key, v in op.waits:
                eng.wait_ge(semfor(key), v)
            if op.fn is None:
                continue
            ins = op.fn(eng)
            if op.chan is not None:
                ins.then_inc(getsem(("c", op.chan)), 16)
            elif op.signal:
                ins.then_inc(esem[op.eng], 1)
        if eng_name == "sp":
            for c in out_chans:
                if c in pg.chans:
                    eng.wait_ge(getsem(("c", c)), pg.chans[c])

    with nc.Block() as block:
        @block.sync
        def _(e):
            run_engine("sp", e)

        @block.tensor
        def _(e):
            run_engine("pe", e)

        @block.scalar
        def _(e):
            run_engine("act", e)

        @block.vector
        def _(e):
            run_engine("dve", e)

        @block.gpsimd
        def _(e):
            run_engine("pool", e)

    es.close()
    return nc


def make_consts():
    j = np.arange(128)
    same = (j[:, None] // 64) == (j[None, :] // 64)
    BT = (same & (j[:, None] <= j[None, :])).astype(np.float32)
    YS = (same & (j[:, None] > j[None, :])).astype(np.float32)
    ONES = np.ones((128, 128), np.float32)
    MS = (same & (j[None, :] < j[:, None])).astype(np.float32)
    MIT = (same & (j[:, None] <= j[None, :])).astype(np.float32)
    cF = np.concatenate([BT, YS, ONES, MS, MIT], axis=1)
    ident = np.eye(128, dtype=np.float32)
    negm = np.where(j[:, None] > j[None, :], -30000.0, 0.0).astype(np.float32)
    blk2 = same.astype(np.float32)
    cB = np.concatenate([ident, ONES, negm, blk2, YS], axis=1).astype(ml_dtypes.bfloat16)
    return cF, cB


def pack_params(inp):
    prm = np.zeros((P, NPRM), np.float32)
    prm[:, 0:8] = inp["norm_gain"][0].reshape(KD, P).T
    cw = inp["conv_w"][0]
    prm[:, 8:56] = cw.reshape(4, 12, P).transpose(2, 1, 0).reshape(P, 48)
    prm[:, 56:60] = inp["a_log"][0][None, :]
    prm[:, 60:64] = inp["dt_bias"][0][None, :]
    prm[:, 64:192] = inp["dn_out_gain"][0][None, :]
    prm[:, 192:320] = inp["df_out_gain"][0][None, :]
    prm[:, 320] = np.tile(inp["q_gain"][0], 2)
    prm[:, 321] = np.tile(inp["k_gain"][0], 2)
    prm[:, 322:386] = inp["lambda_q1"][0][None, :]
    prm[:, 386:450] = inp["lambda_k1"][0][None, :]
    prm[:, 450:514] = inp["lambda_q2"][0][None, :]
    prm[:, 514:578] = inp["lambda_k2"][0][None, :]
    return prm


def kernel(**inputs):
    inp = {k: np.asarray(v) for k, v in inputs.items()}
    n = 8
    nc = build_nc(DEBUG)
    cF, cB = make_consts()
    prm = pack_params(inp)
    w_in = np.ascontiguousarray(inp["w_in"][0], dtype=np.float32)
    w_out = np.ascontiguousarray(inp["w_out"][0], dtype=np.float32)
    in_maps = []
    for c in range(n):
        in_maps.append({"x": np.ascontiguousarray(inp["x"][c], dtype=np.float32), "w_in": w_in, "w_out": w_out,
                        "prm": prm, "cF": cF, "cB": cB})
    res = run_bass_kernel_spmd(nc, in_maps, core_ids=list(range(n)))
    kernel.last = res
    return np.stack([r["y"] for r in res.results], axis=0).astype(np.float32)
```

```python
import numpy as np
import ml_dtypes
from contextlib import ExitStack
import concourse.bass as bass
import concourse.mybir as mybir
from concourse.bass_utils import run_bass_kernel_spmd

F32 = mybir.dt.float32
BF16 = mybir.dt.bfloat16
AF = mybir.ActivationFunctionType
ALU = mybir.AluOpType
AX = mybir.AxisListType

P = 128
T = 2048
D = 1024
NT = 16
KD = 8
EPS = 1e-6
IN_COLS = 4104
LAMBDA_INIT = 0.8 - 0.6
NPRM = 578

DEBUG = None


class Res:
    __slots__ = ("name", "w", "rs")

    def __init__(self, name):
        self.name = name
        self.w = None
        self.rs = []


class Op:
    __slots__ = ("eng", "fn", "deps", "idx", "signal", "cnt", "chan", "waits", "dur")


class Prog:
    DEF_DUR = {"pe": 0.5, "act": 0.6, "dve": 0.7, "pool": 1.5, "sp": 3.0}

    def __init__(self):
        self.ops = []
        self.chans = {}
        self.cuts = []

    def cut(self):
        self.cuts.append(len(self.ops))

    def schedule(self, lat=0.5):
        bounds = [0] + [c for c in self.cuts if 0 < c < len(self.ops)] + [len(self.ops)]
        new = []
        for lo, hi in zip(bounds[:-1], bounds[1:]):
            if hi > lo:
                new += self._sched(self.ops[lo:hi], lat)
        self.ops = new
        for k, op in enumerate(self.ops):
            op.idx = k

    @staticmethod
    def _sched(ops, lat):
        n = len(ops)
        pos = {id(op): k for k, op in enumerate(ops)}
        preds = [[pos[id(d)] for d in op.deps if id(d) in pos] for op in ops]
        succs = [[] for _ in range(n)]
        for k, pl in enumerate(preds):
            for p in pl:
                succs[p].append(k)
        dur = [op.dur for op in ops]
        busy = [0.75 * op.dur if op.chan is not None else op.dur for op in ops]
        prio = [0.0] * n
        for k in range(n - 1, -1, -1):
            m = 0.0
            for s_ in succs[k]:
                if prio[s_] > m:
                    m = prio[s_]
            prio[k] = dur[k] + m
        indeg = [len(pl) for pl in preds]
        rtime = [0.0] * n
        ready = {}
        for k in range(n):
            if indeg[k] == 0:
                ready.setdefault(ops[k].eng, []).append(k)
        free = {}
        order = []
        while len(order) < n:
            best = None
            for e, lst in ready.items():
                if not lst:
                    continue
                t_e = free.get(e, 0.0)
                cand = None
                for k in lst:
                    st = rtime[k] if rtime[k] > t_e else t_e
                    key = (st, -prio[k], k)
                    if cand is None or key < cand[0]:
                        cand = (key, k)
                if best is None or cand[0] < best[0]:
                    best = (cand[0], cand[1], e)
            (st, _, _), k, e = best
            ready[e].remove(k)
            free[e] = st + busy[k]
            fin = st + dur[k]
            order.append(k)
            for s_ in succs[k]:
                if rtime[s_] < fin + lat:
                    rtime[s_] = fin + lat
                indeg[s_] -= 1
                if indeg[s_] == 0:
                    ready.setdefault(ops[s_].eng, []).append(s_)
        return [ops[k] for k in order]

    def mark(self, name):
        import os
        if os.environ.get("KSTOP") == name:
            self.frozen = True

    def add(self, eng, fn, reads=(), writes=(), chan=None, dur=None):
        if getattr(self, "frozen", False):
            return None
        op = Op()
        op.dur = self.DEF_DUR[eng] if dur is None else dur
        op.eng = eng
        op.fn = fn
        op.chan = chan
        op.idx = len(self.ops)
        op.signal = False
        op.cnt = 0
        deps = {}
        for r in reads:
            if r.w is not None:
                deps[r.w.idx] = r.w
        for r in writes:
            if r.w is not None:
                deps[r.w.idx] = r.w
            for rd in r.rs:
                deps[rd.idx] = rd
        op.deps = list(deps.values())
        for r in reads:
            r.rs.append(op)
        for r in writes:
            r.w = op
            r.rs = []
        self.ops.append(op)
        return op

    def barrier(self, mk, pe_extra=(), rd=()):
        rs = {e: Res("bar_" + e) for e in ("pe", "act", "dve", "pool")}
        for e in ("pe", "act", "dve", "pool"):
            self.add(e, mk(e), reads=list(rd), writes=[rs[e]] + (list(pe_extra) if e == "pe" else []))
        allr = list(rs.values())
        for e in ("pe", "act", "dve", "pool"):
            self.add(e, mk(e), reads=allr + list(rd), writes=[Res("bar2_" + e)] + (list(pe_extra) if e == "pe" else []))
        self.add("sp", None, reads=allr)

    def plan(self):
        for op in self.ops:
            for d in op.deps:
                if d.chan is not None:
                    continue
                if d.eng == "pe" and op.eng == "pe":
                    continue
                d.signal = True
        cnt = {}
        for op in self.ops:
            if op.chan is not None:
                c = self.chans.get(op.chan, 0) + 16
                self.chans[op.chan] = c
                op.cnt = c
            elif op.signal:
                c = cnt.get(op.eng, 0) + 1
                cnt[op.eng] = c
                op.cnt = c
        seen = {}
        for op in self.ops:
            need = {}
            for d in op.deps:
                if d.chan is not None:
                    key = ("c", d.chan)
                elif d.eng == "pe" and op.eng == "pe":
                    continue
                else:
                    key = ("e", d.eng)
                if need.get(key, 0) < d.cnt:
                    need[key] = d.cnt
            s = seen.setdefault(op.eng, {})
            w = []
            for key, v in need.items():
                if s.get(key, 0) < v:
                    s[key] = v
                    w.append((key, v))
            op.waits = w


class Buf:
    def __init__(self, t, F, dtype):
        self.t = t
        self.F = F
        self.dtype = dtype

    def ap(self, off=0, dims=None, p0=0, np_=P):
        if dims is None:
            dims = [(1, self.F - off)]
        return bass.AP(self.t, p0 * self.F + off, [[self.F, np_]] + [[s, c] for s, c in dims])


def build_nc(debug=None):
    nc = bass.Bass("TRN2", target_bir_lowering=False)
    x_d = nc.dram_tensor("x", [T, D], F32, kind="ExternalInput").ap()
    win_d = nc.dram_tensor("w_in", [D, IN_COLS], F32, kind="ExternalInput").ap()
    wout_d = nc.dram_tensor("w_out", [D, D], F32, kind="ExternalInput").ap()
    prm_d = nc.dram_tensor("prm", [P, NPRM], F32, kind="ExternalInput").ap()
    cF_d = nc.dram_tensor("cF", [P, 5 * 128], F32, kind="ExternalInput").ap()
    cB_d = nc.dram_tensor("cB", [P, 4 * 128], BF16, kind="ExternalInput").ap()
    y_d = nc.dram_tensor("y", [T, D], F32, kind="ExternalOutput").ap()
    dbg_d = {}
    if debug:
        for k, shp in debug.items():
            dbg_d[k] = nc.dram_tensor("dbg_" + k, list(shp), F32, kind="ExternalOutput").ap()

    pg = Prog()
    es = ExitStack()

    def sb(name, F, dtype):
        t = es.enter_context(nc.sbuf_tensor("sb_" + name, [P, F], dtype))
        return Buf(t, F, dtype)

    cF = sb("cF", 5 * 128, F32)
    cB = sb("cB", 4 * 128, BF16)
    prm = sb("prm", NPRM, F32)
    hT = sb("hT", KD * T, BF16)
    mixT = hT
    small = sb("small", 768, F32)
    REG_BYTES = 167 * 1024
    reg = sb("reg", REG_BYTES // 2, BF16)
    regf = Buf(reg.t.bitcast(F32), REG_BYTES // 4, F32)

    class Carver:
        def __init__(self, lo, hi):
            self.lo = lo
            self.hi = hi
            self.off = lo

        def reset(self):
            self.off = self.lo

        def take(self, nelem, dtype):
            bpe = 4 if dtype == F32 else 2
            self.off = (self.off + 3) // 4 * 4
            o = self.off
            self.off += nelem * bpe
            assert self.off <= self.hi, ("region overflow", self.off, self.hi)
            base = regf if dtype == F32 else reg
            return View(base, o // bpe, nelem)

    class View:
        def __init__(self, base, off, n):
            self.base = base
            self.off = off
            self.F = n
            self.dtype = base.dtype

        def ap(self, off=0, dims=None, p0=0, np_=P):
            if dims is None:
                dims = [(1, self.F - off)]
            return self.base.ap(self.off + off, dims, p0, np_)

    KB = 1024
    cvG = Carver(0, 64 * KB)
    cvD = Carver(64 * KB, 64 * KB + 65792)
    cvT = Carver(64 * KB + 65792, REG_BYTES)
    R2 = lambda nm: [Res(nm + "0"), Res(nm + "1")]
    H4 = [(128, 4), (1, 128)]

    class Rec:
        def __init__(self):
            self.items = []

        def add(self, eng, fn, reads=(), writes=(), chan=None, cost=0, dur=None):
            if eng == "pe" and cost == 0:
                cost = 4
            if dur is None and eng == "pe":
                dur = 0.06 + 0.115 * cost
            self.items.append((cost, eng, fn, list(reads), list(writes), chan, dur))

    def merge_streams(a, b):
        import os
        if os.environ.get("KOPS"):
            a.items = a.items[:int(os.environ["KOPS"])]
            print("KOPS", len(a.items), [ (i, it[1], it[2].__code__.co_firstlineno) for i, it in enumerate(a.items)][-3:])
        if os.environ.get("KSEQ") in ("1", "2", "3"):
            its = {"1": a.items + b.items, "2": a.items, "3": b.items}[os.environ.get("KSEQ")]
            for it in its:
                pg.add(it[1], it[2], reads=it[3], writes=it[4], chan=it[5], dur=it[6])
            return
        ta = float(sum(it[0] for it in a.items)) or 1.0
        tb = float(sum(it[0] for it in b.items)) or 1.0
        ia = ib = 0
        ca = cb = 0.0
        while ia < len(a.items) or ib < len(b.items):
            if ib >= len(b.items) or (ia < len(a.items) and ca / ta <= cb / tb):
                it = a.items[ia]
                ia += 1
                ca += it[0]
            else:
                it = b.items[ib]
                ib += 1
                cb += it[0]
            pg.add(it[1], it[2], reads=it[3], writes=it[4], chan=it[5], dur=it[6])

    ps = []
    psb = []
    for i in range(8):
        t = es.enter_context(nc.psum_tensor(f"ps{i}", [P, 512], F32))
        ps.append(Buf(t, 512, F32))
        psb.append(Buf(t.bitcast(BF16), 1024, BF16))
    psr = [Res(f"ps{i}") for i in range(8)]

    sems = {}

    def getsem(key):
        if key not in sems:
            sems[key] = es.enter_context(nc.semaphore("s_" + "_".join(str(k) for k in key)))
        return sems[key]

    BT = cF.ap(0, [(1, 128)])
    YS = cF.ap(128, [(1, 128)])
    ONESF = cF.ap(256, [(1, 128)])
    IDB = cB.ap(0, [(1, 128)])
    ONESB = cB.ap(128, [(1, 128)])
    NEGM = cB.ap(256, [(1, 128)])
    BLK2 = cB.ap(384, [(1, 128)])
    r_const = Res("const")

    def dma(out, in_, chan, reads=(), writes=(), eng="sp", dur=4.5):
        pg.add(eng, lambda e: e.dma_start(out=out, in_=in_), reads=reads, writes=writes, chan=chan, dur=dur)

    dma(cF.ap(), cF_d, "prm", writes=[r_const])
    dma(cB.ap(), cB_d, "prm", writes=[r_const])
    dma(prm.ap(), prm_d, "prm", writes=[r_const])

    def mk_bar(e):
        if e == "pe":
            return lambda eng: eng.matmul(ps[7].ap(0, [(1, 8)], 0, 8), cB.ap(0, [(1, 8)], 0, 8), cB.ap(0, [(1, 8)], 0, 8), start=True, stop=True)
        col = {"act": 740, "dve": 744, "pool": 748}[e]
        if e == "act":
            return lambda eng: eng.copy(small.ap(col, [(1, 2)]), prm.ap(0, [(1, 2)]))
        return lambda eng: eng.tensor_copy(small.ap(col, [(1, 2)]), prm.ap(0, [(1, 2)]))

    def barrier():
        pg.cut()
        pg.barrier(mk_bar, pe_extra=[psr[7]], rd=[r_const])
        pg.cut()

    cv = cvT
    cv.reset()
    xt = [cv.take(D, F32) for _ in range(4)]
    xs = [cv.take(D, BF16) for _ in range(2)]
    junk = cv.take(D, BF16)
    r_xt = [Res(f"xt{j}") for j in range(4)]
    r_xs = [Res("xs0"), Res("xs1")]
    r_ss = [Res(f"ss{i}") for i in range(NT)]
    r_junk = Res("junk")
    r_hT = [Res(f"hT{i}") for i in range(NT)]
    SS0 = 0
    RS0 = 16
    for i in range(NT):
        s = i % 4
        s2 = i % 2
        dma(xt[s].ap(), x_d[i * P:(i + 1) * P, :], f"x{s}", writes=[r_xt[s]])
        pg.add("act", lambda e, s=s, i=i: e.activation(junk.ap(), xt[s].ap(), AF.Square, accum_out=small.ap(SS0 + i, [(1, 1)])),
               reads=[r_xt[s]], writes=[r_ss[i], r_junk], dur=1.1)
        pg.add("act", lambda e, i=i: e.activation(small.ap(RS0 + i, [(1, 1)]), small.ap(SS0 + i, [(1, 1)]), AF.Ln, bias=float(D * EPS), scale=1.0),
               reads=[r_ss[i]], writes=[r_ss[i]], dur=0.25)
        pg.add("act", lambda e, i=i: e.activation(small.ap(RS0 + i, [(1, 1)]), small.ap(RS0 + i, [(1, 1)]), AF.Exp, scale=-0.5),
               reads=[r_ss[i]], writes=[r_ss[i]], dur=0.25)
        pg.add("dve", lambda e, s=s, s2=s2, i=i: e.tensor_scalar(xs[s2].ap(), xt[s].ap(), small.ap(RS0 + i, [(1, 1)]), float(np.sqrt(D)), ALU.mult, ALU.mult),
               reads=[r_xt[s], r_ss[i], r_const], writes=[r_xs[s2]])
        b = i % 2

        def tr(e, s=s2, b=b):
            ins = None
            for kd in range(KD):
                ins = e.transpose(psb[b].ap(kd * 128, [(1, 128)]), xs[s].ap(kd * 128, [(1, 128)]), IDB)
            return ins
        pg.add("pe", tr, reads=[r_xs[s2], r_const], writes=[psr[b]], dur=1.0)
        pg.add("act", lambda e, i=i, b=b: e.copy(hT.ap(i * P, [(T, KD), (1, P)]), psb[b].ap(0, [(128, KD), (1, 128)])),
               reads=[psr[b]], writes=[r_hT[i]], dur=1.05)

    pg.mark("P1")


    def mm_acc(e, out, pairs):
        ins = None
        n = len(pairs)
        for idx, (l, r) in enumerate(pairs):
            ins = e.matmul(out, l, r, start=(idx == 0), stop=(idx == n - 1))
        return ins

    GAINB = prm.ap(0, [(1, KD), (0, 128)])

    cvG.reset()
    gq = cvG.take(4 * T, BF16)
    gk = cvG.take(4 * T, BF16)
    vtok = cvG.take(16 * 512, BF16)
    zgdn = cvG.take(16 * 512, BF16)
    cv = cvD
    cv.reset()
    wst = [cv.take(KD * 256, F32) for _ in range(2)]
    wb = [cv.take(KD * 512, BF16) for _ in range(2)]
    r_wst = [Res("wst0"), Res("wst1")]
    r_wbq = [[Res(f"wb{s}_{q}") for q in range(4)] for s in range(2)]
    wsm = cv.take(KD * 8, BF16)
    wsmf = cv.take(KD * 8, F32)
    raw = [cv.take(4 + T, BF16) for _ in range(2)]
    dgw = [cv.take(4 * 128, BF16) for _ in range(2)]
    accb = [cv.take(512, F32) for _ in range(3)]
    r_accb = [Res(f"accb{j}") for j in range(3)]
    acc_rr = [0]
    r_dgw = [Res("dgw0"), Res("dgw1")]
    sqb = cv.take(T, BF16)
    r_raw = [[Res(f"raw{b}_{tb}") for tb in range(4)] for b in range(2)]
    r_stgf = Res("stgf")
    r_gq = [Res(f"gq{h}") for h in range(4)]
    r_gk = [Res(f"gk{h}") for h in range(4)]
    r_vtok = [Res(f"vtok{i}") for i in range(NT)]
    r_zgdn = [Res(f"zgdn{i}") for i in range(NT)]
    r_sqb = Res("sqb")
    r_lnv = [Res("lnv0"), Res("lnv1")]
    r_ba = Res("ba")
    BA0 = 64

    wq_count = [0]

    def load_wgroup(slot, c0, bufs, ncols=512, src=None, rows_src=None, pw=128, extra_w=(), cp="w"):
        wst, wb, r_wst, r_wbq = bufs
        src = win_d if src is None else src
        nq = pw // 128
        for q0 in range(0, ncols // 128, nq):
            st = wq_count[0] % 2
            wq_count[0] += 1
            dma(wst[st].ap(0, [(pw, KD), (1, pw)]),
                src.rearrange("(k p) c -> p k c", p=P)[:, :, c0 + q0 * 128:c0 + q0 * 128 + pw],
                f"{cp}{st}", writes=[r_wst[st]] + list(extra_w), dur=4.5 * nq)
            pg.add("pool", lambda e, st=st, slot=slot, q0=q0, wb=wb, wst=wst: e.tensor_tensor(
                wb[slot].ap(q0 * 128, [(512, KD), (1, pw)]), wst[st].ap(0, [(pw, KD), (1, pw)]), prm.ap(0, [(1, KD), (0, pw)]), ALU.mult),
                reads=[r_wst[st], r_const], writes=[r_wbq[slot][q0 + j] for j in range(nq)] + list(extra_w), dur=3.6 * nq)

    for b in range(2):
        pg.add("dve", lambda e, b=b: e.memset(raw[b].ap(0, [(1, 3)]), 0.0), writes=[r_raw[b][0]])

    r_wsm = Res("wsm")
    dma(wsmf.ap(0, [(8, KD), (1, 8)]), win_d.rearrange("(k p) c -> p k c", p=P)[:, :, 2048:2056], "wsm", writes=[r_wsm])
    pg.add("dve", lambda e: e.tensor_tensor(wsm.ap(0, [(8, KD), (1, 8)]), wsmf.ap(0, [(8, KD), (1, 8)]), prm.ap(0, [(1, KD), (0, 8)]), ALU.mult),
           reads=[r_wsm, r_const], writes=[r_wsm])

    def ba_mm(e):
        ins = None
        for i in range(NT):
            ins = mm_acc(e, ps[6].ap(i * 8, [(1, 8)]),
                         [(hT.ap(kd * T + i * P, [(1, P)]), wsm.ap(kd * 8, [(1, 8)])) for kd in range(KD)])
        return ins
    pg.add("pe", ba_mm, reads=[r_wsm] + r_hT, writes=[psr[6]])
    pg.add("act", lambda e: e.copy(small.ap(BA0, [(1, 128)]), ps[6].ap(0, [(1, 128)])), reads=[psr[6]], writes=[r_ba])

    groups = [(0, "q"), (512, "k"), (1024, "v"), (1536, "z")]
    bufsA = (wst, wb, r_wst, r_wbq)
    load_wgroup(0, groups[0][0], bufsA, pw=256)
    pcount = [0]
    chunk_count = [0]
    for gi, (c0, kind) in enumerate(groups):
        slot = gi % 2
        if gi + 1 < len(groups):
            load_wgroup((gi + 1) % 2, groups[gi + 1][0], bufsA, pw=256)
        if kind in ("q", "k", "v"):
            for h in range(4):
                ch = {"q": 0, "k": 4, "v": 8}[kind] + h
                rb = chunk_count[0] % 2
                chunk_count[0] += 1
                for tb in range(4):
                    pb = 2 + pcount[0] % 2
                    pcount[0] += 1
                    pg.add("pe", lambda e, pb=pb, slot=slot, h=h, tb=tb: mm_acc(
                        e, ps[pb].ap(), [(wb[slot].ap(kd * 512 + h * 128, [(1, 128)]), hT.ap(kd * T + tb * 512, [(1, 512)])) for kd in range(KD)]),
                        reads=[r_wbq[slot][h]] + r_hT[tb * 4:(tb + 1) * 4], writes=[psr[pb]], dur=2.15)
                    pg.add("act", lambda e, pb=pb, rb=rb, tb=tb: e.copy(raw[rb].ap(3 + tb * 512, [(1, 512)]), ps[pb].ap()),
                           reads=[psr[pb]], writes=[r_raw[rb][tb]])
                db = chunk_count[0] % 2
                pg.add("dve", lambda e, db=db, ch=ch: [e.tensor_scalar(dgw[db].ap(j * 128, [(1, 128)]), IDB, prm.ap(8 + ch * 4 + j, [(1, 1)]), None, ALU.mult) for j in (2, 3)][-1],
                       reads=[r_const], writes=[r_dgw[db]], dur=0.4)
                for tb in range(4):
                    pb = 4 + tb % 2
                    ab = acc_rr[0] % 3
                    acc_rr[0] += 1
                    pg.add("pe", lambda e, pb=pb, rb=rb, tb=tb, db=db: mm_acc(
                        e, ps[pb].ap(), [(dgw[db].ap(j * 128, [(1, 128)]), raw[rb].ap(tb * 512 + j, [(1, 512)])) for j in (2, 3)]),
                        reads=r_raw[rb] + [r_dgw[db]], writes=[psr[pb]], dur=0.6)
                    pg.add("dve", lambda e, pb=pb, rb=rb, tb=tb, ab=ab, ch=ch: e.scalar_tensor_tensor(
                        accb[ab].ap(), raw[rb].ap(tb * 512 + 1, [(1, 512)]), prm.ap(8 + ch * 4 + 1, [(1, 1)]), ps[pb].ap(), ALU.mult, ALU.add),
                        reads=r_raw[rb] + [psr[pb], r_const], writes=[r_accb[ab]], dur=0.65)
                    pg.add("dve", lambda e, rb=rb, tb=tb, ab=ab, ch=ch: e.scalar_tensor_tensor(
                        accb[ab].ap(), raw[rb].ap(tb * 512, [(1, 512)]), prm.ap(8 + ch * 4, [(1, 1)]), accb[ab].ap(), ALU.mult, ALU.add),
                        reads=r_raw[rb] + [r_accb[ab], r_const], writes=[r_accb[ab]], dur=0.65)
                    dst = {"q": gq, "k": gk}.get(kind)
                    if dst is not None:
                        rr = (r_gq if kind == "q" else r_gk)[h]
                        pg.add("act", lambda e, ab=ab, dst=dst, h=h, tb=tb: e.activation(dst.ap(h * T + tb * 512, [(1, 512)]), accb[ab].ap(), AF.Silu),
                               reads=[r_accb[ab]], writes=[rr], dur=0.55)
                    else:
                        pg.add("act", lambda e, ab=ab, tb=tb: e.activation(sqb.ap(tb * 512, [(1, 512)]), accb[ab].ap(), AF.Silu),
                               reads=[r_accb[ab]], writes=[r_sqb], dur=0.55)
                if kind == "v":
                    for g4 in range(4):
                        pb = g4 % 2

                        def trv(e, g4=g4, pb=pb):
                            ins = None
                            for q in range(4):
                                ins = e.transpose(psb[pb].ap(q * 128, [(1, 128)]), sqb.ap((g4 * 4 + q) * 128, [(1, 128)]), IDB)
                            return ins
                        pg.add("pe", trv, reads=[r_sqb, r_const], writes=[psr[pb]])
                        pg.add("dve", lambda e, g4=g4, pb=pb, h=h: e.tensor_copy(
                            vtok.ap((g4 * 4 * 4 + h) * 128, [(512, 4), (1, 128)]), psb[pb].ap(0, [(128, 4), (1, 128)])),
                            reads=[psr[pb]], writes=r_vtok[g4 * 4:(g4 + 1) * 4])
        else:
            for i in range(NT):
                pb = 6 + i % 2
                pg.add("pe", lambda e, pb=pb, slot=slot, i=i: mm_acc(
                    e, ps[pb].ap(), [(hT.ap(kd * T + i * P, [(1, P)]), wb[slot].ap(kd * 512, [(1, 512)])) for kd in range(KD)]),
                    reads=r_wbq[slot] + [r_hT[i]], writes=[psr[pb]], dur=2.15)
                pg.add("act", lambda e, pb=pb, i=i: e.activation(zgdn.ap(i * 512, [(1, 512)]), ps[pb].ap(), AF.Silu),
                       reads=[psr[pb]], writes=[r_zgdn[i]])
                pg.add("pool", lambda e, i=i: e.tensor_tensor(zgdn.ap(i * 512, [(128, 4), (1, 128)]), zgdn.ap(i * 512, [(128, 4), (1, 128)]),
                                                              prm.ap(64, [(0, 4), (1, 128)]), ALU.mult),
                       reads=[r_zgdn[i], r_const], writes=[r_zgdn[i]])

    cvD.reset()
    dQ = cvD.take(4 * T, BF16)
    dK = cvD.take(4 * T, BF16)
    VA = cvD.take(NT * 520, BF16)
    zgdf = cvD.take(NT * 512, BF16)
    cv = cvT
    cv.reset()
    wstB = [cv.take(KD * 128, F32) for _ in range(2)]
    wbB = [cv.take(KD * 512, BF16) for _ in range(2)]
    ND = 3
    sqd = [cv.take(512, BF16) for _ in range(ND)]
    yr = [cv.take(512, BF16) for _ in range(ND)]
    lnvB = [cv.take(512, F32) for _ in range(ND)]
    r_wstB = [Res("bwst0"), Res("bwst1")]
    r_wbqB = [[Res(f"bwb{s_}_{q}") for q in range(4)] for s_ in range(2)]
    r_sqd = [Res(f"sqd{j}") for j in range(ND)]
    r_yr = [Res(f"yr{j}") for j in range(ND)]
    r_lnvB = [Res(f"blnv{j}") for j in range(ND)]
    r_dQ = [Res(f"dQ{h}") for h in range(4)]
    r_dK = [Res(f"dK{h}") for h in range(4)]
    r_VA = [Res(f"VA{i}") for i in range(NT)]
    r_zgdf = [Res(f"zgdf{i}") for i in range(NT)]
    LNQ = float(-0.5 * np.log(128.0))
    PBK = (1, 2, 3)
    OBK = (4, 5, 0)

    def norm_tail(u, ob, kind_bias, fin):
        pass

    def l2_unit(kind, h, tb):
        u = cnt2[0] % ND
        cnt2[0] += 1
        ob = OBK[u]
        buf = gq if kind == "q" else gk
        rr = (r_gq if kind == "q" else r_gk)[h]
        sl = lambda: buf.ap(h * T + tb * 512, [(1, 512)])
        pg.add("act", lambda e: e.activation(sqd[u].ap(), sl(), AF.Square), reads=[rr], writes=[r_sqd[u]], dur=0.55)
        pg.add("pe", lambda e: e.matmul(ps[ob].ap(), ONESB, sqd[u].ap(), start=True, stop=True), reads=[r_sqd[u], r_const], writes=[psr[ob]], dur=0.3)
        pg.add("act", lambda e: e.activation(lnvB[u].ap(), ps[ob].ap(), AF.Ln, bias=float(EPS), scale=1.0), reads=[psr[ob]], writes=[r_lnvB[u]], dur=0.55)
        pg.add("act", lambda e: e.activation(lnvB[u].ap(), lnvB[u].ap(), AF.Exp, bias=(LNQ if kind == "q" else 0.0), scale=-0.5),
               reads=[r_lnvB[u]], writes=[r_lnvB[u]], dur=0.55)
        pg.add("dve", lambda e: e.tensor_tensor(sl(), sl(), lnvB[u].ap(), ALU.mult), reads=[r_lnvB[u], rr], writes=[rr], dur=0.65)

    groupsB = [(2056, "Q"), (2568, "K"), (3080, "V"), (3592, "Z")]
    bufsB = (wstB, wbB, r_wstB, r_wbqB)
    load_wgroup(0, groupsB[0][0], bufsB, extra_w=r_xt + r_xs + [r_junk], cp="wb")
    barrier()
    pg.mark("A1")
    pg.add("pool", lambda e: e.memset(VA.ap(128, [(130, 64), (1, 1)]), 1.0), writes=r_VA)
    cnt2 = [0]
    for gi, (c0, kind) in enumerate(groupsB):
        slot = gi % 2
        if gi + 1 < len(groupsB):
            load_wgroup((gi + 1) % 2, groupsB[gi + 1][0], bufsB, cp="wb")
        if kind in ("Q", "K"):
            dst = dQ if kind == "Q" else dK
            rdst = r_dQ if kind == "Q" else r_dK
            gcol = 320 if kind == "Q" else 321
            ebias = 0.0 if kind == "Q" else float(np.log(8.0))
            for h in range(4):
                for tb in range(4):
                    l2_unit("q" if kind == "Q" else "k", h, tb)
                    u = cnt2[0] % ND
                    cnt2[0] += 1
                    pb, ob = PBK[u], OBK[u]
                    pg.add("pe", lambda e, pb=pb, slot=slot, h=h, tb=tb: mm_acc(
                        e, ps[pb].ap(), [(wbB[slot].ap(kd * 512 + h * 128, [(1, 128)]), hT.ap(kd * T + tb * 512, [(1, 512)])) for kd in range(KD)]),
                        reads=[r_wbqB[slot][h]] + r_hT[tb * 4:(tb + 1) * 4], writes=[psr[pb]], dur=2.15)
                    pg.add("act", lambda e, pb=pb, u=u: e.activation(sqd[u].ap(), ps[pb].ap(), AF.Square), reads=[psr[pb]], writes=[r_sqd[u]], dur=0.55)
                    pg.add("dve", lambda e, pb=pb, u=u: e.tensor_copy(yr[u].ap(), ps[pb].ap()), writes=[psr[pb], r_yr[u]], dur=0.65)
                    pg.add("pe", lambda e, ob=ob, u=u: e.matmul(ps[ob].ap(), BLK2, sqd[u].ap(), start=True, stop=True),
                           reads=[r_sqd[u], r_const], writes=[psr[ob]], dur=0.3)
                    pg.add("act", lambda e, ob=ob, u=u: e.activation(lnvB[u].ap(), ps[ob].ap(), AF.Ln, bias=float(64 * EPS), scale=1.0),
                           reads=[psr[ob]], writes=[r_lnvB[u]], dur=0.55)
                    pg.add("act", lambda e, u=u, ebias=ebias: e.activation(lnvB[u].ap(), lnvB[u].ap(), AF.Exp, bias=ebias, scale=-0.5),
                           reads=[r_lnvB[u]], writes=[r_lnvB[u]], dur=0.55)
                    pg.add("dve", lambda e, u=u, dst=dst, h=h, tb=tb, gcol=gcol: e.scalar_tensor_tensor(
                        dst.ap(h * T + tb * 512, [(1, 512)]), yr[u].ap(), prm.ap(gcol, [(1, 1)]), lnvB[u].ap(), ALU.mult, ALU.mult),
                        reads=[r_yr[u], r_lnvB[u], r_const], writes=[rdst[h]], dur=0.65)
        else:
            if kind == "Z":
                pg.cut()
            for i in range(NT):
                pb = 6 + i % 2
                pg.add("pe", lambda e, pb=pb, slot=slot, i=i: mm_acc(
                    e, ps[pb].ap(), [(hT.ap(kd * T + i * P, [(1, P)]), wbB[slot].ap(kd * 512, [(1, 512)])) for kd in range(KD)]),
                    reads=r_wbqB[slot] + [r_hT[i]], writes=[psr[pb]], dur=2.15)
                if kind == "V":
                    pg.add("act", lambda e, pb=pb, i=i: e.copy(VA.ap(i * 520, [(130, 4), (1, 128)]), ps[pb].ap(0, H4)), reads=[psr[pb]], writes=[r_VA[i]])
                else:
                    pg.add("act", lambda e, pb=pb, i=i: e.activation(zgdf.ap(i * 512, [(1, 512)]), ps[pb].ap(), AF.Silu), reads=[psr[pb]], writes=[r_zgdf[i]])
                    pg.add("pool", lambda e, i=i: e.tensor_tensor(zgdf.ap(i * 512, H4), zgdf.ap(i * 512, H4), prm.ap(192, [(0, 4), (1, 128)]), ALU.mult),
                           reads=[r_zgdf[i], r_const], writes=[r_zgdf[i]])
    if debug and "dQ" in debug:
        for nm, buf, rl in (("dQ", dQ, r_dQ), ("dK", dK, r_dK)):
            for h in range(4):
                for q in range(4):
                    pg.add("dve", lambda e, buf=buf, h=h, q=q: e.tensor_copy(lnvB[0].ap(), buf.ap(h * T + q * 512, [(1, 512)])), reads=rl, writes=[r_lnvB[0]])
                    dma(dbg_d[nm][h * P:(h + 1) * P, q * 512:(q + 1) * 512], lnvB[0].ap(), "dbg", reads=[r_lnvB[0]], writes=[Res("dbgout")])
        for i in range(NT):
            pg.add("dve", lambda e, i=i: e.tensor_copy(lnvB[0].ap(), zgdf.ap(i * 512, [(1, 512)])), reads=r_zgdf, writes=[r_lnvB[0]])
            dma(dbg_d["zgdf"][i * P:(i + 1) * P, :], lnvB[0].ap(), "dbg", reads=[r_lnvB[0]], writes=[Res("dbgout")])
            pg.add("dve", lambda e, i=i: e.tensor_copy(lnvB[0].ap(0, [(130, 4), (1, 128)]), VA.ap(i * 520, [(130, 4), (1, 128)])), reads=r_VA, writes=[r_lnvB[0]])
            pg.add("dve", lambda e, i=i: e.tensor_copy(lnvB[0].ap(128, [(130, 4), (1, 2)]), VA.ap(i * 520 + 128, [(130, 4), (1, 2)])), reads=r_VA, writes=[r_lnvB[0]])
            dma(dbg_d["VA"][i * P:(i + 1) * P, :], lnvB[0].ap(), "dbg", reads=[r_lnvB[0]], writes=[Res("dbgout")])
    barrier()
    pg.mark("B1")


    cv = cvT
    cv.reset()
    class _Proxy:
        cur = None

        def add(self, *a, **k):
            _Proxy.cur.add(*a, **k)
    sg = _Proxy()
    g_prep = Rec()
    _Proxy.cur = g_prep
    f4 = lambda: cv.take(512, F32)
    b4 = lambda: cv.take(512, BF16)
    Xf = [f4()]
    Ef = [b4()]
    ETf = [b4()]
    egcf = [b4()]
    otok = [b4()]
    S32 = f4()
    S16 = b4()
    Pb = [[b4(), b4()]]
    Qb = [[b4(), b4()]]
    Rb = [[b4(), b4()]]
    AqkT = [b4(), b4()]
    qg = [b4(), b4()]
    nkbg = [b4()]
    kdec = [b4(), b4()]
    vb = [b4()]
    ub = [b4(), b4()]
    nwT = [b4(), b4()]
    vn16 = [b4()]
    mixtok = [b4()]
    sqo = mixtok[0]
    r_X, r_E, r_ET, r_egc, r_otok = R2("X"), R2("E"), R2("ET"), R2("egc"), R2("otok")
    r_P = [R2("P0_"), R2("P1_")]
    r_Q = [R2("Q0_"), R2("Q1_")]
    r_R = [R2("R0_"), R2("R1_")]
    r_Aqk, r_qg, r_nkbg, r_kdec, r_vb, r_u, r_nwT, r_vn, r_mixtok = (R2("Aqk"), R2("qg"), R2("nkbg"), R2("kdec"), R2("vb"),
                                                                    R2("u"), R2("nwT"), R2("vn"), R2("mixtok"))
    r_S32, r_S16 = Res("S32"), Res("S16")
    r_sqo = r_mixtok[0]
    r_g = Res("gsmall")
    r_egl = [Res(f"egl{i}") for i in range(NT)]
    r_ssn = Res("ssn")
    r_mixT = [Res(f"mixT{i}") for i in range(NT)]
    BETA0, NB0, G0, AL0, NBG0, KDS0, EGL0, SSN0, TMP0 = 192, 256, 320, 384, 392, 456, 520, 648, 656
    MSB = cF.ap(384, [(0, 4), (1, 128)])
    MITB = cF.ap(512, [(0, 4), (1, 128)])
    BTB = cF.ap(0, [(0, 4), (1, 128)])
    IDB4 = cB.ap(0, [(0, 4), (1, 128)])

    def colb(base, i):
        return small.ap(base + i * 4, [(1, 4), (0, 128)])

    bank_rr = [0]

    def nb():
        bank_rr[0] = (bank_rr[0] + 1) % 2
        return bank_rr[0]

    TH = [(4, 16), (1, 4)]
    sg.add("act", lambda e: e.activation(small.ap(BETA0, TH), small.ap(BA0, [(8, 16), (1, 4)]), AF.Exp, scale=-1.0), reads=[r_ba], writes=[r_g])
    sg.add("dve", lambda e: e.tensor_scalar(small.ap(BETA0, TH), small.ap(BETA0, TH), 1.0, None, ALU.add), reads=[r_g], writes=[r_g])
    sg.add("dve", lambda e: e.reciprocal(small.ap(BETA0, TH), small.ap(BETA0, TH)), reads=[r_g], writes=[r_g])
    sg.add("dve", lambda e: e.tensor_scalar(small.ap(NB0, TH), small.ap(BETA0, TH), -1.0, None, ALU.mult), reads=[r_g], writes=[r_g])
    sg.add("dve", lambda e: e.tensor_tensor(small.ap(G0, TH), small.ap(BA0 + 4, [(8, 16), (1, 4)]), prm.ap(60, [(0, 16), (1, 4)]), ALU.add),
           reads=[r_ba, r_const, r_g], writes=[r_g])
    sg.add("act", lambda e: e.activation(small.ap(G0, TH), small.ap(G0, TH), AF.Exp), reads=[r_g], writes=[r_g])
    sg.add("act", lambda e: e.activation(small.ap(G0, TH), small.ap(G0, TH), AF.Ln, bias=1.0, scale=1.0), reads=[r_g], writes=[r_g])
    sg.add("act", lambda e: e.activation(small.ap(AL0, [(1, 4)]), prm.ap(56, [(1, 4)]), AF.Exp), reads=[r_const, r_g], writes=[r_g])
    sg.add("dve", lambda e: e.scalar_tensor_tensor(small.ap(G0, TH), small.ap(G0, TH), -1.0, small.ap(AL0, [(0, 16), (1, 4)]), ALU.mult, ALU.mult),
           reads=[r_g], writes=[r_g])
    sg.add("pe", lambda e: e.matmul(ps[0].ap(0, [(1, 64)]), BT, small.ap(G0, [(1, 64)]), start=True, stop=True), reads=[r_g, r_const], writes=[psr[0]])
    sg.add("pe", lambda e: e.matmul(ps[1].ap(0, [(1, 64)]), YS, small.ap(G0, [(1, 64)]), start=True, stop=True), reads=[r_g, r_const], writes=[psr[1]])
    sg.add("act", lambda e: e.activation(small.ap(NBG0, [(1, 64)]), ps[0].ap(0, [(1, 64)]), AF.Exp), reads=[psr[0], r_g], writes=[r_g])
    sg.add("dve", lambda e: e.tensor_tensor(small.ap(NBG0, [(1, 64)]), small.ap(NBG0, [(1, 64)]), small.ap(NB0, [(1, 64)]), ALU.mult), reads=[r_g], writes=[r_g])
    sg.add("act", lambda e: e.activation(small.ap(KDS0, [(1, 64)]), ps[1].ap(0, [(1, 64)]), AF.Exp), reads=[psr[1], r_g], writes=[r_g])
    sg.add("dve", lambda e: e.memset(S32.ap(), 0.0), writes=[r_S32])
    sg.add("dve", lambda e: e.memset(S16.ap(), 0.0), writes=[r_S16])

    def mm4(e, bank, lhs, rhs, bf=False, ident_rhs=None):
        ins = None
        for h in range(4):
            out = (psb if bf else ps)[bank].ap(h * 128, [(1, 128)])
            if ident_rhs is None:
                ins = e.matmul(out, lhs(h), rhs(h), start=True, stop=True)
            else:
                e.matmul(out, lhs(h), rhs(h), start=True, stop=False)
                ins = e.matmul(out, IDB, ident_rhs(h), start=False, stop=True)
        return ins

    hv = lambda buf: (lambda h: buf.ap(h * 128, [(1, 128)]))
    copy_rr = [0]

    def evac(bank, dst, r_dst, bf_src=False):
        copy_rr[0] += 1
        src = (psb if bf_src else ps)[bank].ap(0, [(1, 512)])
        if copy_rr[0] % 3:
            sg.add("act", lambda e: e.copy(dst.ap(), src), reads=[psr[bank]], writes=[r_dst])
        else:
            sg.add("dve", lambda e: e.tensor_copy(dst.ap(), src), reads=[psr[bank]], writes=[r_dst])

    def gdn_par(i):
        b = 0
        bb = i % 2
        sg.add("dve", lambda e, b=b, bb=bb, i=i: e.tensor_tensor(Xf[b].ap(0, H4), BTB, colb(G0, i), ALU.mult), reads=[r_g, r_const], writes=[r_X[b]])
        k0 = nb()
        sg.add("pe", lambda e, b=b, bb=bb, k0=k0: mm4(e, k0, hv(Xf[b]), lambda h: YS), reads=[r_X[b], r_const], writes=[psr[k0]], cost=16)
        sg.add("act", lambda e, b=b, bb=bb, k0=k0: e.activation(Ef[b].ap(), ps[k0].ap(), AF.Exp), reads=[psr[k0]], writes=[r_E[b]])
        k1 = nb()
        sg.add("pe", lambda e, b=b, bb=bb, k1=k1: e.matmul(ps[k1].ap(), YS, Xf[b].ap(), start=True, stop=True), reads=[r_X[b], r_const], writes=[psr[k1]], cost=16)
        sg.add("act", lambda e, b=b, bb=bb, k1=k1: e.activation(ETf[b].ap(), ps[k1].ap(), AF.Exp), reads=[psr[k1]], writes=[r_ET[b]])
        k2 = nb()
        sg.add("pe", lambda e, b=b, bb=bb, k2=k2: e.matmul(ps[k2].ap(), ONESF, Xf[b].ap(), start=True, stop=True), reads=[r_X[b], r_const], writes=[psr[k2]], cost=16)
        sg.add("act", lambda e, b=b, bb=bb, k2=k2: e.activation(egcf[b].ap(), ps[k2].ap(), AF.Exp), reads=[psr[k2]], writes=[r_egc[b]])
        sg.add("act", lambda e, k2=k2, i=i: e.activation(small.ap(EGL0 + i * 8, [(2, 4), (1, 2)]), ps[k2].ap(63, [(128, 4), (64, 2)]), AF.Exp),
               reads=[psr[k2]], writes=[r_egl[i]])
        sg.add("dve", lambda e, b=b, bb=bb: e.tensor_tensor(Ef[b].ap(0, H4), Ef[b].ap(0, H4), MSB, ALU.mult), reads=[r_E[b], r_const], writes=[r_E[b]])
        sg.add("dve", lambda e, b=b, bb=bb, i=i: e.tensor_tensor(Ef[b].ap(0, H4), Ef[b].ap(0, H4), colb(NB0, i), ALU.mult), reads=[r_E[b], r_g], writes=[r_E[b]])
        sg.add("dve", lambda e, b=b, bb=bb: e.tensor_tensor(ETf[b].ap(0, H4), ETf[b].ap(0, H4), MITB, ALU.mult), reads=[r_ET[b], r_const], writes=[r_ET[b]])
        gkt = lambda h, i=i: gk.ap(h * T + i * P, [(1, P)])
        gqt = lambda h, i=i: gq.ap(h * T + i * P, [(1, P)])
        k3, k4 = nb(), nb()
        sg.add("pe", lambda e, k3=k3, gkt=gkt: mm4(e, k3, gkt, gkt), reads=r_gk, writes=[psr[k3]])
        sg.add("pe", lambda e, k4=k4, gkt=gkt, gqt=gqt: mm4(e, k4, gkt, gqt), reads=r_gk + r_gq, writes=[psr[k4]])
        sg.add("dve", lambda e, b=b, bb=bb, k3=k3: e.tensor_tensor(Pb[b][0].ap(), ps[k3].ap(), Ef[b].ap(), ALU.mult), reads=[psr[k3], r_E[b]], writes=[r_P[b][0]])
        sg.add("dve", lambda e, b=b, bb=bb, k4=k4: e.tensor_tensor(AqkT[bb].ap(), ps[k4].ap(), ETf[b].ap(), ALU.mult), reads=[psr[k4], r_ET[b]], writes=[r_Aqk[bb]])
        k5 = nb()

        def trP(e, b=b, k5=k5):
            ins = None
            for h in range(4):
                ins = e.transpose(psb[k5].ap(h * 128, [(1, 128)]), Pb[b][0].ap(h * 128, [(1, 128)]), IDB)
            return ins
        sg.add("pe", trP, reads=[r_P[b][0], r_const], writes=[psr[k5]])
        evac(k5, Qb[b][0], r_Q[b][0], bf_src=True)
        sg.add("dve", lambda e, b=b, bb=bb: e.tensor_tensor(Rb[b][0].ap(0, H4), Qb[b][0].ap(0, H4), IDB4, ALU.add), reads=[r_Q[b][0], r_const], writes=[r_R[b][0]])
        for k in range(5):
            c, n = k % 2, (k + 1) % 2
            kp = nb()
            sg.add("pe", lambda e, b=b, bb=bb, c=c, kp=kp: mm4(e, kp, hv(Qb[b][c]), hv(Pb[b][c])), reads=[r_Q[b][c], r_P[b][c]], writes=[psr[kp]])
            if k < 4:
                kq = nb()
                sg.add("pe", lambda e, b=b, bb=bb, c=c, kq=kq: mm4(e, kq, hv(Pb[b][c]), hv(Qb[b][c])), reads=[r_Q[b][c], r_P[b][c]], writes=[psr[kq]])
            evac(kp, Pb[b][n], r_P[b][n])
            if k < 4:
                evac(kq, Qb[b][n], r_Q[b][n])
            kr = nb()
            sg.add("pe", lambda e, b=b, bb=bb, c=c, n=n, kr=kr: mm4(e, kr, hv(Pb[b][n]), hv(Rb[b][c])),
                   reads=[r_P[b][n], r_R[b][c]], writes=[psr[kr]], cost=4)
            sg.add("dve", lambda e, b=b, bb=bb, c=c, n=n, kr=kr: e.tensor_tensor(Rb[b][n].ap(), ps[kr].ap(), Rb[b][c].ap(), ALU.add),
                   reads=[psr[kr], r_R[b][c]], writes=[r_R[b][n]])
        TT = Rb[b][1]
        r_TT = r_R[b][1]
        sg.add("dve", lambda e, b=b, bb=bb, i=i: e.tensor_tensor(qg[bb].ap(0, H4), gq.ap(i * P, [(T, 4), (1, P)]), egcf[b].ap(0, H4), ALU.mult),
               reads=r_gq + [r_egc[b]], writes=[r_qg[bb]])
        k6 = nb()

        def trK(e, k6=k6, gkt=gkt):
            ins = None
            for h in range(4):
                ins = e.transpose(psb[k6].ap(h * 128, [(1, 128)]), gkt(h), IDB)
            return ins
        sg.add("pe", trK, reads=r_gk + [r_const], writes=[psr[k6]])
        sg.add("dve", lambda e, b=b, bb=bb, i=i, k6=k6: e.tensor_tensor(nkbg[b].ap(0, H4), psb[k6].ap(0, H4), colb(NBG0, i), ALU.mult),
               reads=[psr[k6], r_g], writes=[r_nkbg[b]])
        sg.add("dve", lambda e, b=b, bb=bb, i=i, k6=k6: e.tensor_tensor(kdec[bb].ap(0, H4), psb[k6].ap(0, H4), colb(KDS0, i), ALU.mult),
               reads=[psr[k6], r_g], writes=[r_kdec[bb]])
        sg.add("pool", lambda e, b=b, bb=bb, i=i: e.tensor_tensor(vb[b].ap(0, H4), vtok.ap(i * 512, H4), colb(BETA0, i), ALU.mult),
               reads=[r_vtok[i], r_g], writes=[r_vb[b]])
        k7, k8 = nb(), nb()
        sg.add("pe", lambda e, b=b, bb=bb, k7=k7, TT=TT: mm4(e, k7, hv(TT), hv(vb[b])), reads=[r_TT, r_vb[b]], writes=[psr[k7]])
        evac(k7, ub[bb], r_u[bb])
        sg.add("pe", lambda e, b=b, bb=bb, k8=k8, TT=TT: mm4(e, k8, hv(nkbg[b]), hv(TT)), reads=[r_TT, r_nkbg[b]], writes=[psr[k8]])
        evac(k8, nwT[bb], r_nwT[bb])

    def gdn_seq(i):
        b = 0
        bb = i % 2
        for ci in range(2):
            r0 = 64 * ci
            rows = lambda buf, h, r0=r0: buf.ap(h * 128, [(1, 128)], p0=r0, np_=64)
            kv = 2
            sg.add("pe", lambda e, b=b, bb=bb, kv=kv: mm4(e, kv, hv(nwT[bb]), hv(S16)), reads=[r_nwT[bb], r_S16], writes=[psr[kv]], cost=4)
            sg.add("dve", lambda e, b=b, bb=bb, r0=r0, kv=kv: e.tensor_tensor(vn16[b].ap(0, [(1, 512)], p0=r0, np_=64), ps[kv].ap(0, [(1, 512)], p0=r0, np_=64),
                                                                       ub[bb].ap(0, [(1, 512)], p0=r0, np_=64), ALU.add),
                   reads=[psr[kv], r_u[bb]], writes=[r_vn[b]])

            def omm(e, b=b, bb=bb, rows=rows):
                ins = None
                for h in range(4):
                    out = ps[3].ap(h * 128, [(1, 128)])
                    e.matmul(out, qg[bb].ap(h * 128, [(1, 128)]), S16.ap(h * 128, [(1, 128)]), start=True, stop=False)
                    ins = e.matmul(out, rows(AqkT[bb], h), rows(vn16[b], h), start=False, stop=True)
                return ins
            sg.add("pe", omm, reads=[r_qg[bb], r_S16, r_Aqk[bb], r_vn[b]], writes=[psr[3]], cost=8)
            sg.add("act", lambda e, b=b, bb=bb, r0=r0: e.copy(otok[b].ap(0, [(1, 512)], p0=r0, np_=64), ps[3].ap(0, [(1, 512)], p0=r0, np_=64)),
                   reads=[psr[3]], writes=[r_otok[b]])
            ks = 2
            sg.add("pe", lambda e, b=b, bb=bb, rows=rows, ks=ks: mm4(e, ks, lambda h: rows(kdec[bb], h), lambda h: rows(vn16[b], h)),
                   reads=[r_kdec[bb], r_vn[b]], writes=[psr[ks]], cost=4)

            def supd(e, i=i, ci=ci, ks=ks):
                ins = None
                for h in range(4):
                    ins = e.scalar_tensor_tensor(S32.ap(h * 128, [(1, 128)]), S32.ap(h * 128, [(1, 128)]),
                                                 small.ap(EGL0 + i * 8 + h * 2 + ci, [(1, 1)]), ps[ks].ap(h * 128, [(1, 128)]), ALU.mult, ALU.add)
                return ins
            sg.add("dve", supd, reads=[psr[ks], r_egl[i], r_S32], writes=[r_S32])
            sg.add("act", lambda e: e.copy(S16.ap(), S32.ap()), reads=[r_S32], writes=[r_S16])
        def sqs(e, b=b):
            ins = None
            for h in range(4):
                ins = e.activation(sqo.ap(h * 128, [(1, 128)]), otok[b].ap(h * 128, [(1, 128)]), AF.Square, accum_out=small.ap(SSN0 + h, [(1, 1)]))
            return ins
        sg.add("act", sqs, reads=[r_otok[b], r_ssn], writes=[r_sqo, r_ssn])
        sg.add("act", lambda e: e.activation(small.ap(SSN0, [(1, 4)]), small.ap(SSN0, [(1, 4)]), AF.Ln, bias=float(128 * EPS), scale=1.0), reads=[r_ssn], writes=[r_ssn])
        sg.add("act", lambda e: e.activation(small.ap(SSN0, [(1, 4)]), small.ap(SSN0, [(1, 4)]), AF.Exp, bias=float(0.5 * np.log(128.0)), scale=-0.5),
               reads=[r_ssn], writes=[r_ssn])
        sg.add("dve", lambda e, b=b, bb=bb: e.tensor_tensor(otok[b].ap(0, H4), otok[b].ap(0, H4), small.ap(SSN0, [(1, 4), (0, 128)]), ALU.mult),
               reads=[r_otok[b], r_ssn], writes=[r_otok[b]])
        sg.add("dve", lambda e, b=b, bb=bb, i=i: e.tensor_tensor(mixtok[b].ap(), otok[b].ap(), zgdn.ap(i * 512, [(1, 512)]), ALU.mult),
               reads=[r_otok[b], r_zgdn[i]], writes=[r_mixtok[b]])
        k9 = 2

        def trM(e, b=b, k9=k9):
            ins = None
            for h in range(4):
                ins = e.transpose(psb[k9].ap(h * 128, [(1, 128)]), mixtok[b].ap(h * 128, [(1, 128)]), IDB)
            return ins
        sg.add("pe", trM, reads=[r_mixtok[b], r_const], writes=[psr[k9]])
        sg.add("act", lambda e, i=i, k9=k9: e.copy(mixT.ap(i * P, [(T, 4), (1, P)]), psb[k9].ap(0, H4)), reads=[psr[k9]], writes=[r_mixT[i]])

    sa = Rec()
    PT = [cv.take(512, BF16) for _ in range(4)]
    r_PT = [Res(f"PT{j}") for j in range(4)]
    o12 = cv.take(4 * 258, F32)
    r_o12 = [Res(f"o12_{h}") for h in range(4)]
    mixtk = cv.take(512, BF16)
    r_mixtk = Res("mixtk")
    r_lam = Res("lam")
    r_rl = Res("rl")
    r_ss2 = Res("ss2")
    LAM0, NLAM, RL0, SS20, LTMP = 700, 704, 708, 720, 192
    O1 = [(258, 4), (1, 128)]
    sa.add("dve", lambda e: e.tensor_tensor(o12.ap(0, [(64, 2), (1, 64)]), prm.ap(322, [(128, 2), (1, 64)]), prm.ap(386, [(128, 2), (1, 64)]), ALU.mult),
           reads=[r_const], writes=r_o12)
    sa.add("dve", lambda e: e.tensor_reduce(small.ap(LAM0, [(1, 2)]), o12.ap(0, [(64, 2), (1, 64)]), AX.X, ALU.add), reads=r_o12, writes=[r_lam])
    sa.add("act", lambda e: e.activation(small.ap(LAM0, [(1, 2)]), small.ap(LAM0, [(1, 2)]), AF.Exp), reads=[r_lam], writes=[r_lam])
    sa.add("dve", lambda e: e.tensor_tensor(small.ap(NLAM, [(1, 1)]), small.ap(LAM0 + 1, [(1, 1)]), small.ap(LAM0, [(1, 1)]), ALU.subtract), reads=[r_lam], writes=[r_lam])
    sa.add("dve", lambda e: e.tensor_scalar(small.ap(NLAM, [(1, 1)]), small.ap(NLAM, [(1, 1)]), float(-LAMBDA_INIT), None, ALU.add), reads=[r_lam], writes=[r_lam])
    EB2 = float(0.5 * np.log(128.0) + np.log(1.0 - LAMBDA_INIT))
    OB = 7

    groups = []
    for qt in range(NT):
        for h in range(4):
            g0s = list(range(0, qt + 1, 4))
            for gi_, g0 in enumerate(g0s):
                groups.append(dict(qt=qt, h=h, kts=list(range(g0, min(g0 + 4, qt + 1))), last=(gi_ == len(g0s) - 1)))
    for k, g in enumerate(groups):
        g["sb"] = (4 + (2 * k) % 3, 4 + (2 * k + 1) % 3)
        g["pj"] = ((2 * k) % 4, (2 * k + 1) % 4)

    def emit_scores(g):
        kts, sbks, h, qt, pjs = g["kts"], g["sb"], g["h"], g["qt"], g["pj"]
        n = len(kts)

        def smm(e):
            ins = None
            for j, kt in enumerate(kts):
                for c in range(2):
                    out = ps[sbks[c]].ap(j * 128, [(1, 128)])
                    lh = dK.ap(h * T + kt * P, [(1, P)], p0=c * 64, np_=64)
                    rh = dQ.ap(h * T + qt * P, [(1, P)], p0=c * 64, np_=64)
                    ins = e.matmul(out, lh, rh, start=True, stop=(kt != qt))
                if kt == qt:
                    for c in range(2):
                        ins = e.matmul(ps[sbks[c]].ap(j * 128, [(1, 128)]), IDB, NEGM, start=False, stop=True)
            return ins
        sa.add("pe", smm, reads=[r_dK[h], r_dQ[h], r_const], writes=[psr[sbks[0]], psr[sbks[1]]], cost=2 * n + (1 if qt in kts else 0))
        for c in range(2):
            sa.add("act", lambda e, sbk=sbks[c], pj=pjs[c]: e.activation(PT[pj].ap(0, [(1, n * 128)]), ps[sbk].ap(0, [(1, n * 128)]), AF.Exp),
                   reads=[psr[sbks[c]]], writes=[r_PT[pjs[c]]])

    def emit_pv(g):
        kts, h, qt, pjs = g["kts"], g["h"], g["qt"], g["pj"]

        def pvm(e):
            ins = None
            for c in range(2):
                for j, kt in enumerate(kts):
                    ins = e.matmul(ps[OB].ap(c * 129, [(1, 129)]), PT[pjs[c]].ap(j * 128, [(1, 128)]), VA.ap(kt * 520 + h * 130, [(1, 129)]),
                                   start=(kt == 0 and c == 0), stop=(kt == qt), skip_group_check=True)
            return ins
        sa.add("pe", pvm, reads=[r_PT[pjs[0]], r_PT[pjs[1]]] + [r_VA[kt] for kt in kts], writes=[psr[OB]], cost=2 * len(kts))
        if g["last"]:
            sa.add("act", lambda e: e.copy(o12.ap(h * 258, [(1, 258)]), ps[OB].ap(0, [(1, 258)])), reads=[psr[OB]], writes=[r_o12[h]])
            if h == 3:
                emit_epilogue(qt)

    def emit_epilogue(qt):
        sa.add("dve", lambda e: e.reciprocal(small.ap(RL0, [(1, 8)]), o12.ap(128, [(129, 8), (1, 1)])), reads=r_o12, writes=[r_rl])
        sa.add("dve", lambda e: e.tensor_tensor(o12.ap(0, O1), o12.ap(0, O1), small.ap(RL0, [(2, 4), (0, 128)]), ALU.mult), reads=r_o12 + [r_rl], writes=r_o12)
        sa.add("dve", lambda e: e.tensor_tensor(o12.ap(129, O1), o12.ap(129, O1), small.ap(RL0 + 1, [(2, 4), (0, 128)]), ALU.mult), reads=r_o12 + [r_rl], writes=r_o12)
        sa.add("dve", lambda e: e.scalar_tensor_tensor(o12.ap(0, O1), o12.ap(129, O1), small.ap(NLAM, [(1, 1)]), o12.ap(0, O1), ALU.mult, ALU.add),
               reads=r_o12 + [r_lam], writes=r_o12)

        def sqs2(e):
            ins = None
            for h in range(4):
                ins = e.activation(o12.ap(h * 258 + 129, [(1, 128)]), o12.ap(h * 258, [(1, 128)]), AF.Square, accum_out=small.ap(SS20 + h, [(1, 1)]))
            return ins
        sa.add("act", sqs2, reads=r_o12 + [r_ss2], writes=r_o12 + [r_ss2])
        sa.add("act", lambda e: e.activation(small.ap(SS20, [(1, 4)]), small.ap(SS20, [(1, 4)]), AF.Ln, bias=float(128 * EPS), scale=1.0), reads=[r_ss2], writes=[r_ss2])
        sa.add("act", lambda e: e.activation(small.ap(SS20, [(1, 4)]), small.ap(SS20, [(1, 4)]), AF.Exp, bias=EB2, scale=-0.5), reads=[r_ss2], writes=[r_ss2])
        sa.add("dve", lambda e: e.tensor_tensor(o12.ap(0, O1), o12.ap(0, O1), small.ap(SS20, [(1, 4), (0, 128)]), ALU.mult), reads=r_o12 + [r_ss2], writes=r_o12)
        sa.add("dve", lambda e: e.tensor_tensor(mixtk.ap(0, H4), o12.ap(0, O1), zgdf.ap(qt * 512, H4), ALU.mult),
               reads=r_o12 + [r_zgdf[qt]], writes=[r_mixtk])
        tbk = groups[min(len(groups) - 1, 0)]["sb"][0]
        tbk = 4 + (qt % 3)

        def trD(e):
            ins = None
            for h in range(4):
                ins = e.transpose(psb[tbk].ap(h * 128, [(1, 128)]), mixtk.ap(h * 128, [(1, 128)]), IDB)
            return ins
        sa.add("pe", trD, reads=[r_mixtk, r_const], writes=[psr[tbk]])
        sa.add("act", lambda e: e.copy(mixT.ap(4 * T + qt * P, [(T, 4), (1, P)]), psb[tbk].ap(0, H4)), reads=[psr[tbk]], writes=[r_mixT[qt]])

    for k, g in enumerate(groups):
        emit_scores(g)
        if k >= 1:
            emit_pv(groups[k - 1])
    emit_pv(groups[-1])

    def merge2(a, b):
        out = Rec()
        ta = float(sum(it[0] for it in a.items)) or 1.0
        tb = float(sum(it[0] for it in b.items)) or 1.0
        ia = ib = 0
        ca = cb = 0.0
        while ia < len(a.items) or ib < len(b.items):
            if ib >= len(b.items) or (ia < len(a.items) and ca / ta <= cb / tb):
                it = a.items[ia]
                ia += 1
                ca += it[0]
            else:
                it = b.items[ib]
                ib += 1
                cb += it[0]
            out.items.append(it)
        return out

    pars, seqs = [], []
    for i in range(NT):
        _Proxy.cur = Rec()
        gdn_par(i)
        pars.append(_Proxy.cur)
        _Proxy.cur = Rec()
        gdn_seq(i)
        seqs.append(_Proxy.cur)
    sgall = Rec()
    sgall.items += g_prep.items + pars[0].items
    for i in range(NT):
        nxt = pars[i + 1] if i + 1 < NT else Rec()
        sgall.items += merge2(nxt, seqs[i]).items
    sg = sgall

    merge_streams(sg, sa)
    if debug and "odn" in debug:
        for c in range(4):
            for q in range(4):
                pg.add("dve", lambda e, c=c, q=q: e.tensor_copy(o12.ap(0, [(1, 512)]), mixT.ap(c * T + q * 512, [(1, 512)])), reads=r_mixT, writes=r_o12)
                dma(dbg_d["odn"][c * P:(c + 1) * P, q * 512:(q + 1) * 512], o12.ap(0, [(1, 512)]), "dbg", reads=r_o12, writes=[Res("dbgout")])
    if debug and "odf" in debug:
        for c in range(4):
            for q in range(4):
                pg.add("dve", lambda e, c=c, q=q: e.tensor_copy(o12.ap(0, [(1, 512)]), mixT.ap((4 + c) * T + q * 512, [(1, 512)])), reads=r_mixT, writes=r_o12)
                dma(dbg_d["odf"][c * P:(c + 1) * P, q * 512:(q + 1) * 512], o12.ap(0, [(1, 512)]), "dbg", reads=r_o12, writes=[Res("dbgout")])

    pg.mark("M")

    cv = cvG
    cv.reset()
    wob = cv.take(KD * 1024, BF16)
    wstC = [cv.take(KD * 256, F32) for _ in range(2)]
    r_wob_parts = {}
    xt2 = [cv.take(D, F32) for _ in range(2)]
    yo = [cv.take(D, F32) for _ in range(2)]
    r_wstC = [Res("cwst0"), Res("cwst1")]
    r_wob = [Res(f"wob{q}") for q in range(8)]
    r_xt2, r_yo = R2("xt2"), R2("yo")
    for q in range(4):
        st = q % 2
        dma(wstC[st].ap(0, [(256, KD), (1, 256)]), wout_d.rearrange("(k p) c -> p k c", p=P)[:, :, q * 256:(q + 1) * 256], f"wc{st}", writes=[r_wstC[st]] + r_gk[2 * st:2 * st + 2], dur=9.0)
        for kd in range(KD):
            ce = ("pool", "dve", "act", "dve")[kd % 4]
            src_ap = lambda st=st, kd=kd: wstC[st].ap(kd * 256, [(1, 256)])
            dst_ap = lambda q=q, kd=kd: wob.ap(kd * 1024 + q * 256, [(1, 256)])
            rw = Res(f"wobp{q}_{kd}")
            r_wob_parts.setdefault(q, []).append(rw)
            if ce == "act":
                pg.add("act", lambda e, src_ap=src_ap, dst_ap=dst_ap: e.copy(dst_ap(), src_ap()), reads=[r_wstC[st]], writes=[rw, r_gq[kd // 2]], dur=0.4)
            else:
                pg.add(ce, lambda e, src_ap=src_ap, dst_ap=dst_ap: e.tensor_copy(dst_ap(), src_ap()), reads=[r_wstC[st]], writes=[rw, r_gq[kd // 2]], dur=(0.9 if ce == "pool" else 0.4))
    for i in range(NT):
        s_ = i % 2
        dma(xt2[s_].ap(), x_d[i * P:(i + 1) * P, :], f"x{s_}", writes=[r_xt2[s_]] + r_vtok[4 * s_:4 * s_ + 4])
        for half in range(2):
            pb = (i * 2 + half) % 4
            pg.add("pe", lambda e, pb=pb, i=i, half=half: mm_acc(
                e, ps[pb].ap(), [(mixT.ap(c * T + i * P, [(1, P)]), wob.ap(c * 1024 + half * 512, [(1, 512)])) for c in range(8)]),
                reads=[r_mixT[i]] + r_wob_parts[2 * half] + r_wob_parts[2 * half + 1], writes=[psr[pb]], dur=2.15)
            pg.add("dve", lambda e, pb=pb, s_=s_, half=half: e.tensor_tensor(yo[s_].ap(half * 512, [(1, 512)]), ps[pb].ap(), xt2[s_].ap(half * 512, [(1, 512)]), ALU.add),
                   reads=[psr[pb], r_xt2[s_]], writes=[r_yo[s_]] + r_vtok[8 + 4 * s_:12 + 4 * s_])
        dma(y_d[i * P:(i + 1) * P, :], yo[s_].ap(), f"y{s_}", reads=[r_yo[s_]], writes=[Res("yout")])

    if debug and "hT" in debug:
        cv.reset()
        stg = cv.take(T, F32)
        r_stg = Res("stg")
        for kd in range(KD):
            pg.add("dve", lambda e, kd=kd: e.tensor_copy(stg.ap(), hT.ap(kd * T, [(1, T)])), reads=r_hT, writes=[r_stg])
            dma(dbg_d["hT"][kd * P:(kd + 1) * P, :], stg.ap(), "dbg", reads=[r_stg], writes=[Res("dbgout")])

    out_chans = [c for c in ("y0", "y1", "dbg")]
    import os as _os
    if _os.environ.get("KNOSCHED") != "1":
        pg.schedule(lat=float(_os.environ.get("KLAT", "0.5")))
    pg.plan()

    esem = {e: getsem(("e", e)) for e in ("pe", "act", "dve", "pool")}

    def semfor(key):
        if key[0] == "e":
            return esem[key[1]]
        return getsem(key)

    def run_engine(eng_name, eng):
        for op in pg.ops:
            if op.eng != eng_name:
                continue
            for key, v in op.waits:
                eng.wait_ge(semfor(key), v)
            if op.fn is None:
                continue
            ins = op.fn(eng)
            if op.chan is not None:
                ins.then_inc(getsem(("c", op.chan)), 16)
            elif op.signal:
                ins.then_inc(esem[op.eng], 1)
        if eng_name == "sp":
            for c in out_chans:
                if c in pg.chans:
                    eng.wait_ge(getsem(("c", c)), pg.chans[c])

    with nc.Block() as block:
        @block.sync
        def _(e):
            run_engine("sp", e)

        @block.tensor
        def _(e):
            run_engine("pe", e)

        @block.scalar
        def _(e):
            run_engine("act", e)

        @block.vector
        def _(e):
            run_engine("dve", e)

        @block.gpsimd
        def _(e):
            run_engine("pool", e)

    es.close()
    return nc


def make_consts():
    j = np.arange(128)
    same = (j[:, None] // 64) == (j[None, :] // 64)
    BT = (same & (j[:, None] <= j[None, :])).astype(np.float32)
    YS = (same & (j[:, None] > j[None, :])).astype(np.float32)
    ONES = np.ones((128, 128), np.float32)
    MS = (same & (j[None, :] < j[:, None])).astype(np.float32)
    MIT = (same & (j[:, None] <= j[None, :])).astype(np.float32)
    cF = np.concatenate([BT, YS, ONES, MS, MIT], axis=1)
    ident = np.eye(128, dtype=np.float32)
    negm = np.where(j[:, None] > j[None, :], -30000.0, 0.0).astype(np.float32)
    blk2 = same.astype(np.float32)
    cB = np.concatenate([ident, ONES, negm, blk2], axis=1).astype(ml_dtypes.bfloat16)
    return cF, cB


def pack_params(inp):
    prm = np.zeros((P, NPRM), np.float32)
    prm[:, 0:8] = inp["norm_gain"][0].reshape(KD, P).T
    cw = inp["conv_w"][0]
    prm[:, 8:56] = cw.reshape(4, 12, P).transpose(2, 1, 0).reshape(P, 48)
    prm[:, 56:60] = inp["a_log"][0][None, :]
    prm[:, 60:64] = inp["dt_bias"][0][None, :]
    prm[:, 64:192] = inp["dn_out_gain"][0][None, :]
    prm[:, 192:320] = inp["df_out_gain"][0][None, :]
    prm[:, 320] = np.tile(inp["q_gain"][0], 2)
    prm[:, 321] = np.tile(inp["k_gain"][0], 2)
    prm[:, 322:386] = inp["lambda_q1"][0][None, :]
    prm[:, 386:450] = inp["lambda_k1"][0][None, :]
    prm[:, 450:514] = inp["lambda_q2"][0][None, :]
    prm[:, 514:578] = inp["lambda_k2"][0][None, :]
    return prm


def kernel(**inputs):
    inp = {k: np.asarray(v) for k, v in inputs.items()}
    n = 8
    nc = build_nc(DEBUG)
    cF, cB = make_consts()
    prm = pack_params(inp)
    w_in = np.ascontiguousarray(inp["w_in"][0], dtype=np.float32)
    w_out = np.ascontiguousarray(inp["w_out"][0], dtype=np.float32)
    in_maps = []
    for c in range(n):
        in_maps.append({"x": np.ascontiguousarray(inp["x"][c], dtype=np.float32), "w_in": w_in, "w_out": w_out,
                        "prm": prm, "cF": cF, "cB": cB})
    res = run_bass_kernel_spmd(nc, in_maps, core_ids=list(range(n)))
    kernel.last = res
    return np.stack([r["y"] for r in res.results], axis=0).astype(np.float32)
```

```python
import numpy as np
import ml_dtypes
from contextlib import ExitStack
import concourse.bass as bass
import concourse.mybir as mybir
from concourse.bass_utils import run_bass_kernel_spmd

F32 = mybir.dt.float32
BF16 = mybir.dt.bfloat16
AF = mybir.ActivationFunctionType
ALU = mybir.AluOpType
AX = mybir.AxisListType

P = 128
T = 2048
D = 1024
NT = 16
KD = 8
EPS = 1e-6
IN_COLS = 4104
LAMBDA_INIT = 0.8 - 0.6
NPRM = 578

DEBUG = None


class Res:
    __slots__ = ("name", "w", "rs")

    def __init__(self, name):
        self.name = name
        self.w = None
        self.rs = []


class Op:
    __slots__ = ("eng", "fn", "deps", "idx", "signal", "cnt", "chan", "waits", "dur")


class Prog:
    DEF_DUR = {"pe": 0.5, "act": 0.6, "dve": 0.7, "pool": 1.5, "sp": 3.0}

    def __init__(self):
        self.ops = []
        self.chans = {}
        self.cuts = []

    def cut(self):
        self.cuts.append(len(self.ops))

    def schedule(self, lat=0.5):
        bounds = [0] + [c for c in self.cuts if 0 < c < len(self.ops)] + [len(self.ops)]
        new = []
        for lo, hi in zip(bounds[:-1], bounds[1:]):
            if hi > lo:
                new += self._sched(self.ops[lo:hi], lat)
        self.ops = new
        for k, op in enumerate(self.ops):
            op.idx = k

    @staticmethod
    def _sched(ops, lat):
        n = len(ops)
        pos = {id(op): k for k, op in enumerate(ops)}
        preds = [[pos[id(d)] for d in op.deps if id(d) in pos] for op in ops]
        succs = [[] for _ in range(n)]
        for k, pl in enumerate(preds):
            for p in pl:
                succs[p].append(k)
        dur = [op.dur for op in ops]
        busy = [0.75 * op.dur if op.chan is not None else op.dur for op in ops]
        prio = [0.0] * n
        for k in range(n - 1, -1, -1):
            m = 0.0
            for s_ in succs[k]:
                if prio[s_] > m:
                    m = prio[s_]
            prio[k] = dur[k] + m
        indeg = [len(pl) for pl in preds]
        rtime = [0.0] * n
        ready = {}
        for k in range(n):
            if indeg[k] == 0:
                ready.setdefault(ops[k].eng, []).append(k)
        free = {}
        order = []
        while len(order) < n:
            best = None
            for e, lst in ready.items():
                if not lst:
                    continue
                t_e = free.get(e, 0.0)
                cand = None
                for k in lst:
                    st = rtime[k] if rtime[k] > t_e else t_e
                    key = (st, -prio[k], k)
                    if cand is None or key < cand[0]:
                        cand = (key, k)
                if best is None or cand[0] < best[0]:
                    best = (cand[0], cand[1], e)
            (st, _, _), k, e = best
            ready[e].remove(k)
            free[e] = st + busy[k]
            fin = st + dur[k]
            order.append(k)
            for s_ in succs[k]:
                if rtime[s_] < fin + lat:
                    rtime[s_] = fin + lat
                indeg[s_] -= 1
                if indeg[s_] == 0:
                    ready.setdefault(ops[s_].eng, []).append(s_)
        return [ops[k] for k in order]

    def mark(self, name):
        import os
        if os.environ.get("KSTOP") == name:
            self.frozen = True

    def add(self, eng, fn, reads=(), writes=(), chan=None, dur=None):
        if getattr(self, "frozen", False):
            return None
        op = Op()
        op.dur = self.DEF_DUR[eng] if dur is None else dur
        op.eng = eng
        op.fn = fn
        op.chan = chan
        op.idx = len(self.ops)
        op.signal = False
        op.cnt = 0
        deps = {}
        for r in reads:
            if r.w is not None:
                deps[r.w.idx] = r.w
        for r in writes:
            if r.w is not None:
                deps[r.w.idx] = r.w
            for rd in r.rs:
                deps[rd.idx] = rd
        op.deps = list(deps.values())
        for r in reads:
            r.rs.append(op)
        for r in writes:
            r.w = op
            r.rs = []
        self.ops.append(op)
        return op

    def barrier(self, mk, pe_extra=(), rd=()):
        rs = {e: Res("bar_" + e) for e in ("pe", "act", "dve", "pool")}
        for e in ("pe", "act", "dve", "pool"):
            self.add(e, mk(e), reads=list(rd), writes=[rs[e]] + (list(pe_extra) if e == "pe" else []))
        allr = list(rs.values())
        for e in ("pe", "act", "dve", "pool"):
            self.add(e, mk(e), reads=allr + list(rd), writes=[Res("bar2_" + e)] + (list(pe_extra) if e == "pe" else []))
        self.add("sp", None, reads=allr)

    def plan(self):
        for op in self.ops:
            for d in op.deps:
                if d.chan is not None:
                    continue
                if d.eng == "pe" and op.eng == "pe":
                    continue
                d.signal = True
        cnt = {}
        for op in self.ops:
            if op.chan is not None:
                c = self.chans.get(op.chan, 0) + 16
                self.chans[op.chan] = c
                op.cnt = c
            elif op.signal:
                c = cnt.get(op.eng, 0) + 1
                cnt[op.eng] = c
                op.cnt = c
        seen = {}
        for op in self.ops:
            need = {}
            for d in op.deps:
                if d.chan is not None:
                    key = ("c", d.chan)
                elif d.eng == "pe" and op.eng == "pe":
                    continue
                else:
                    key = ("e", d.eng)
                if need.get(key, 0) < d.cnt:
                    need[key] = d.cnt
            s = seen.setdefault(op.eng, {})
            w = []
            for key, v in need.items():
                if s.get(key, 0) < v:
                    s[key] = v
                    w.append((key, v))
            op.waits = w


class Buf:
    def __init__(self, t, F, dtype):
        self.t = t
        self.F = F
        self.dtype = dtype

    def ap(self, off=0, dims=None, p0=0, np_=P):
        if dims is None:
            dims = [(1, self.F - off)]
        return bass.AP(self.t, p0 * self.F + off, [[self.F, np_]] + [[s, c] for s, c in dims])


def build_nc(debug=None):
    nc = bass.Bass("TRN2", target_bir_lowering=False)
    x_d = nc.dram_tensor("x", [T, D], F32, kind="ExternalInput").ap()
    win_d = nc.dram_tensor("w_in", [D, IN_COLS], F32, kind="ExternalInput").ap()
    wout_d = nc.dram_tensor("w_out", [D, D], F32, kind="ExternalInput").ap()
    prm_d = nc.dram_tensor("prm", [P, NPRM], F32, kind="ExternalInput").ap()
    cF_d = nc.dram_tensor("cF", [P, 5 * 128], F32, kind="ExternalInput").ap()
    cB_d = nc.dram_tensor("cB", [P, 4 * 128], BF16, kind="ExternalInput").ap()
    y_d = nc.dram_tensor("y", [T, D], F32, kind="ExternalOutput").ap()
    dbg_d = {}
    if debug:
        for k, shp in debug.items():
            dbg_d[k] = nc.dram_tensor("dbg_" + k, list(shp), F32, kind="ExternalOutput").ap()

    pg = Prog()
    es = ExitStack()

    def sb(name, F, dtype):
        t = es.enter_context(nc.sbuf_tensor("sb_" + name, [P, F], dtype))
        return Buf(t, F, dtype)

    cF = sb("cF", 5 * 128, F32)
    cB = sb("cB", 4 * 128, BF16)
    prm = sb("prm", NPRM, F32)
    hT = sb("hT", KD * T, BF16)
    mixT = hT
    small = sb("small", 768, F32)
    REG_BYTES = 167 * 1024
    reg = sb("reg", REG_BYTES // 2, BF16)
    regf = Buf(reg.t.bitcast(F32), REG_BYTES // 4, F32)

    class Carver:
        def __init__(self, lo, hi):
            self.lo = lo
            self.hi = hi
            self.off = lo

        def reset(self):
            self.off = self.lo

        def take(self, nelem, dtype):
            bpe = 4 if dtype == F32 else 2
            self.off = (self.off + 3) // 4 * 4
            o = self.off
            self.off += nelem * bpe
            assert self.off <= self.hi, ("region overflow", self.off, self.hi)
            base = regf if dtype == F32 else reg
            return View(base, o // bpe, nelem)

    class View:
        def __init__(self, base, off, n):
            self.base = base
            self.off = off
            self.F = n
            self.dtype = base.dtype

        def ap(self, off=0, dims=None, p0=0, np_=P):
            if dims is None:
                dims = [(1, self.F - off)]
            return self.base.ap(self.off + off, dims, p0, np_)

    KB = 1024
    cvG = Carver(0, 64 * KB)
    cvD = Carver(64 * KB, 64 * KB + 65792)
    cvT = Carver(64 * KB + 65792, REG_BYTES)
    R2 = lambda nm: [Res(nm + "0"), Res(nm + "1")]
    H4 = [(128, 4), (1, 128)]

    class Rec:
        def __init__(self):
            self.items = []

        def add(self, eng, fn, reads=(), writes=(), chan=None, cost=0, dur=None):
            if eng == "pe" and cost == 0:
                cost = 4
            if dur is None and eng == "pe":
                dur = 0.06 + 0.115 * cost
            self.items.append((cost, eng, fn, list(reads), list(writes), chan, dur))

    def merge_streams(a, b):
        import os
        if os.environ.get("KOPS"):
            a.items = a.items[:int(os.environ["KOPS"])]
            print("KOPS", len(a.items), [ (i, it[1], it[2].__code__.co_firstlineno) for i, it in enumerate(a.items)][-3:])
        if os.environ.get("KSEQ") in ("1", "2", "3"):
            its = {"1": a.items + b.items, "2": a.items, "3": b.items}[os.environ.get("KSEQ")]
            for it in its:
                pg.add(it[1], it[2], reads=it[3], writes=it[4], chan=it[5], dur=it[6])
            return
        ta = float(sum(it[0] for it in a.items)) or 1.0
        tb = float(sum(it[0] for it in b.items)) or 1.0
        ia = ib = 0
        ca = cb = 0.0
        while ia < len(a.items) or ib < len(b.items):
            if ib >= len(b.items) or (ia < len(a.items) and ca / ta <= cb / tb):
                it = a.items[ia]
                ia += 1
                ca += it[0]
            else:
                it = b.items[ib]
                ib += 1
                cb += it[0]
            pg.add(it[1], it[2], reads=it[3], writes=it[4], chan=it[5], dur=it[6])

    ps = []
    psb = []
    for i in range(8):
        t = es.enter_context(nc.psum_tensor(f"ps{i}", [P, 512], F32))
        ps.append(Buf(t, 512, F32))
        psb.append(Buf(t.bitcast(BF16), 1024, BF16))
    psr = [Res(f"ps{i}") for i in range(8)]

    sems = {}

    def getsem(key):
        if key not in sems:
            sems[key] = es.enter_context(nc.semaphore("s_" + "_".join(str(k) for k in key)))
        return sems[key]

    BT = cF.ap(0, [(1, 128)])
    YS = cF.ap(128, [(1, 128)])
    ONESF = cF.ap(256, [(1, 128)])
    IDB = cB.ap(0, [(1, 128)])
    ONESB = cB.ap(128, [(1, 128)])
    NEGM = cB.ap(256, [(1, 128)])
    BLK2 = cB.ap(384, [(1, 128)])
    r_const = Res("const")

    def dma(out, in_, chan, reads=(), writes=(), eng="sp", dur=4.5):
        pg.add(eng, lambda e: e.dma_start(out=out, in_=in_), reads=reads, writes=writes, chan=chan, dur=dur)

    dma(cF.ap(), cF_d, "prm", writes=[r_const])
    dma(cB.ap(), cB_d, "prm", writes=[r_const])
    dma(prm.ap(), prm_d, "prm", writes=[r_const])

    def mk_bar(e):
        if e == "pe":
            return lambda eng: eng.matmul(ps[7].ap(0, [(1, 8)], 0, 8), cB.ap(0, [(1, 8)], 0, 8), cB.ap(0, [(1, 8)], 0, 8), start=True, stop=True)
        col = {"act": 740, "dve": 744, "pool": 748}[e]
        if e == "act":
            return lambda eng: eng.copy(small.ap(col, [(1, 2)]), prm.ap(0, [(1, 2)]))
        return lambda eng: eng.tensor_copy(small.ap(col, [(1, 2)]), prm.ap(0, [(1, 2)]))

    def barrier():
        pg.cut()
        pg.barrier(mk_bar, pe_extra=[psr[7]], rd=[r_const])
        pg.cut()

    cv = cvT
    cv.reset()
    xt = [cv.take(D, F32) for _ in range(4)]
    xs = [cv.take(D, BF16) for _ in range(2)]
    junk = cv.take(D, BF16)
    r_xt = [Res(f"xt{j}") for j in range(4)]
    r_xs = [Res("xs0"), Res("xs1")]
    r_ss = [Res(f"ss{i}") for i in range(NT)]
    r_junk = Res("junk")
    r_hT = [Res(f"hT{i}") for i in range(NT)]
    SS0 = 0
    RS0 = 16
    for i in range(NT):
        s = i % 4
        s2 = i % 2
        dma(xt[s].ap(), x_d[i * P:(i + 1) * P, :], f"x{s}", writes=[r_xt[s]])
        pg.add("act", lambda e, s=s, i=i: e.activation(junk.ap(), xt[s].ap(), AF.Square, accum_out=small.ap(SS0 + i, [(1, 1)])),
               reads=[r_xt[s]], writes=[r_ss[i], r_junk], dur=1.1)
        pg.add("act", lambda e, i=i: e.activation(small.ap(RS0 + i, [(1, 1)]), small.ap(SS0 + i, [(1, 1)]), AF.Ln, bias=float(D * EPS), scale=1.0),
               reads=[r_ss[i]], writes=[r_ss[i]], dur=0.25)
        pg.add("act", lambda e, i=i: e.activation(small.ap(RS0 + i, [(1, 1)]), small.ap(RS0 + i, [(1, 1)]), AF.Exp, scale=-0.5),
               reads=[r_ss[i]], writes=[r_ss[i]], dur=0.25)
        pg.add("dve", lambda e, s=s, s2=s2, i=i: e.tensor_scalar(xs[s2].ap(), xt[s].ap(), small.ap(RS0 + i, [(1, 1)]), float(np.sqrt(D)), ALU.mult, ALU.mult),
               reads=[r_xt[s], r_ss[i], r_const], writes=[r_xs[s2]])
        b = i % 2

        def tr(e, s=s2, b=b):
            ins = None
            for kd in range(KD):
                ins = e.transpose(psb[b].ap(kd * 128, [(1, 128)]), xs[s].ap(kd * 128, [(1, 128)]), IDB)
            return ins
        pg.add("pe", tr, reads=[r_xs[s2], r_const], writes=[psr[b]], dur=1.0)
        pg.add("act", lambda e, i=i, b=b: e.copy(hT.ap(i * P, [(T, KD), (1, P)]), psb[b].ap(0, [(128, KD), (1, 128)])),
               reads=[psr[b]], writes=[r_hT[i]], dur=1.05)

    pg.mark("P1")


    def mm_acc(e, out, pairs):
        ins = None
        n = len(pairs)
        for idx, (l, r) in enumerate(pairs):
            ins = e.matmul(out, l, r, start=(idx == 0), stop=(idx == n - 1))
        return ins

    GAINB = prm.ap(0, [(1, KD), (0, 128)])

    cvG.reset()
    gq = cvG.take(4 * T, BF16)
    gk = cvG.take(4 * T, BF16)
    vtok = cvG.take(16 * 512, BF16)
    zgdn = cvG.take(16 * 512, BF16)
    cv = cvD
    cv.reset()
    wst = [cv.take(KD * 256, F32) for _ in range(2)]
    wb = [cv.take(KD * 512, BF16) for _ in range(2)]
    r_wst = [Res("wst0"), Res("wst1")]
    r_wbq = [[Res(f"wb{s}_{q}") for q in range(4)] for s in range(2)]
    wsm = cv.take(KD * 8, BF16)
    wsmf = cv.take(KD * 8, F32)
    raw = [cv.take(4 + T, BF16) for _ in range(2)]
    dgw = [cv.take(4 * 128, BF16) for _ in range(2)]
    accb = [cv.take(512, F32) for _ in range(3)]
    r_accb = [Res(f"accb{j}") for j in range(3)]
    acc_rr = [0]
    r_dgw = [Res("dgw0"), Res("dgw1")]
    sqb = cv.take(T, BF16)
    r_raw = [[Res(f"raw{b}_{tb}") for tb in range(4)] for b in range(2)]
    r_stgf = Res("stgf")
    r_gq = [Res(f"gq{h}") for h in range(4)]
    r_gk = [Res(f"gk{h}") for h in range(4)]
    r_vtok = [Res(f"vtok{i}") for i in range(NT)]
    r_zgdn = [Res(f"zgdn{i}") for i in range(NT)]
    r_sqb = Res("sqb")
    r_lnv = [Res("lnv0"), Res("lnv1")]
    r_ba = Res("ba")
    BA0 = 64

    wq_count = [0]

    def load_wgroup(slot, c0, bufs, ncols=512, src=None, rows_src=None, pw=128, extra_w=(), cp="w"):
        wst, wb, r_wst, r_wbq = bufs
        src = win_d if src is None else src
        nq = pw // 128
        for q0 in range(0, ncols // 128, nq):
            st = wq_count[0] % 2
            wq_count[0] += 1
            dma(wst[st].ap(0, [(pw, KD), (1, pw)]),
                src.rearrange("(k p) c -> p k c", p=P)[:, :, c0 + q0 * 128:c0 + q0 * 128 + pw],
                f"{cp}{st}", writes=[r_wst[st]] + list(extra_w), dur=4.5 * nq)
            pg.add("pool", lambda e, st=st, slot=slot, q0=q0, wb=wb, wst=wst: e.tensor_tensor(
                wb[slot].ap(q0 * 128, [(512, KD), (1, pw)]), wst[st].ap(0, [(pw, KD), (1, pw)]), prm.ap(0, [(1, KD), (0, pw)]), ALU.mult),
                reads=[r_wst[st], r_const], writes=[r_wbq[slot][q0 + j] for j in range(nq)] + list(extra_w), dur=3.6 * nq)

    for b in range(2):
        pg.add("dve", lambda e, b=b: e.memset(raw[b].ap(0, [(1, 3)]), 0.0), writes=[r_raw[b][0]])

    r_wsm = Res("wsm")
    dma(wsmf.ap(0, [(8, KD), (1, 8)]), win_d.rearrange("(k p) c -> p k c", p=P)[:, :, 2048:2056], "wsm", writes=[r_wsm])
    pg.add("dve", lambda e: e.tensor_tensor(wsm.ap(0, [(8, KD), (1, 8)]), wsmf.ap(0, [(8, KD), (1, 8)]), prm.ap(0, [(1, KD), (0, 8)]), ALU.mult),
           reads=[r_wsm, r_const], writes=[r_wsm])

    def ba_mm(e):
        ins = None
        for i in range(NT):
            ins = mm_acc(e, ps[6].ap(i * 8, [(1, 8)]),
                         [(hT.ap(kd * T + i * P, [(1, P)]), wsm.ap(kd * 8, [(1, 8)])) for kd in range(KD)])
        return ins
    pg.add("pe", ba_mm, reads=[r_wsm] + r_hT, writes=[psr[6]])
    pg.add("act", lambda e: e.copy(small.ap(BA0, [(1, 128)]), ps[6].ap(0, [(1, 128)])), reads=[psr[6]], writes=[r_ba])

    groups = [(0, "q"), (512, "k"), (1024, "v"), (1536, "z")]
    bufsA = (wst, wb, r_wst, r_wbq)
    load_wgroup(0, groups[0][0], bufsA, pw=256)
    pcount = [0]
    chunk_count = [0]
    for gi, (c0, kind) in enumerate(groups):
        slot = gi % 2
        if gi + 1 < len(groups):
            load_wgroup((gi + 1) % 2, groups[gi + 1][0], bufsA, pw=256)
        if kind in ("q", "k", "v"):
            for h in range(4):
                ch = {"q": 0, "k": 4, "v": 8}[kind] + h
                rb = chunk_count[0] % 2
                chunk_count[0] += 1
                for tb in range(4):
                    pb = 2 + pcount[0] % 2
                    pcount[0] += 1
                    pg.add("pe", lambda e, pb=pb, slot=slot, h=h, tb=tb: mm_acc(
                        e, ps[pb].ap(), [(wb[slot].ap(kd * 512 + h * 128, [(1, 128)]), hT.ap(kd * T + tb * 512, [(1, 512)])) for kd in range(KD)]),
                        reads=[r_wbq[slot][h]] + r_hT[tb * 4:(tb + 1) * 4], writes=[psr[pb]], dur=2.15)
                    pg.add("act", lambda e, pb=pb, rb=rb, tb=tb: e.copy(raw[rb].ap(3 + tb * 512, [(1, 512)]), ps[pb].ap()),
                           reads=[psr[pb]], writes=[r_raw[rb][tb]])
                db = chunk_count[0] % 2
                pg.add("dve", lambda e, db=db, ch=ch: [e.tensor_scalar(dgw[db].ap(j * 128, [(1, 128)]), IDB, prm.ap(8 + ch * 4 + j, [(1, 1)]), None, ALU.mult) for j in (1, 2, 3)][-1],
                       reads=[r_const], writes=[r_dgw[db]], dur=0.4)
                for tb in range(4):
                    pb = 4 + tb % 2
                    ab = acc_rr[0] % 3
                    acc_rr[0] += 1
                    pg.add("pe", lambda e, pb=pb, rb=rb, tb=tb, db=db: mm_acc(
                        e, ps[pb].ap(), [(dgw[db].ap(j * 128, [(1, 128)]), raw[rb].ap(tb * 512 + j, [(1, 512)])) for j in (1, 2, 3)]),
                        reads=r_raw[rb] + [r_dgw[db]], writes=[psr[pb]], dur=0.85)
                    pg.add("dve", lambda e, pb=pb, rb=rb, tb=tb, ab=ab, ch=ch: e.scalar_tensor_tensor(
                        accb[ab].ap(), raw[rb].ap(tb * 512, [(1, 512)]), prm.ap(8 + ch * 4, [(1, 1)]), ps[pb].ap(), ALU.mult, ALU.add),
                        reads=r_raw[rb] + [psr[pb], r_const], writes=[r_accb[ab]], dur=0.65)
                    dst = {"q": gq, "k": gk}.get(kind)
                    if dst is not None:
                        rr = (r_gq if kind == "q" else r_gk)[h]
                        pg.add("act", lambda e, ab=ab, dst=dst, h=h, tb=tb: e.activation(dst.ap(h * T + tb * 512, [(1, 512)]), accb[ab].ap(), AF.Silu),
                               reads=[r_accb[ab]], writes=[rr], dur=0.55)
                    else:
                        pg.add("act", lambda e, ab=ab, tb=tb: e.activation(sqb.ap(tb * 512, [(1, 512)]), accb[ab].ap(), AF.Silu),
                               reads=[r_accb[ab]], writes=[r_sqb], dur=0.55)
                if kind == "v":
                    for g4 in range(4):
                        pb = g4 % 2

                        def trv(e, g4=g4, pb=pb):
                            ins = None
                            for q in range(4):
                                ins = e.transpose(psb[pb].ap(q * 128, [(1, 128)]), sqb.ap((g4 * 4 + q) * 128, [(1, 128)]), IDB)
                            return ins
                        pg.add("pe", trv, reads=[r_sqb, r_const], writes=[psr[pb]])
                        pg.add("dve", lambda e, g4=g4, pb=pb, h=h: e.tensor_copy(
                            vtok.ap((g4 * 4 * 4 + h) * 128, [(512, 4), (1, 128)]), psb[pb].ap(0, [(128, 4), (1, 128)])),
                            reads=[psr[pb]], writes=r_vtok[g4 * 4:(g4 + 1) * 4])
        else:
            for i in range(NT):
                pb = 6 + i % 2
                pg.add("pe", lambda e, pb=pb, slot=slot, i=i: mm_acc(
                    e, ps[pb].ap(), [(hT.ap(kd * T + i * P, [(1, P)]), wb[slot].ap(kd * 512, [(1, 512)])) for kd in range(KD)]),
                    reads=r_wbq[slot] + [r_hT[i]], writes=[psr[pb]], dur=2.15)
                pg.add("act", lambda e, pb=pb, i=i: e.activation(zgdn.ap(i * 512, [(1, 512)]), ps[pb].ap(), AF.Silu),
                       reads=[psr[pb]], writes=[r_zgdn[i]])
                pg.add("pool", lambda e, i=i: e.tensor_tensor(zgdn.ap(i * 512, [(128, 4), (1, 128)]), zgdn.ap(i * 512, [(128, 4), (1, 128)]),
                                                              prm.ap(64, [(0, 4), (1, 128)]), ALU.mult),
                       reads=[r_zgdn[i], r_const], writes=[r_zgdn[i]])

    cvD.reset()
    dQ = cvD.take(4 * T, BF16)
    dK = cvD.take(4 * T, BF16)
    VA = cvD.take(NT * 520, BF16)
    zgdf = cvD.take(NT * 512, BF16)
    cv = cvT
    cv.reset()
    wstB = [cv.take(KD * 128, F32) for _ in range(2)]
    wbB = [cv.take(KD * 512, BF16) for _ in range(2)]
    ND = 3
    sqd = [cv.take(512, BF16) for _ in range(ND)]
    yr = [cv.take(512, BF16) for _ in range(ND)]
    lnvB = [cv.take(512, F32) for _ in range(ND)]
    r_wstB = [Res("bwst0"), Res("bwst1")]
    r_wbqB = [[Res(f"bwb{s_}_{q}") for q in range(4)] for s_ in range(2)]
    r_sqd = [Res(f"sqd{j}") for j in range(ND)]
    r_yr = [Res(f"yr{j}") for j in range(ND)]
    r_lnvB = [Res(f"blnv{j}") for j in range(ND)]
    r_dQ = [Res(f"dQ{h}") for h in range(4)]
    r_dK = [Res(f"dK{h}") for h in range(4)]
    r_VA = [Res(f"VA{i}") for i in range(NT)]
    r_zgdf = [Res(f"zgdf{i}") for i in range(NT)]
    LNQ = float(-0.5 * np.log(128.0))
    PBK = (1, 2, 3)
    OBK = (4, 5, 0)

    def norm_tail(u, ob, kind_bias, fin):
        pass

    def l2_unit(kind, h, tb):
        u = cnt2[0] % ND
        cnt2[0] += 1
        ob = OBK[u]
        buf = gq if kind == "q" else gk
        rr = (r_gq if kind == "q" else r_gk)[h]
        sl = lambda: buf.ap(h * T + tb * 512, [(1, 512)])
        pg.add("act", lambda e: e.activation(sqd[u].ap(), sl(), AF.Square), reads=[rr], writes=[r_sqd[u]], dur=0.55)
        pg.add("pe", lambda e: e.matmul(ps[ob].ap(), ONESB, sqd[u].ap(), start=True, stop=True), reads=[r_sqd[u], r_const], writes=[psr[ob]], dur=0.3)
        pg.add("act", lambda e: e.activation(lnvB[u].ap(), ps[ob].ap(), AF.Ln, bias=float(EPS), scale=1.0), reads=[psr[ob]], writes=[r_lnvB[u]], dur=0.55)
        pg.add("act", lambda e: e.activation(lnvB[u].ap(), lnvB[u].ap(), AF.Exp, bias=(LNQ if kind == "q" else 0.0), scale=-0.5),
               reads=[r_lnvB[u]], writes=[r_lnvB[u]], dur=0.55)
        pg.add("dve", lambda e: e.tensor_tensor(sl(), sl(), lnvB[u].ap(), ALU.mult), reads=[r_lnvB[u], rr], writes=[rr], dur=0.65)

    groupsB = [(2056, "Q"), (2568, "K"), (3080, "V"), (3592, "Z")]
    bufsB = (wstB, wbB, r_wstB, r_wbqB)
    load_wgroup(0, groupsB[0][0], bufsB, extra_w=r_xt + r_xs + [r_junk], cp="wb")
    barrier()
    pg.mark("A1")
    pg.add("pool", lambda e: e.memset(VA.ap(128, [(130, 64), (1, 1)]), 1.0), writes=r_VA)
    cnt2 = [0]
    for gi, (c0, kind) in enumerate(groupsB):
        slot = gi % 2
        if gi + 1 < len(groupsB):
            load_wgroup((gi + 1) % 2, groupsB[gi + 1][0], bufsB, cp="wb")
        if kind in ("Q", "K"):
            dst = dQ if kind == "Q" else dK
            rdst = r_dQ if kind == "Q" else r_dK
            gcol = 320 if kind == "Q" else 321
            ebias = 0.0 if kind == "Q" else float(np.log(8.0))
            for h in range(4):
                for tb in range(4):
                    l2_unit("q" if kind == "Q" else "k", h, tb)
                    u = cnt2[0] % ND
                    cnt2[0] += 1
                    pb, ob = PBK[u], OBK[u]
                    pg.add("pe", lambda e, pb=pb, slot=slot, h=h, tb=tb: mm_acc(
                        e, ps[pb].ap(), [(wbB[slot].ap(kd * 512 + h * 128, [(1, 128)]), hT.ap(kd * T + tb * 512, [(1, 512)])) for kd in range(KD)]),
                        reads=[r_wbqB[slot][h]] + r_hT[tb * 4:(tb + 1) * 4], writes=[psr[pb]], dur=2.15)
                    pg.add("act", lambda e, pb=pb, u=u: e.activation(sqd[u].ap(), ps[pb].ap(), AF.Square), reads=[psr[pb]], writes=[r_sqd[u]], dur=0.55)
                    pg.add("dve", lambda e, pb=pb, u=u: e.tensor_copy(yr[u].ap(), ps[pb].ap()), writes=[psr[pb], r_yr[u]], dur=0.65)
                    pg.add("pe", lambda e, ob=ob, u=u: e.matmul(ps[ob].ap(), BLK2, sqd[u].ap(), start=True, stop=True),
                           reads=[r_sqd[u], r_const], writes=[psr[ob]], dur=0.3)
                    pg.add("act", lambda e, ob=ob, u=u: e.activation(lnvB[u].ap(), ps[ob].ap(), AF.Ln, bias=float(64 * EPS), scale=1.0),
                           reads=[psr[ob]], writes=[r_lnvB[u]], dur=0.55)
                    pg.add("act", lambda e, u=u, ebias=ebias: e.activation(lnvB[u].ap(), lnvB[u].ap(), AF.Exp, bias=ebias, scale=-0.5),
                           reads=[r_lnvB[u]], writes=[r_lnvB[u]], dur=0.55)
                    pg.add("dve", lambda e, u=u, dst=dst, h=h, tb=tb, gcol=gcol: e.scalar_tensor_tensor(
                        dst.ap(h * T + tb * 512, [(1, 512)]), yr[u].ap(), prm.ap(gcol, [(1, 1)]), lnvB[u].ap(), ALU.mult, ALU.mult),
                        reads=[r_yr[u], r_lnvB[u], r_const], writes=[rdst[h]], dur=0.65)
        else:
            if kind == "Z":
                pg.cut()
            for i in range(NT):
                pb = 6 + i % 2
                pg.add("pe", lambda e, pb=pb, slot=slot, i=i: mm_acc(
                    e, ps[pb].ap(), [(hT.ap(kd * T + i * P, [(1, P)]), wbB[slot].ap(kd * 512, [(1, 512)])) for kd in range(KD)]),
                    reads=r_wbqB[slot] + [r_hT[i]], writes=[psr[pb]], dur=2.15)
                if kind == "V":
                    pg.add("act", lambda e, pb=pb, i=i: e.copy(VA.ap(i * 520, [(130, 4), (1, 128)]), ps[pb].ap(0, H4)), reads=[psr[pb]], writes=[r_VA[i]])
                else:
                    pg.add("act", lambda e, pb=pb, i=i: e.activation(zgdf.ap(i * 512, [(1, 512)]), ps[pb].ap(), AF.Silu), reads=[psr[pb]], writes=[r_zgdf[i]])
                    pg.add("pool", lambda e, i=i: e.tensor_tensor(zgdf.ap(i * 512, H4), zgdf.ap(i * 512, H4), prm.ap(192, [(0, 4), (1, 128)]), ALU.mult),
                           reads=[r_zgdf[i], r_const], writes=[r_zgdf[i]])
    if debug and "dQ" in debug:
        for nm, buf, rl in (("dQ", dQ, r_dQ), ("dK", dK, r_dK)):
            for h in range(4):
                for q in range(4):
                    pg.add("dve", lambda e, buf=buf, h=h, q=q: e.tensor_copy(lnvB[0].ap(), buf.ap(h * T + q * 512, [(1, 512)])), reads=rl, writes=[r_lnvB[0]])
                    dma(dbg_d[nm][h * P:(h + 1) * P, q * 512:(q + 1) * 512], lnvB[0].ap(), "dbg", reads=[r_lnvB[0]], writes=[Res("dbgout")])
        for i in range(NT):
            pg.add("dve", lambda e, i=i: e.tensor_copy(lnvB[0].ap(), zgdf.ap(i * 512, [(1, 512)])), reads=r_zgdf, writes=[r_lnvB[0]])
            dma(dbg_d["zgdf"][i * P:(i + 1) * P, :], lnvB[0].ap(), "dbg", reads=[r_lnvB[0]], writes=[Res("dbgout")])
            pg.add("dve", lambda e, i=i: e.tensor_copy(lnvB[0].ap(0, [(130, 4), (1, 128)]), VA.ap(i * 520, [(130, 4), (1, 128)])), reads=r_VA, writes=[r_lnvB[0]])
            pg.add("dve", lambda e, i=i: e.tensor_copy(lnvB[0].ap(128, [(130, 4), (1, 2)]), VA.ap(i * 520 + 128, [(130, 4), (1, 2)])), reads=r_VA, writes=[r_lnvB[0]])
            dma(dbg_d["VA"][i * P:(i + 1) * P, :], lnvB[0].ap(), "dbg", reads=[r_lnvB[0]], writes=[Res("dbgout")])
    barrier()
    pg.mark("B1")


    cv = cvT
    cv.reset()
    class _Proxy:
        cur = None

        def add(self, *a, **k):
            _Proxy.cur.add(*a, **k)
    sg = _Proxy()
    g_prep = Rec()
    _Proxy.cur = g_prep
    f4 = lambda: cv.take(512, F32)
    b4 = lambda: cv.take(512, BF16)
    Xf = [f4()]
    Ef = [b4()]
    ETf = [b4()]
    egcf = [b4()]
    otok = [b4()]
    S32 = f4()
    S16 = b4()
    Pb = [[b4(), b4()]]
    Qb = [[b4(), b4()]]
    Rb = [[b4(), b4()]]
    AqkT = [b4(), b4()]
    qg = [b4(), b4()]
    nkbg = [b4()]
    kdec = [b4(), b4()]
    vb = [b4()]
    ub = [b4(), b4()]
    nwT = [b4(), b4()]
    vn16 = [b4()]
    mixtok = [b4()]
    sqo = mixtok[0]
    r_X, r_E, r_ET, r_egc, r_otok = R2("X"), R2("E"), R2("ET"), R2("egc"), R2("otok")
    r_P = [R2("P0_"), R2("P1_")]
    r_Q = [R2("Q0_"), R2("Q1_")]
    r_R = [R2("R0_"), R2("R1_")]
    r_Aqk, r_qg, r_nkbg, r_kdec, r_vb, r_u, r_nwT, r_vn, r_mixtok = (R2("Aqk"), R2("qg"), R2("nkbg"), R2("kdec"), R2("vb"),
                                                                    R2("u"), R2("nwT"), R2("vn"), R2("mixtok"))
    r_S32, r_S16 = Res("S32"), Res("S16")
    r_sqo = r_mixtok[0]
    r_g = Res("gsmall")
    r_egl = [Res(f"egl{i}") for i in range(NT)]
    r_ssn = Res("ssn")
    r_mixT = [Res(f"mixT{i}") for i in range(NT)]
    BETA0, NB0, G0, AL0, NBG0, KDS0, EGL0, SSN0, TMP0 = 192, 256, 320, 384, 392, 456, 520, 648, 656
    MSB = cF.ap(384, [(0, 4), (1, 128)])
    MITB = cF.ap(512, [(0, 4), (1, 128)])
    BTB = cF.ap(0, [(0, 4), (1, 128)])
    IDB4 = cB.ap(0, [(0, 4), (1, 128)])

    def colb(base, i):
        return small.ap(base + i * 4, [(1, 4), (0, 128)])

    bank_rr = [0]

    def nb():
        bank_rr[0] = (bank_rr[0] + 1) % 2
        return bank_rr[0]

    TH = [(4, 16), (1, 4)]
    sg.add("act", lambda e: e.activation(small.ap(BETA0, TH), small.ap(BA0, [(8, 16), (1, 4)]), AF.Exp, scale=-1.0), reads=[r_ba], writes=[r_g])
    sg.add("dve", lambda e: e.tensor_scalar(small.ap(BETA0, TH), small.ap(BETA0, TH), 1.0, None, ALU.add), reads=[r_g], writes=[r_g])
    sg.add("dve", lambda e: e.reciprocal(small.ap(BETA0, TH), small.ap(BETA0, TH)), reads=[r_g], writes=[r_g])
    sg.add("dve", lambda e: e.tensor_scalar(small.ap(NB0, TH), small.ap(BETA0, TH), -1.0, None, ALU.mult), reads=[r_g], writes=[r_g])
    sg.add("dve", lambda e: e.tensor_tensor(small.ap(G0, TH), small.ap(BA0 + 4, [(8, 16), (1, 4)]), prm.ap(60, [(0, 16), (1, 4)]), ALU.add),
           reads=[r_ba, r_const, r_g], writes=[r_g])
    sg.add("act", lambda e: e.activation(small.ap(G0, TH), small.ap(G0, TH), AF.Exp), reads=[r_g], writes=[r_g])
    sg.add("act", lambda e: e.activation(small.ap(G0, TH), small.ap(G0, TH), AF.Ln, bias=1.0, scale=1.0), reads=[r_g], writes=[r_g])
    sg.add("act", lambda e: e.activation(small.ap(AL0, [(1, 4)]), prm.ap(56, [(1, 4)]), AF.Exp), reads=[r_const, r_g], writes=[r_g])
    sg.add("dve", lambda e: e.scalar_tensor_tensor(small.ap(G0, TH), small.ap(G0, TH), -1.0, small.ap(AL0, [(0, 16), (1, 4)]), ALU.mult, ALU.mult),
           reads=[r_g], writes=[r_g])
    sg.add("pe", lambda e: e.matmul(ps[0].ap(0, [(1, 64)]), BT, small.ap(G0, [(1, 64)]), start=True, stop=True), reads=[r_g, r_const], writes=[psr[0]])
    sg.add("pe", lambda e: e.matmul(ps[1].ap(0, [(1, 64)]), YS, small.ap(G0, [(1, 64)]), start=True, stop=True), reads=[r_g, r_const], writes=[psr[1]])
    sg.add("act", lambda e: e.activation(small.ap(NBG0, [(1, 64)]), ps[0].ap(0, [(1, 64)]), AF.Exp), reads=[psr[0], r_g], writes=[r_g])
    sg.add("dve", lambda e: e.tensor_tensor(small.ap(NBG0, [(1, 64)]), small.ap(NBG0, [(1, 64)]), small.ap(NB0, [(1, 64)]), ALU.mult), reads=[r_g], writes=[r_g])
    sg.add("act", lambda e: e.activation(small.ap(KDS0, [(1, 64)]), ps[1].ap(0, [(1, 64)]), AF.Exp), reads=[psr[1], r_g], writes=[r_g])
    sg.add("dve", lambda e: e.memset(S32.ap(), 0.0), writes=[r_S32])
    sg.add("dve", lambda e: e.memset(S16.ap(), 0.0), writes=[r_S16])

    def mm4(e, bank, lhs, rhs, bf=False, ident_rhs=None):
        ins = None
        for h in range(4):
            out = (psb if bf else ps)[bank].ap(h * 128, [(1, 128)])
            if ident_rhs is None:
                ins = e.matmul(out, lhs(h), rhs(h), start=True, stop=True)
            else:
                e.matmul(out, lhs(h), rhs(h), start=True, stop=False)
                ins = e.matmul(out, IDB, ident_rhs(h), start=False, stop=True)
        return ins

    hv = lambda buf: (lambda h: buf.ap(h * 128, [(1, 128)]))
    copy_rr = [0]

    def evac(bank, dst, r_dst, bf_src=False):
        copy_rr[0] += 1
        src = (psb if bf_src else ps)[bank].ap(0, [(1, 512)])
        if copy_rr[0] % 3:
            sg.add("act", lambda e: e.copy(dst.ap(), src), reads=[psr[bank]], writes=[r_dst])
        else:
            sg.add("dve", lambda e: e.tensor_copy(dst.ap(), src), reads=[psr[bank]], writes=[r_dst])

    def gdn_par(i):
        b = 0
        bb = i % 2
        sg.add("dve", lambda e, b=b, bb=bb, i=i: e.tensor_tensor(Xf[b].ap(0, H4), BTB, colb(G0, i), ALU.mult), reads=[r_g, r_const], writes=[r_X[b]])
        k0 = nb()
        sg.add("pe", lambda e, b=b, bb=bb, k0=k0: mm4(e, k0, hv(Xf[b]), lambda h: YS), reads=[r_X[b], r_const], writes=[psr[k0]], cost=16)
        sg.add("act", lambda e, b=b, bb=bb, k0=k0: e.activation(Ef[b].ap(), ps[k0].ap(), AF.Exp), reads=[psr[k0]], writes=[r_E[b]])
        k1 = nb()
        sg.add("pe", lambda e, b=b, bb=bb, k1=k1: e.matmul(ps[k1].ap(), YS, Xf[b].ap(), start=True, stop=True), reads=[r_X[b], r_const], writes=[psr[k1]], cost=16)
        sg.add("act", lambda e, b=b, bb=bb, k1=k1: e.activation(ETf[b].ap(), ps[k1].ap(), AF.Exp), reads=[psr[k1]], writes=[r_ET[b]])
        k2 = nb()
        sg.add("pe", lambda e, b=b, bb=bb, k2=k2: e.matmul(ps[k2].ap(), ONESF, Xf[b].ap(), start=True, stop=True), reads=[r_X[b], r_const], writes=[psr[k2]], cost=16)
        sg.add("act", lambda e, b=b, bb=bb, k2=k2: e.activation(egcf[b].ap(), ps[k2].ap(), AF.Exp), reads=[psr[k2]], writes=[r_egc[b]])
        sg.add("act", lambda e, k2=k2, i=i: e.activation(small.ap(EGL0 + i * 8, [(2, 4), (1, 2)]), ps[k2].ap(63, [(128, 4), (64, 2)]), AF.Exp),
               reads=[psr[k2]], writes=[r_egl[i]])
        sg.add("dve", lambda e, b=b, bb=bb: e.tensor_tensor(Ef[b].ap(0, H4), Ef[b].ap(0, H4), MSB, ALU.mult), reads=[r_E[b], r_const], writes=[r_E[b]])
        sg.add("dve", lambda e, b=b, bb=bb, i=i: e.tensor_tensor(Ef[b].ap(0, H4), Ef[b].ap(0, H4), colb(NB0, i), ALU.mult), reads=[r_E[b], r_g], writes=[r_E[b]])
        sg.add("dve", lambda e, b=b, bb=bb: e.tensor_tensor(ETf[b].ap(0, H4), ETf[b].ap(0, H4), MITB, ALU.mult), reads=[r_ET[b], r_const], writes=[r_ET[b]])
        gkt = lambda h, i=i: gk.ap(h * T + i * P, [(1, P)])
        gqt = lambda h, i=i: gq.ap(h * T + i * P, [(1, P)])
        k3, k4 = nb(), nb()
        sg.add("pe", lambda e, k3=k3, gkt=gkt: mm4(e, k3, gkt, gkt), reads=r_gk, writes=[psr[k3]])
        sg.add("pe", lambda e, k4=k4, gkt=gkt, gqt=gqt: mm4(e, k4, gkt, gqt), reads=r_gk + r_gq, writes=[psr[k4]])
        sg.add("dve", lambda e, b=b, bb=bb, k3=k3: e.tensor_tensor(Pb[b][0].ap(), ps[k3].ap(), Ef[b].ap(), ALU.mult), reads=[psr[k3], r_E[b]], writes=[r_P[b][0]])
        sg.add("dve", lambda e, b=b, bb=bb, k4=k4: e.tensor_tensor(AqkT[bb].ap(), ps[k4].ap(), ETf[b].ap(), ALU.mult), reads=[psr[k4], r_ET[b]], writes=[r_Aqk[bb]])
        k5 = nb()

        def trP(e, b=b, k5=k5):
            ins = None
            for h in range(4):
                ins = e.transpose(psb[k5].ap(h * 128, [(1, 128)]), Pb[b][0].ap(h * 128, [(1, 128)]), IDB)
            return ins
        sg.add("pe", trP, reads=[r_P[b][0], r_const], writes=[psr[k5]])
        evac(k5, Qb[b][0], r_Q[b][0], bf_src=True)
        sg.add("dve", lambda e, b=b, bb=bb: e.tensor_tensor(Rb[b][0].ap(0, H4), Qb[b][0].ap(0, H4), IDB4, ALU.add), reads=[r_Q[b][0], r_const], writes=[r_R[b][0]])
        for k in range(5):
            c, n = k % 2, (k + 1) % 2
            kp = nb()
            sg.add("pe", lambda e, b=b, bb=bb, c=c, kp=kp: mm4(e, kp, hv(Qb[b][c]), hv(Pb[b][c])), reads=[r_Q[b][c], r_P[b][c]], writes=[psr[kp]])
            if k < 4:
                kq = nb()
                sg.add("pe", lambda e, b=b, bb=bb, c=c, kq=kq: mm4(e, kq, hv(Pb[b][c]), hv(Qb[b][c])), reads=[r_Q[b][c], r_P[b][c]], writes=[psr[kq]])
            evac(kp, Pb[b][n], r_P[b][n])
            if k < 4:
                evac(kq, Qb[b][n], r_Q[b][n])
            kr = nb()
            sg.add("pe", lambda e, b=b, bb=bb, c=c, n=n, kr=kr: mm4(e, kr, hv(Pb[b][n]), hv(Rb[b][c])),
                   reads=[r_P[b][n], r_R[b][c]], writes=[psr[kr]], cost=4)
            sg.add("dve", lambda e, b=b, bb=bb, c=c, n=n, kr=kr: e.tensor_tensor(Rb[b][n].ap(), ps[kr].ap(), Rb[b][c].ap(), ALU.add),
                   reads=[psr[kr], r_R[b][c]], writes=[r_R[b][n]])
        TT = Rb[b][1]
        r_TT = r_R[b][1]
        sg.add("dve", lambda e, b=b, bb=bb, i=i: e.tensor_tensor(qg[bb].ap(0, H4), gq.ap(i * P, [(T, 4), (1, P)]), egcf[b].ap(0, H4), ALU.mult),
               reads=r_gq + [r_egc[b]], writes=[r_qg[bb]])
        k6 = nb()

        def trK(e, k6=k6, gkt=gkt):
            ins = None
            for h in range(4):
                ins = e.transpose(psb[k6].ap(h * 128, [(1, 128)]), gkt(h), IDB)
            return ins
        sg.add("pe", trK, reads=r_gk + [r_const], writes=[psr[k6]])
        sg.add("dve", lambda e, b=b, bb=bb, i=i, k6=k6: e.tensor_tensor(nkbg[b].ap(0, H4), psb[k6].ap(0, H4), colb(NBG0, i), ALU.mult),
               reads=[psr[k6], r_g], writes=[r_nkbg[b]])
        sg.add("dve", lambda e, b=b, bb=bb, i=i, k6=k6: e.tensor_tensor(kdec[bb].ap(0, H4), psb[k6].ap(0, H4), colb(KDS0, i), ALU.mult),
               reads=[psr[k6], r_g], writes=[r_kdec[bb]])
        sg.add("pool", lambda e, b=b, bb=bb, i=i: e.tensor_tensor(vb[b].ap(0, H4), vtok.ap(i * 512, H4), colb(BETA0, i), ALU.mult),
               reads=[r_vtok[i], r_g], writes=[r_vb[b]])
        k7, k8 = nb(), nb()
        sg.add("pe", lambda e, b=b, bb=bb, k7=k7, TT=TT: mm4(e, k7, hv(TT), hv(vb[b])), reads=[r_TT, r_vb[b]], writes=[psr[k7]])
        evac(k7, ub[bb], r_u[bb])
        sg.add("pe", lambda e, b=b, bb=bb, k8=k8, TT=TT: mm4(e, k8, hv(nkbg[b]), hv(TT)), reads=[r_TT, r_nkbg[b]], writes=[psr[k8]])
        evac(k8, nwT[bb], r_nwT[bb])

    def gdn_seq(i):
        b = 0
        bb = i % 2
        for ci in range(2):
            r0 = 64 * ci
            rows = lambda buf, h, r0=r0: buf.ap(h * 128, [(1, 128)], p0=r0, np_=64)
            kv = 2
            sg.add("pe", lambda e, b=b, bb=bb, kv=kv: mm4(e, kv, hv(nwT[bb]), hv(S16)), reads=[r_nwT[bb], r_S16], writes=[psr[kv]], cost=4)
            sg.add("dve", lambda e, b=b, bb=bb, r0=r0, kv=kv: e.tensor_tensor(vn16[b].ap(0, [(1, 512)], p0=r0, np_=64), ps[kv].ap(0, [(1, 512)], p0=r0, np_=64),
                                                                       ub[bb].ap(0, [(1, 512)], p0=r0, np_=64), ALU.add),
                   reads=[psr[kv], r_u[bb]], writes=[r_vn[b]])

            def omm(e, b=b, bb=bb, rows=rows):
                ins = None
                for h in range(4):
                    out = ps[3].ap(h * 128, [(1, 128)])
                    e.matmul(out, qg[bb].ap(h * 128, [(1, 128)]), S16.ap(h * 128, [(1, 128)]), start=True, stop=False)
                    ins = e.matmul(out, rows(AqkT[bb], h), rows(vn16[b], h), start=False, stop=True)
                return ins
            sg.add("pe", omm, reads=[r_qg[bb], r_S16, r_Aqk[bb], r_vn[b]], writes=[psr[3]], cost=8)
            sg.add("act", lambda e, b=b, bb=bb, r0=r0: e.copy(otok[b].ap(0, [(1, 512)], p0=r0, np_=64), ps[3].ap(0, [(1, 512)], p0=r0, np_=64)),
                   reads=[psr[3]], writes=[r_otok[b]])
            ks = 2
            sg.add("pe", lambda e, b=b, bb=bb, rows=rows, ks=ks: mm4(e, ks, lambda h: rows(kdec[bb], h), lambda h: rows(vn16[b], h)),
                   reads=[r_kdec[bb], r_vn[b]], writes=[psr[ks]], cost=4)

            def supd(e, i=i, ci=ci, ks=ks):
                ins = None
                for h in range(4):
                    ins = e.scalar_tensor_tensor(S32.ap(h * 128, [(1, 128)]), S32.ap(h * 128, [(1, 128)]),
                                                 small.ap(EGL0 + i * 8 + h * 2 + ci, [(1, 1)]), ps[ks].ap(h * 128, [(1, 128)]), ALU.mult, ALU.add)
                return ins
            sg.add("dve", supd, reads=[psr[ks], r_egl[i], r_S32], writes=[r_S32])
            sg.add("act", lambda e: e.copy(S16.ap(), S32.ap()), reads=[r_S32], writes=[r_S16])
        def sqs(e, b=b):
            ins = None
            for h in range(4):
                ins = e.activation(sqo.ap(h * 128, [(1, 128)]), otok[b].ap(h * 128, [(1, 128)]), AF.Square, accum_out=small.ap(SSN0 + h, [(1, 1)]))
            return ins
        sg.add("act", sqs, reads=[r_otok[b], r_ssn], writes=[r_sqo, r_ssn])
        sg.add("act", lambda e: e.activation(small.ap(SSN0, [(1, 4)]), small.ap(SSN0, [(1, 4)]), AF.Ln, bias=float(128 * EPS), scale=1.0), reads=[r_ssn], writes=[r_ssn])
        sg.add("act", lambda e: e.activation(small.ap(SSN0, [(1, 4)]), small.ap(SSN0, [(1, 4)]), AF.Exp, bias=float(0.5 * np.log(128.0)), scale=-0.5),
               reads=[r_ssn], writes=[r_ssn])
        sg.add("dve", lambda e, b=b, bb=bb: e.tensor_tensor(otok[b].ap(0, H4), otok[b].ap(0, H4), small.ap(SSN0, [(1, 4), (0, 128)]), ALU.mult),
               reads=[r_otok[b], r_ssn], writes=[r_otok[b]])
        sg.add("dve", lambda e, b=b, bb=bb, i=i: e.tensor_tensor(mixtok[b].ap(), otok[b].ap(), zgdn.ap(i * 512, [(1, 512)]), ALU.mult),
               reads=[r_otok[b], r_zgdn[i]], writes=[r_mixtok[b]])
        k9 = 2

        def trM(e, b=b, k9=k9):
            ins = None
            for h in range(4):
                ins = e.transpose(psb[k9].ap(h * 128, [(1, 128)]), mixtok[b].ap(h * 128, [(1, 128)]), IDB)
            return ins
        sg.add("pe", trM, reads=[r_mixtok[b], r_const], writes=[psr[k9]])
        sg.add("act", lambda e, i=i, k9=k9: e.copy(mixT.ap(i * P, [(T, 4), (1, P)]), psb[k9].ap(0, H4)), reads=[psr[k9]], writes=[r_mixT[i]])

    sa = Rec()
    PT = [cv.take(512, BF16) for _ in range(4)]
    r_PT = [Res(f"PT{j}") for j in range(4)]
    o12 = cv.take(4 * 258, F32)
    r_o12 = [Res(f"o12_{h}") for h in range(4)]
    mixtk = cv.take(512, BF16)
    r_mixtk = Res("mixtk")
    r_lam = Res("lam")
    r_rl = Res("rl")
    r_ss2 = Res("ss2")
    LAM0, NLAM, RL0, SS20, LTMP = 700, 704, 708, 720, 192
    O1 = [(258, 4), (1, 128)]
    sa.add("dve", lambda e: e.tensor_tensor(o12.ap(0, [(64, 2), (1, 64)]), prm.ap(322, [(128, 2), (1, 64)]), prm.ap(386, [(128, 2), (1, 64)]), ALU.mult),
           reads=[r_const], writes=r_o12)
    sa.add("dve", lambda e: e.tensor_reduce(small.ap(LAM0, [(1, 2)]), o12.ap(0, [(64, 2), (1, 64)]), AX.X, ALU.add), reads=r_o12, writes=[r_lam])
    sa.add("act", lambda e: e.activation(small.ap(LAM0, [(1, 2)]), small.ap(LAM0, [(1, 2)]), AF.Exp), reads=[r_lam], writes=[r_lam])
    sa.add("dve", lambda e: e.tensor_tensor(small.ap(NLAM, [(1, 1)]), small.ap(LAM0 + 1, [(1, 1)]), small.ap(LAM0, [(1, 1)]), ALU.subtract), reads=[r_lam], writes=[r_lam])
    sa.add("dve", lambda e: e.tensor_scalar(small.ap(NLAM, [(1, 1)]), small.ap(NLAM, [(1, 1)]), float(-LAMBDA_INIT), None, ALU.add), reads=[r_lam], writes=[r_lam])
    EB2 = float(0.5 * np.log(128.0) + np.log(1.0 - LAMBDA_INIT))
    OB = 7

    groups = []
    for qt in range(NT):
        for h in range(4):
            g0s = list(range(0, qt + 1, 4))
            for gi_, g0 in enumerate(g0s):
                groups.append(dict(qt=qt, h=h, kts=list(range(g0, min(g0 + 4, qt + 1))), last=(gi_ == len(g0s) - 1)))
    for k, g in enumerate(groups):
        g["sb"] = (4 + (2 * k) % 3, 4 + (2 * k + 1) % 3)
        g["pj"] = ((2 * k) % 4, (2 * k + 1) % 4)

    def emit_scores(g):
        kts, sbks, h, qt, pjs = g["kts"], g["sb"], g["h"], g["qt"], g["pj"]
        n = len(kts)

        def smm(e):
            ins = None
            for j, kt in enumerate(kts):
                for c in range(2):
                    out = ps[sbks[c]].ap(j * 128, [(1, 128)])
                    lh = dK.ap(h * T + kt * P, [(1, P)], p0=c * 64, np_=64)
                    rh = dQ.ap(h * T + qt * P, [(1, P)], p0=c * 64, np_=64)
                    ins = e.matmul(out, lh, rh, start=True, stop=(kt != qt))
                if kt == qt:
                    for c in range(2):
                        ins = e.matmul(ps[sbks[c]].ap(j * 128, [(1, 128)]), IDB, NEGM, start=False, stop=True)
            return ins
        sa.add("pe", smm, reads=[r_dK[h], r_dQ[h], r_const], writes=[psr[sbks[0]], psr[sbks[1]]], cost=2 * n + (1 if qt in kts else 0))
        for c in range(2):
            sa.add("act", lambda e, sbk=sbks[c], pj=pjs[c]: e.activation(PT[pj].ap(0, [(1, n * 128)]), ps[sbk].ap(0, [(1, n * 128)]), AF.Exp),
                   reads=[psr[sbks[c]]], writes=[r_PT[pjs[c]]])

    def emit_pv(g):
        kts, h, qt, pjs = g["kts"], g["h"], g["qt"], g["pj"]

        def pvm(e):
            ins = None
            for c in range(2):
                for j, kt in enumerate(kts):
                    ins = e.matmul(ps[OB].ap(c * 129, [(1, 129)]), PT[pjs[c]].ap(j * 128, [(1, 128)]), VA.ap(kt * 520 + h * 130, [(1, 129)]),
                                   start=(kt == 0 and c == 0), stop=(kt == qt), skip_group_check=True)
            return ins
        sa.add("pe", pvm, reads=[r_PT[pjs[0]], r_PT[pjs[1]]] + [r_VA[kt] for kt in kts], writes=[psr[OB]], cost=2 * len(kts))
        if g["last"]:
            sa.add("act", lambda e: e.copy(o12.ap(h * 258, [(1, 258)]), ps[OB].ap(0, [(1, 258)])), reads=[psr[OB]], writes=[r_o12[h]])
            if h == 3:
                emit_epilogue(qt)

    def emit_epilogue(qt):
        sa.add("dve", lambda e: e.reciprocal(small.ap(RL0, [(1, 8)]), o12.ap(128, [(129, 8), (1, 1)])), reads=r_o12, writes=[r_rl])
        sa.add("dve", lambda e: e.tensor_tensor(o12.ap(0, O1), o12.ap(0, O1), small.ap(RL0, [(2, 4), (0, 128)]), ALU.mult), reads=r_o12 + [r_rl], writes=r_o12)
        sa.add("dve", lambda e: e.tensor_tensor(o12.ap(129, O1), o12.ap(129, O1), small.ap(RL0 + 1, [(2, 4), (0, 128)]), ALU.mult), reads=r_o12 + [r_rl], writes=r_o12)
        sa.add("dve", lambda e: e.scalar_tensor_tensor(o12.ap(0, O1), o12.ap(129, O1), small.ap(NLAM, [(1, 1)]), o12.ap(0, O1), ALU.mult, ALU.add),
               reads=r_o12 + [r_lam], writes=r_o12)

        def sqs2(e):
            ins = None
            for h in range(4):
                ins = e.activation(o12.ap(h * 258 + 129, [(1, 128)]), o12.ap(h * 258, [(1, 128)]), AF.Square, accum_out=small.ap(SS20 + h, [(1, 1)]))
            return ins
        sa.add("act", sqs2, reads=r_o12 + [r_ss2], writes=r_o12 + [r_ss2])
        sa.add("act", lambda e: e.activation(small.ap(SS20, [(1, 4)]), small.ap(SS20, [(1, 4)]), AF.Ln, bias=float(128 * EPS), scale=1.0), reads=[r_ss2], writes=[r_ss2])
        sa.add("act", lambda e: e.activation(small.ap(SS20, [(1, 4)]), small.ap(SS20, [(1, 4)]), AF.Exp, bias=EB2, scale=-0.5), reads=[r_ss2], writes=[r_ss2])
        sa.add("dve", lambda e: e.tensor_tensor(o12.ap(0, O1), o12.ap(0, O1), small.ap(SS20, [(1, 4), (0, 128)]), ALU.mult), reads=r_o12 + [r_ss2], writes=r_o12)
        sa.add("dve", lambda e: e.tensor_tensor(mixtk.ap(0, H4), o12.ap(0, O1), zgdf.ap(qt * 512, H4), ALU.mult),
               reads=r_o12 + [r_zgdf[qt]], writes=[r_mixtk])
        tbk = groups[min(len(groups) - 1, 0)]["sb"][0]
        tbk = 4 + (qt % 3)

        def trD(e):
            ins = None
            for h in range(4):
                ins = e.transpose(psb[tbk].ap(h * 128, [(1, 128)]), mixtk.ap(h * 128, [(1, 128)]), IDB)
            return ins
        sa.add("pe", trD, reads=[r_mixtk, r_const], writes=[psr[tbk]])
        sa.add("act", lambda e: e.copy(mixT.ap(4 * T + qt * P, [(T, 4), (1, P)]), psb[tbk].ap(0, H4)), reads=[psr[tbk]], writes=[r_mixT[qt]])

    for k, g in enumerate(groups):
        emit_scores(g)
        if k >= 1:
            emit_pv(groups[k - 1])
    emit_pv(groups[-1])

    def merge2(a, b):
        out = Rec()
        ta = float(sum(it[0] for it in a.items)) or 1.0
        tb = float(sum(it[0] for it in b.items)) or 1.0
        ia = ib = 0
        ca = cb = 0.0
        while ia < len(a.items) or ib < len(b.items):
            if ib >= len(b.items) or (ia < len(a.items) and ca / ta <= cb / tb):
                it = a.items[ia]
                ia += 1
                ca += it[0]
            else:
                it = b.items[ib]
                ib += 1
                cb += it[0]
            out.items.append(it)
        return out

    pars, seqs = [], []
    for i in range(NT):
        _Proxy.cur = Rec()
        gdn_par(i)
        pars.append(_Proxy.cur)
        _Proxy.cur = Rec()
        gdn_seq(i)
        seqs.append(_Proxy.cur)
    sgall = Rec()
    sgall.items += g_prep.items + pars[0].items
    for i in range(NT):
        nxt = pars[i + 1] if i + 1 < NT else Rec()
        sgall.items += merge2(nxt, seqs[i]).items
    sg = sgall

    merge_streams(sg, sa)
    if debug and "odn" in debug:
        for c in range(4):
            for q in range(4):
                pg.add("dve", lambda e, c=c, q=q: e.tensor_copy(o12.ap(0, [(1, 512)]), mixT.ap(c * T + q * 512, [(1, 512)])), reads=r_mixT, writes=r_o12)
                dma(dbg_d["odn"][c * P:(c + 1) * P, q * 512:(q + 1) * 512], o12.ap(0, [(1, 512)]), "dbg", reads=r_o12, writes=[Res("dbgout")])
    if debug and "odf" in debug:
        for c in range(4):
            for q in range(4):
                pg.add("dve", lambda e, c=c, q=q: e.tensor_copy(o12.ap(0, [(1, 512)]), mixT.ap((4 + c) * T + q * 512, [(1, 512)])), reads=r_mixT, writes=r_o12)
                dma(dbg_d["odf"][c * P:(c + 1) * P, q * 512:(q + 1) * 512], o12.ap(0, [(1, 512)]), "dbg", reads=r_o12, writes=[Res("dbgout")])

    pg.mark("M")

    cv = cvG
    cv.reset()
    wob = cv.take(KD * 1024, BF16)
    wstC = [cv.take(KD * 256, F32) for _ in range(2)]
    r_wob_parts = {}
    xt2 = [cv.take(D, F32) for _ in range(2)]
    yo = [cv.take(D, F32) for _ in range(2)]
    r_wstC = [Res("cwst0"), Res("cwst1")]
    r_wob = [Res(f"wob{q}") for q in range(8)]
    r_xt2, r_yo = R2("xt2"), R2("yo")
    for q in range(4):
        st = q % 2
        dma(wstC[st].ap(0, [(256, KD), (1, 256)]), wout_d.rearrange("(k p) c -> p k c", p=P)[:, :, q * 256:(q + 1) * 256], f"wc{st}", writes=[r_wstC[st]] + r_gk[2 * st:2 * st + 2], dur=9.0)
        for kd in range(KD):
            ce = ("pool", "dve", "act", "dve")[kd % 4]
            src_ap = lambda st=st, kd=kd: wstC[st].ap(kd * 256, [(1, 256)])
            dst_ap = lambda q=q, kd=kd: wob.ap(kd * 1024 + q * 256, [(1, 256)])
            rw = Res(f"wobp{q}_{kd}")
            r_wob_parts.setdefault(q, []).append(rw)
            if ce == "act":
                pg.add("act", lambda e, src_ap=src_ap, dst_ap=dst_ap: e.copy(dst_ap(), src_ap()), reads=[r_wstC[st]], writes=[rw, r_gq[kd // 2]], dur=0.4)
            else:
                pg.add(ce, lambda e, src_ap=src_ap, dst_ap=dst_ap: e.tensor_copy(dst_ap(), src_ap()), reads=[r_wstC[st]], writes=[rw, r_gq[kd // 2]], dur=(0.9 if ce == "pool" else 0.4))
    for i in range(NT):
        s_ = i % 2
        dma(xt2[s_].ap(), x_d[i * P:(i + 1) * P, :], f"x{s_}", writes=[r_xt2[s_]] + r_vtok[4 * s_:4 * s_ + 4])
        for half in range(2):
            pb = (i * 2 + half) % 4
            pg.add("pe", lambda e, pb=pb, i=i, half=half: mm_acc(
                e, ps[pb].ap(), [(mixT.ap(c * T + i * P, [(1, P)]), wob.ap(c * 1024 + half * 512, [(1, 512)])) for c in range(8)]),
                reads=[r_mixT[i]] + r_wob_parts[2 * half] + r_wob_parts[2 * half + 1], writes=[psr[pb]], dur=2.15)
            pg.add("dve", lambda e, pb=pb, s_=s_, half=half: e.tensor_tensor(yo[s_].ap(half * 512, [(1, 512)]), ps[pb].ap(), xt2[s_].ap(half * 512, [(1, 512)]), ALU.add),
                   reads=[psr[pb], r_xt2[s_]], writes=[r_yo[s_]] + r_vtok[8 + 4 * s_:12 + 4 * s_])
        dma(y_d[i * P:(i + 1) * P, :], yo[s_].ap(), f"y{s_}", reads=[r_yo[s_]], writes=[Res("yout")])

    if debug and "hT" in debug:
        cv.reset()
        stg = cv.take(T, F32)
        r_stg = Res("stg")
        for kd in range(KD):
            pg.add("dve", lambda e, kd=kd: e.tensor_copy(stg.ap(), hT.ap(kd * T, [(1, T)])), reads=r_hT, writes=[r_stg])
            dma(dbg_d["hT"][kd * P:(kd + 1) * P, :], stg.ap(), "dbg", reads=[r_stg], writes=[Res("dbgout")])

    out_chans = [c for c in ("y0", "y1", "dbg")]
    import os as _os
    if _os.environ.get("KNOSCHED") != "1":
        pg.schedule(lat=float(_os.environ.get("KLAT", "0.5")))
    pg.plan()

    esem = {e: getsem(("e", e)) for e in ("pe", "act", "dve", "pool")}

    def semfor(key):
        if key[0] == "e":
            return esem[key[1]]
        return getsem(key)

    def run_engine(eng_name, eng):
        for op in pg.ops:
            if op.eng != eng_name:
                continue
            for key, v in op.waits:
                eng.wait_ge(semfor(key), v)
            if op.fn is None:
                continue
            ins = op.fn(eng)
            if op.chan is not None:
                ins.then_inc(getsem(("c", op.chan)), 16)
            elif op.signal:
                ins.then_inc(esem[op.eng], 1)
        if eng_name == "sp":
            for c in out_chans:
                if c in pg.chans:
                    eng.wait_ge(getsem(("c", c)), pg.chans[c])

    with nc.Block() as block:
        @block.sync
        def _(e):
            run_engine("sp", e)

        @block.tensor
        def _(e):
            run_engine("pe", e)

        @block.scalar
        def _(e):
            run_engine("act", e)

        @block.vector
        def _(e):
            run_engine("dve", e)

        @block.gpsimd
        def _(e):
            run_engine("pool", e)

    es.close()
    return nc


def make_consts():
    j = np.arange(128)
    same = (j[:, None] // 64) == (j[None, :] // 64)
    BT = (same & (j[:, None] <= j[None, :])).astype(np.float32)
    YS = (same & (j[:, None] > j[None, :])).astype(np.float32)
    ONES = np.ones((128, 128), np.float32)
    MS = (same & (j[None, :] < j[:, None])).astype(np.float32)
    MIT = (same & (j[:, None] <= j[None, :])).astype(np.float32)
    cF = np.concatenate([BT, YS, ONES, MS, MIT], axis=1)
    ident = np.eye(128, dtype=np.float32)
    negm = np.where(j[:, None] > j[None, :], -30000.0, 0.0).astype(np.float32)
    blk2 = same.astype(np.float32)
    cB = np.concatenate([ident, ONES, negm, blk2], axis=1).astype(ml_dtypes.bfloat16)
    return cF, cB


def pack_params(inp):
    prm = np.zeros((P, NPRM), np.float32)
    prm[:, 0:8] = inp["norm_gain"][0].reshape(KD, P).T
    cw = inp["conv_w"][0]
    prm[:, 8:56] = cw.reshape(4, 12, P).transpose(2, 1, 0).reshape(P, 48)
    prm[:, 56:60] = inp["a_log"][0][None, :]
    prm[:, 60:64] = inp["dt_bias"][0][None, :]
    prm[:, 64:192] = inp["dn_out_gain"][0][None, :]
    prm[:, 192:320] = inp["df_out_gain"][0][None, :]
    prm[:, 320] = np.tile(inp["q_gain"][0], 2)
    prm[:, 321] = np.tile(inp["k_gain"][0], 2)
    prm[:, 322:386] = inp["lambda_q1"][0][None, :]
    prm[:, 386:450] = inp["lambda_k1"][0][None, :]
    prm[:, 450:514] = inp["lambda_q2"][0][None, :]
    prm[:, 514:578] = inp["lambda_k2"][0][None, :]
    return prm


def kernel(**inputs):
    inp = {k: np.asarray(v) for k, v in inputs.items()}
    n = 8
    nc = build_nc(DEBUG)
    cF, cB = make_consts()
    prm = pack_params(inp)
    w_in = np.ascontiguousarray(inp["w_in"][0], dtype=np.float32)
    w_out = np.ascontiguousarray(inp["w_out"][0], dtype=np.float32)
    in_maps = []
    for c in range(n):
        in_maps.append({"x": np.ascontiguousarray(inp["x"][c], dtype=np.float32), "w_in": w_in, "w_out": w_out,
                        "prm": prm, "cF": cF, "cB": cB})
    res = run_bass_kernel_spmd(nc, in_maps, core_ids=list(range(n)))
    kernel.last = res
    return np.stack([r["y"] for r in res.results], axis=0).astype(np.float32)
```
